# Optimizing a Trainium2 kernel written in Bass

```python
import jax, jax.numpy as jnp
from jax import lax
import numpy as np

D_MODEL = 1024
BATCH = 4
SEQ = 8192
DEPTH = 2

HEAD_DIM = 64
ROT_DIM = HEAD_DIM // 4
ROPE_THETA = 500000.0
QBLK = 128

DILATED_GROUPS = ((128, 1), (512, 4), (2048, 16))
A_HEADS_PER_GROUP = 2
A_HEADS = A_HEADS_PER_GROUP * len(DILATED_GROUPS)
B_HEADS = 4
IDX_HEADS = 4
IDX_DIM = 64
DSA_TOPK = 256
C_HEADS = 4
D_HEADS = 4

N_BRANCH = 4
D_FF = 2816
LN_EPS = 1e-5
DEEPNORM_ALPHA = (2 * DEPTH) ** 0.25
DEEPNORM_BETA = (8 * DEPTH) ** -0.25

A_QKV_W = 3 * A_HEADS * HEAD_DIM
B_QKV_W = 3 * B_HEADS * HEAD_DIM
IDX_Q_W = IDX_HEADS * IDX_DIM
IDX_K_W = IDX_DIM
IDX_W_W = IDX_HEADS
C_QKV_W = 3 * C_HEADS * HEAD_DIM
D_QKV_W = 3 * D_HEADS * HEAD_DIM
FG_W = D_HEADS
GATE_W = N_BRANCH * D_MODEL
OFF_B = A_QKV_W
OFF_IQ = OFF_B + B_QKV_W
OFF_IK = OFF_IQ + IDX_Q_W
OFF_IW = OFF_IK + IDX_K_W
OFF_C = OFF_IW + IDX_W_W
OFF_D = OFF_C + C_QKV_W
OFF_FG = OFF_D + D_QKV_W
OFF_GATE = OFF_FG + FG_W
N_IN = OFF_GATE + GATE_W
SPLIT_POINTS = [OFF_B, OFF_IQ, OFF_IK, OFF_IW, OFF_C, OFF_D, OFF_FG, OFF_GATE]

BRANCH_WIDTHS = (A_HEADS_PER_GROUP * HEAD_DIM, B_HEADS * HEAD_DIM, C_HEADS * HEAD_DIM, D_HEADS * HEAD_DIM)
BRANCH_OFFSETS = (0, BRANCH_WIDTHS[0], BRANCH_WIDTHS[0] + BRANCH_WIDTHS[1],
                  BRANCH_WIDTHS[0] + BRANCH_WIDTHS[1] + BRANCH_WIDTHS[2],
                  BRANCH_WIDTHS[0] + BRANCH_WIDTHS[1] + BRANCH_WIDTHS[2] + BRANCH_WIDTHS[3])

kernel_name = 'hybrid_gated_dilated_dsa_stickbreak_fox_macaron_deepnorm'

F32 = jnp.float32


def _layer_norm(x, g, b):
    xf = x.astype(F32)
    mu = jnp.mean(xf, axis=-1, keepdims=True)
    var = jnp.mean(jnp.square(xf - mu), axis=-1, keepdims=True)
    return ((xf - mu) * lax.rsqrt(var + LN_EPS) * g.astype(F32) + b.astype(F32)).astype(x.dtype)


def _modulate(h, shift, scale):
    return h * (1 + scale[:, None, :]) + shift[:, None, :]


def _swiglu(u, w_in, w_out):
    g, up = jnp.split(u @ w_in, 2, axis=-1)
    return (jax.nn.silu(g) * up) @ w_out


def _rope_partial(x, pos):
    half = ROT_DIM // 2
    inv_freq = ROPE_THETA ** (-(jnp.arange(half, dtype=F32) * (2.0 / ROT_DIM)))
    ang = pos.astype(F32)[:, None] * inv_freq[None, :]
    cos = jnp.cos(ang)[:, None, :]
    sin = jnp.sin(ang)[:, None, :]
    xr = x[..., :ROT_DIM].astype(F32)
    x1, x2 = xr[..., :half], xr[..., half:]
    rot = jnp.concatenate([x1 * cos - x2 * sin, x2 * cos + x1 * sin], axis=-1)
    return jnp.concatenate([rot.astype(x.dtype), x[..., ROT_DIM:]], axis=-1)


def _to_blocks(a):
    b, t = a.shape[:2]
    return a.reshape(b, t // QBLK, QBLK, *a.shape[2:]).swapaxes(0, 1)


def _from_blocks(a):
    nb, b, q = a.shape[:3]
    return a.swapaxes(0, 1).reshape(b, nb * q, *a.shape[3:])


def _dilated_window_attention(q, k, v, window, dilation):
    bsz, t, h, dh = q.shape
    span = window // dilation
    unit = dilation * span
    t_pad = -(-t // unit) * unit
    nb = t_pad // unit

    def prep(a):
        a = jnp.pad(a, ((0, 0), (0, t_pad - t), (0, 0), (0, 0)))
        a = a.reshape(bsz, t_pad // dilation, dilation, h, dh).transpose(0, 2, 3, 1, 4)
        return a.reshape(bsz, dilation, h, nb, span, dh)

    def with_prev(a):
        prev = jnp.pad(a, ((0, 0), (0, 0), (0, 0), (1, 0), (0, 0), (0, 0)))[:, :, :, :-1]
        return jnp.concatenate([prev, a], axis=4)

    qb = prep(q)
    kk = with_prev(prep(k))
    vv = with_prev(prep(v))
    s = jnp.einsum('brhnqe,brhnke->brhnqk', qb, kk, preferred_element_type=F32) * (dh ** -0.5)
    qi = jnp.arange(span)[:, None]
    kj = jnp.arange(2 * span)[None, :]
    dist = span + qi - kj
    band = (dist >= 0) & (dist <= span)
    before_start = (jnp.arange(nb) == 0)[:, None, None] & (kj < span)[None]
    mask = band[None] & ~before_start
    s = jnp.where(mask, s, -jnp.inf)
    lse = jax.nn.logsumexp(s, axis=-1)
    p = jnp.exp(s - lse[..., None])
    o = jnp.einsum('brhnqk,brhnke->brhnqe', p.astype(v.dtype), vv)
    o = o.reshape(bsz, dilation, h, t_pad // dilation, dh).transpose(0, 3, 1, 2, 4).reshape(bsz, t_pad, h, dh)[:, :t]
    lse = lse.reshape(bsz, dilation, h, t_pad // dilation).transpose(0, 3, 1, 2).reshape(bsz, t_pad, h)[:, :t]
    return o, lse


def _dsa_attention(q, k, v, q_idx, k_idx, w_idx):
    bsz, t, h, dh = q.shape
    topk = min(DSA_TOPK, t // 4)
    key_pos = jnp.arange(t)
    gather = jax.vmap(lambda arr, idx: arr[idx])

    def one_block(args):
        qb, qib, wb, t0 = args
        qpos = t0 + jnp.arange(QBLK)
        rel = jnp.maximum(jnp.einsum('bqhe,bse->bqhs', qib, k_idx, preferred_element_type=F32), 0.0)
        score = jnp.einsum('bqh,bqhs->bqs', wb.astype(F32), rel)
        causal = key_pos[None, :] <= qpos[:, None]
        score = jnp.where(causal[None], score, -jnp.inf)
        _, sel = lax.top_k(score, topk)
        valid = sel <= qpos[None, :, None]
        k_sel = gather(k, sel)
        v_sel = gather(v, sel)
        s = jnp.einsum('bqhe,bqkhe->bhqk', qb, k_sel, preferred_element_type=F32) * (dh ** -0.5)
        s = jnp.where(valid[:, None], s, -jnp.inf)
        p = jax.nn.softmax(s, axis=-1)
        return jnp.einsum('bhqk,bqkhe->bqhe', p.astype(v.dtype), v_sel)

    t0s = jnp.arange(t // QBLK) * QBLK
    out = lax.map(one_block, (_to_blocks(q), _to_blocks(q_idx), _to_blocks(w_idx), t0s))
    return _from_blocks(out)


def _stick_breaking_attention(q, k, v):
    bsz, t, h, dh = q.shape
    key_pos = jnp.arange(t)

    def one_block(args):
        qb, t0 = args
        qpos = t0 + jnp.arange(QBLK)
        z = jnp.einsum('bqhe,bshe->bhqs', qb, k, preferred_element_type=F32) * (dh ** -0.5)
        before = key_pos[None, :] < qpos[:, None]
        log_beta = jax.nn.log_sigmoid(z)
        log_keep = jnp.where(before, jax.nn.log_sigmoid(-z), 0.0)
        later = lax.cumsum(log_keep, axis=3, reverse=True) - log_keep
        a = jnp.where(before, jnp.exp(log_beta + later), 0.0)
        return jnp.einsum('bhqs,bshe->bqhe', a.astype(v.dtype), v)

    t0s = jnp.arange(t // QBLK) * QBLK
    return _from_blocks(lax.map(one_block, (_to_blocks(q), t0s)))


def _forgetting_attention(q, k, v, log_f):
    bsz, t, h, dh = q.shape
    cum = lax.cumsum(log_f, axis=1)
    cum_keys = cum.transpose(0, 2, 1)
    key_pos = jnp.arange(t)

    def one_block(args):
        qb, cq, t0 = args
        qpos = t0 + jnp.arange(QBLK)
        s = jnp.einsum('bqhe,bshe->bhqs', qb, k, preferred_element_type=F32) * (dh ** -0.5)
        s = s + cq.transpose(0, 2, 1)[..., None] - cum_keys[:, :, None, :]
        causal = key_pos[None, :] <= qpos[:, None]
        s = jnp.where(causal, s, -jnp.inf)
        p = jax.nn.softmax(s, axis=-1)
        return jnp.einsum('bhqs,bshe->bqhe', p.astype(v.dtype), v)

    t0s = jnp.arange(t // QBLK) * QBLK
    return _from_blocks(lax.map(one_block, (_to_blocks(q), _to_blocks(cum), t0s)))


def _hybrid_mixer(u, w_in, b_gate, b_forget, w_branch, w_out):
    bsz, t, _ = u.shape
    pos = jnp.arange(t)
    proj = u @ w_in
    a_qkv, b_qkv, i_q, i_k, i_w, c_qkv, d_qkv, f_logit, g_logit = jnp.split(proj, SPLIT_POINTS, axis=-1)

    def qkv(a, h):
        a = a.reshape(bsz, t, 3, h, HEAD_DIM)
        return a[:, :, 0], a[:, :, 1], a[:, :, 2]

    qa, ka, va = qkv(a_qkv, A_HEADS)
    qa, ka = _rope_partial(qa, pos), _rope_partial(ka, pos)
    outs, lses = [], []
    for g, (window, dilation) in enumerate(DILATED_GROUPS):
        hs = slice(g * A_HEADS_PER_GROUP, (g + 1) * A_HEADS_PER_GROUP)
        o, l = _dilated_window_attention(qa[:, :, hs], ka[:, :, hs], va[:, :, hs], window, dilation)
        outs.append(o)
        lses.append(l)
    wts = jax.nn.softmax(jnp.stack(lses), axis=0)
    y_a = jnp.sum(wts[..., None].astype(va.dtype) * jnp.stack(outs), axis=0).reshape(bsz, t, -1)

    qb, kb, vb = qkv(b_qkv, B_HEADS)
    qb, kb = _rope_partial(qb, pos), _rope_partial(kb, pos)
    q_idx = _rope_partial(i_q.reshape(bsz, t, IDX_HEADS, IDX_DIM), pos)
    k_idx = _rope_partial(i_k[:, :, None, :], pos)[:, :, 0]
    y_b = _dsa_attention(qb, kb, vb, q_idx, k_idx, i_w).reshape(bsz, t, -1)

    qc, kc, vc = qkv(c_qkv, C_HEADS)
    y_c = _stick_breaking_attention(qc, kc, vc).reshape(bsz, t, -1)

    qd, kd, vd = qkv(d_qkv, D_HEADS)
    log_f = jax.nn.log_sigmoid((f_logit + b_forget).astype(F32))
    y_d = _forgetting_attention(qd, kd, vd, log_f).reshape(bsz, t, -1)

    gates = jax.nn.sigmoid((g_logit + b_gate).astype(F32)).astype(u.dtype).reshape(bsz, t, N_BRANCH, D_MODEL)
    branches = (y_a, y_b, y_c, y_d)
    merged = gates[:, :, 0] * (y_a @ w_branch[BRANCH_OFFSETS[0]:BRANCH_OFFSETS[1]])
    for i in range(1, N_BRANCH):
        merged = merged + gates[:, :, i] * (branches[i] @ w_branch[BRANCH_OFFSETS[i]:BRANCH_OFFSETS[i + 1]])
    return merged @ w_out


def setup_inputs(seed: int = 0) -> dict:
    key = jax.random.key(seed)
    ks = jax.random.split(key, 16)
    D = D_MODEL

    def normal(k, shape, std):
        return jax.random.normal(k, shape, F32) * std

    std = D ** -0.5
    x = normal(ks[0], (BATCH, SEQ, D), 1.0)
    c = normal(ks[1], (BATCH, D), 1.0)
    ada_w = normal(ks[2], (DEPTH, D, 9 * D), 0.1 * std)
    ada_b = normal(ks[3], (DEPTH, 9 * D), 0.02)
    ln_g = 1.0 + normal(ks[4], (DEPTH, 3, D), 0.02)
    ln_b = normal(ks[5], (DEPTH, 3, D), 0.02)
    ffn_w_in = normal(ks[6], (DEPTH, 2, D, 2 * D_FF), std)
    ffn_w_out = normal(ks[7], (DEPTH, 2, D_FF, D), (D_FF ** -0.5) * DEEPNORM_BETA)

    pk = jax.random.split(ks[8], 13)

    def qkv_cols(k1, k2, h):
        return [normal(k1, (DEPTH, D, 2 * h * HEAD_DIM), std),
                normal(k2, (DEPTH, D, h * HEAD_DIM), std * DEEPNORM_BETA)]

    pieces = (qkv_cols(pk[0], pk[1], A_HEADS)
              + qkv_cols(pk[2], pk[3], B_HEADS)
              + [normal(pk[4], (DEPTH, D, IDX_Q_W), std),
                 normal(pk[5], (DEPTH, D, IDX_K_W), std),
                 normal(pk[6], (DEPTH, D, IDX_W_W), std)]
              + qkv_cols(pk[7], pk[8], C_HEADS)
              + qkv_cols(pk[9], pk[10], D_HEADS)
              + [normal(pk[11], (DEPTH, D, FG_W), std),
                 normal(pk[12], (DEPTH, D, GATE_W), std)])
    mix_w_in = jnp.concatenate(pieces, axis=-1)
    mix_b_gate = normal(ks[9], (DEPTH, GATE_W), 0.02)
    mix_b_forget = jax.random.uniform(ks[10], (DEPTH, D_HEADS), F32, 1.0, 4.0)
    bk = jax.random.split(ks[11], N_BRANCH)
    mix_w_branch = jnp.concatenate(
        [normal(bk[i], (DEPTH, BRANCH_WIDTHS[i], D), BRANCH_WIDTHS[i] ** -0.5) for i in range(N_BRANCH)], axis=1)
    mix_w_out = normal(ks[12], (DEPTH, D, D), std * DEEPNORM_BETA)
    return {'x': x, 'c': c, 'ada_w': ada_w, 'ada_b': ada_b, 'ln_g': ln_g, 'ln_b': ln_b,
            'ffn_w_in': ffn_w_in, 'ffn_w_out': ffn_w_out, 'mix_w_in': mix_w_in,
            'mix_b_gate': mix_b_gate, 'mix_b_forget': mix_b_forget,
            'mix_w_branch': mix_w_branch, 'mix_w_out': mix_w_out}


def reference(x, c, ada_w, ada_b, ln_g, ln_b, ffn_w_in, ffn_w_out, mix_w_in,
              mix_b_gate, mix_b_forget, mix_w_branch, mix_w_out):
    cond = jax.nn.silu(c)
    for l in range(DEPTH):
        mod = (cond @ ada_w[l] + ada_b[l]).reshape(-1, 3, 3, D_MODEL)

        h = 0.5 * _swiglu(_modulate(x, mod[:, 0, 0], mod[:, 0, 1]), ffn_w_in[l, 0], ffn_w_out[l, 0])
        x = _layer_norm(DEEPNORM_ALPHA * x + (1 + mod[:, 0, 2])[:, None, :] * h, ln_g[l, 0], ln_b[l, 0])

        h = _hybrid_mixer(_modulate(x, mod[:, 1, 0], mod[:, 1, 1]), mix_w_in[l], mix_b_gate[l],
                          mix_b_forget[l], mix_w_branch[l], mix_w_out[l])
        x = _layer_norm(DEEPNORM_ALPHA * x + (1 + mod[:, 1, 2])[:, None, :] * h, ln_g[l, 1], ln_b[l, 1])

        h = 0.5 * _swiglu(_modulate(x, mod[:, 2, 0], mod[:, 2, 1]), ffn_w_in[l, 1], ffn_w_out[l, 1])
        x = _layer_norm(DEEPNORM_ALPHA * x + (1 + mod[:, 2, 2])[:, None, :] * h, ln_g[l, 2], ln_b[l, 2])
    return x
```

```python
import numpy as np
import ml_dtypes
import concourse.bass as bass
import concourse.mybir as mybir
from concourse.bass_utils import run_bass_kernel_spmd
from contextlib import ExitStack

F32 = mybir.dt.float32
BF16 = mybir.dt.bfloat16
AF = mybir.ActivationFunctionType
ALU = mybir.AluOpType
AX = mybir.AxisListType

D = 1024
DFF = 2816
ALPHA = 4.0 ** 0.25
LN_EPS = 1e-5
NITER = 18
TOPK = 256
NDMA = 24
FFN_STOP = 9
ATT_STOP = 9
ATT_ONLY = -1
D_STOP = 9
ND_ON = True

A_QKV_W, B_QKV_W, IDX_Q_W, IDX_K_W, IDX_W_W, C_QKV_W, D_QKV_W, FG_W = 1152, 768, 256, 64, 4, 768, 768, 4
OFF_B = A_QKV_W
OFF_IQ = OFF_B + B_QKV_W
OFF_IK = OFF_IQ + IDX_Q_W
OFF_IW = OFF_IK + IDX_K_W
OFF_C = OFF_IW + IDX_W_W
OFF_D = OFF_C + C_QKV_W
OFF_FG = OFF_D + D_QKV_W
OFF_GATE = OFF_FG + FG_W

NFM = 34
NQK = 21
NROPE = 13
FGCOL = NFM * 128
TMCOL = FGCOL + 4
NTM = 1156
NC1 = TMCOL + NTM


class Buf:
    __slots__ = ("name", "w", "r")

    def __init__(self, name=""):
        self.name = name
        self.w = None
        self.r = {}


class Prog:
    def __init__(self, nc, es, sync_same=("act", "dve", "pool")):
        self.nc = nc
        self.eng = {"pe": nc.tensor, "act": nc.scalar, "dve": nc.vector, "pool": nc.gpsimd, "sp": nc.sync}
        self.sem = {k: es.enter_context(nc.semaphore("s_" + k)) for k in ("pe", "act", "dve", "pool")}
        self.cnt = {k: 0 for k in self.sem}
        self.seen = {k: {} for k in self.eng}
        self.dsem = [es.enter_context(nc.semaphore("d%d" % i)) for i in range(NDMA)]
        self.dcnt = [0] * NDMA
        self.drr = 0
        self.sync_same = set(sync_same)
        self.ninst = 0
        self.nwait = 0

    def _semh(self, key):
        return self.sem[key[1]] if key[0] == "e" else self.dsem[key[1]]

    def _deps(self, reads, writes):
        deps = {}
        for b in reads:
            if b.w is not None:
                k, v = b.w
                if deps.get(k, 0) < v:
                    deps[k] = v
        for b in writes:
            if b.w is not None:
                k, v = b.w
                if deps.get(k, 0) < v:
                    deps[k] = v
            for k, v in b.r.items():
                if deps.get(k, 0) < v:
                    deps[k] = v
        return deps

    def _waits(self, e, deps):
        seen = self.seen[e]
        for k, v in deps.items():
            if seen.get(k, 0) >= v:
                continue
            if k == ("e", e) and e not in self.sync_same:
                continue
            self.eng[e].wait_ge(self._semh(k), v)
            seen[k] = v
            self.nwait += 1

    def _mark(self, key, val, reads, writes):
        for b in reads:
            if b.r.get(key, 0) < val:
                b.r[key] = val
        for b in writes:
            b.w = (key, val)
            b.r = {}

    def op(self, e, fn, reads=(), writes=()):
        self._waits(e, self._deps(reads, writes))
        fn(self.eng[e]).then_inc(self.sem[e], 1)
        self.cnt[e] += 1
        self.ninst += 1
        self._mark(("e", e), self.cnt[e], reads, writes)

    def dma(self, out, in_, reads=(), writes=(), q="sp", **kw):
        i = self.drr
        self.drr = (i + 1) % NDMA
        deps = self._deps(reads, writes)
        if self.dcnt[i] > 0:
            deps[("d", i)] = self.dcnt[i]
        self._waits(q, deps)
        self.dcnt[i] += 16
        self.eng[q].dma_start(out=out, in_=in_, **kw).then_inc(self.dsem[i], 16)
        self.ninst += 1
        self._mark(("d", i), self.dcnt[i], reads, writes)

    def _all(self):
        deps = {("d", i): c for i, c in enumerate(self.dcnt) if c > 0}
        for k, c in self.cnt.items():
            if c > 0:
                deps[("e", k)] = c
        return deps

    def barrier(self):
        deps = self._all()
        for e in ("pe", "act", "dve", "pool", "sp"):
            self._waits(e, dict(deps))

    def finish(self):
        self._waits("sp", self._all())


def hr(h):
    return slice((h % 2) * 64, (h % 2) * 64 + 64)


def sl(h):
    return (h % 2) * 2 + h // 2


def v3(ap):
    return ap.rearrange("p (a b) -> p a b", a=2)


def build(T, dbg=False):
    NB = T // 128
    NT = T // 256
    nc = bass.Bass("TRN2", target_bir_lowering=False)

    def din(name, shape, dt=F32):
        return nc.dram_tensor(name, list(shape), dt, kind="ExternalInput").ap()

    def dscr(name, shape, dt=F32):
        if dbg:
            return nc.dram_tensor(name, list(shape), dt, kind="ExternalOutput").ap()
        return nc.dram_tensor(name, list(shape), dt).ap()

    x_in = din("x", [T, D])
    c_in = din("c", [D])
    ada_w = din("ada_w", [2, D, 9 * D])
    ada_b = din("ada_b", [2, 9 * D])
    ln_g = din("ln_g", [2, 3, D])
    ln_b = din("ln_b", [2, 3, D])
    w_in = din("ffn_w_in", [2, 2, D, 2 * DFF])
    w_out = din("ffn_w_out", [2, 2, DFF, D])
    wm1 = din("wm1", [2, D, NC1])
    wgate = din("wgate", [2, D, 4096])
    b_gate = din("b_gate", [2, 4096])
    b_forget = din("b_forget", [2, 4])
    w_branch = din("w_branch", [2, 896, D])
    w_o = din("w_o", [2, D, D])
    cf_in = din("cf", [128, 5, 128])
    cb_in = din("cb", [128, 15, 128], BF16)
    rope_in = din("rope", [2, 128, T])
    y_out = nc.dram_tensor("y", [T, D], F32, kind="ExternalOutput").ap()

    S1 = dscr("S1", [T, D])
    S2 = dscr("S2", [T, D])
    MODP = dscr("MODP", [2, 9 * D])
    QKT = dscr("QKT", [NQK, 128, T], BF16)
    FGT = dscr("FGT", [4, T])
    VTM = dscr("VTM", [T, 1152], BF16)
    IWT = dscr("IWT", [T, 4])
    GT = dscr("GT", [32, 128, T], BF16)
    YT = dscr("YT", [7, 128, T], BF16)
    b_S1, b_S2, b_MODP, b_QKT, b_FGT, b_VTM, b_IWT, b_GT, b_YT = [Buf() for _ in range(9)]

    with ExitStack() as es:
        P = Prog(nc, es)

        uid = [0]

        def sbt(st, name, shape, dt=F32):
            uid[0] += 1
            return st.enter_context(nc.sbuf_tensor("%s_%d" % (name, uid[0]), list(shape), dt))

        pqq = es.enter_context(nc.psum_tensor("pqq", [128, 4, 512], F32))
        pq = [pqq[:, i, :] for i in range(4)]
        SS = pqq[:, 0:2, 0:256]
        pw = es.enter_context(nc.psum_tensor("pw", [128, 4, 512], F32))
        b_pq = [Buf() for _ in range(4)]
        b_pw = [Buf() for _ in range(4)]

        cf = sbt(es, "cf", [128, 5, 128])
        cb = sbt(es, "cb", [128, 15, 128], BF16)
        b_c = Buf()
        P.dma(cf[:], cf_in, writes=[b_c])
        P.dma(cb[:], cb_in, writes=[b_c])
        ident, Uge, ones, E0, negmask = [cf[:, i, :] for i in range(5)]
        identb = cb[:, 0, :]
        LE4 = cb[:, 1:5, :]
        LT4 = cb[:, 5:9, :]
        MA4 = cb[:, 9:13, :]
        Ugeb = cb[:, 13, :]
        onesb = cb[:, 14, :]

        with ExitStack() as st:
            condT = sbt(st, "condT", [128, 8])
            modrow = sbt(st, "modrow", [1, 9 * D])
            brow = sbt(st, "brow", [1, 9 * D])
            wa = [sbt(st, "wa%d" % i, [128, 8, 512]) for i in range(2)]
            b_cond, b_mod, b_brow = Buf(), Buf(), Buf()
            b_wa = [Buf(), Buf()]
            P.dma(condT[:], c_in.rearrange("(c p) -> p c", p=128), writes=[b_cond], allow_slow_non_contiguous=True)
            P.op("act", lambda e: e.activation(out=condT[:], in_=condT[:], func=AF.Silu), reads=[b_cond], writes=[b_cond])
            for l in range(2):
                P.dma(brow[:], ada_b[l:l + 1, :], writes=[b_brow])
                for blk in range(18):
                    k = blk % 2
                    P.dma(wa[k][:], ada_w[l, :, blk * 512:(blk + 1) * 512].rearrange("(c p) n -> p c n", p=128), writes=[b_wa[k]])
                    for c in range(8):
                        P.op("pe", lambda e: e.matmul(pq[k][0:1, :], lhsT=condT[:, c:c + 1], rhs=wa[k][:, c, :], start=(c == 0), stop=(c == 7)),
                             reads=[b_cond, b_wa[k]], writes=[b_pq[k]])
                    P.op("dve", lambda e: e.tensor_tensor(out=modrow[:, blk * 512:(blk + 1) * 512], in0=pq[k][0:1, :], in1=brow[:, blk * 512:(blk + 1) * 512], op=ALU.add),
                         reads=[b_pq[k], b_brow], writes=[b_mod])
                for s in range(3):
                    sc_ = modrow[:, (s * 3 + 1) * D:(s * 3 + 2) * D]
                    gt_ = modrow[:, (s * 3 + 2) * D:(s * 3 + 3) * D]
                    P.op("dve", lambda e: e.tensor_scalar_add(out=sc_, in0=sc_, scalar1=1.0), reads=[b_mod], writes=[b_mod])
                    f = 1.0 if s == 1 else 0.5
                    P.op("dve", lambda e: e.tensor_scalar(out=gt_, in0=gt_, scalar1=1.0, scalar2=f, op0=ALU.add, op1=ALU.mult), reads=[b_mod], writes=[b_mod])
                P.dma(MODP[l:l + 1, :], modrow[:], reads=[b_mod], writes=[b_MODP])
        P.barrier()

        def load_bcast(tile, src_row, buf):
            P.dma(tile[:], src_row.partition_broadcast(128), reads=[b_MODP], writes=[buf])

        def make_uT(t0, xin, b_xin, xs, b_xs, u, b_u, xT, b_xT, s1, sh, b_ms):
            P.dma(xs[:], xin[t0:t0 + 256, :].rearrange("(j p) d -> p j d", p=128), reads=[b_xin], writes=[b_xs])
            for j in range(2):
                P.op("dve", lambda e: e.tensor_tensor(out=u[:, j, :], in0=xs[:, j, :], in1=s1[:], op=ALU.mult), reads=[b_xs, b_ms], writes=[b_u[j]])
                P.op("pool", lambda e: e.tensor_tensor(out=u[:, j, :], in0=u[:, j, :], in1=sh[:], op=ALU.add), reads=[b_u[j], b_ms], writes=[b_u[j]])
                for c in range(8):
                    P.op("pe", lambda e: e.transpose(out=pq[c // 4][:, (c % 4) * 128:(c % 4 + 1) * 128], in_=u[:, j, c * 128:(c + 1) * 128], identity=ident),
                         reads=[b_u[j], b_c], writes=[b_pq[c // 4]])
                for hh in range(2):
                    P.op("act", lambda e: e.activation(out=xT[:, hh * 4:(hh + 1) * 4, j * 128:(j + 1) * 128],
                                                       in_=pq[hh][:].rearrange("p (c n) -> p c n", c=4), func=AF.Copy),
                         reads=[b_pq[hh]], writes=[b_xT])

        def deepnorm_ln(j, xs, b_xs, gp, lng, lnb, b_ms, t1, r, b_t1, b_r, small, b_small, xo, b_xo):
            pwj = pw[:, 2 * j:2 * j + 2, :]
            P.op("dve", lambda e: e.tensor_tensor(out=t1[:].rearrange("p (a b) -> p a b", a=2), in0=pwj, in1=gp[:].rearrange("p (a b) -> p a b", a=2), op=ALU.mult),
                 reads=[b_pw[2 * j], b_pw[2 * j + 1], b_ms], writes=[b_t1])
            P.op("dve", lambda e: e.scalar_tensor_tensor(out=r[:], in0=xs[:, j, :], scalar=ALPHA, in1=t1[:], op0=ALU.mult, op1=ALU.add),
                 reads=[b_xs, b_t1], writes=[b_r])
            st6 = small[:, 0:12].rearrange("p (a b) -> p a b", a=2)
            for k in range(2):
                P.op("dve", lambda e: e.bn_stats(out=st6[:, k, :], in_=r[:, k * 512:(k + 1) * 512]), reads=[b_r], writes=[b_small])
            mv = small[:, 12:14]
            P.op("dve", lambda e: e.bn_aggr(out=mv, in_=st6), reads=[b_small], writes=[b_small])
            P.op("dve", lambda e: e.tensor_scalar_add(out=small[:, 14:15], in0=small[:, 13:14], scalar1=LN_EPS), reads=[b_small], writes=[b_small])
            P.op("act", lambda e: e.activation(out=small[:, 15:16], in_=small[:, 14:15], func=AF.Sqrt), reads=[b_small], writes=[b_small])
            P.op("dve", lambda e: e.reciprocal(out=small[:, 16:17], in_=small[:, 15:16]), reads=[b_small], writes=[b_small])
            P.op("dve", lambda e: e.tensor_scalar(out=small[:, 17:18], in0=small[:, 12:13], scalar1=small[:, 16:17], scalar2=-1.0, op0=ALU.mult, op1=ALU.mult),
                 reads=[b_small], writes=[b_small])
            P.op("act", lambda e: e.activation(out=t1[:], in_=r[:], func=AF.Identity, scale=small[:, 16:17], bias=small[:, 17:18]),
                 reads=[b_r, b_small], writes=[b_t1])
            P.op("dve", lambda e: e.tensor_tensor(out=t1[:], in0=t1[:], in1=lng[:], op=ALU.mult), reads=[b_t1, b_ms], writes=[b_t1])
            P.op("pool", lambda e: e.tensor_tensor(out=xo[:, j, :], in0=t1[:], in1=lnb[:], op=ALU.add), reads=[b_t1, b_ms], writes=[b_xo])

        def load_w_bf16(dst, src, kchunks, b_dst, piece=1408):
            N = src.shape[1]
            v = src.rearrange("(c p) n -> p c n", p=128)
            for c in range(kchunks):
                for n0 in range(0, N, piece):
                    n1 = min(N, n0 + piece)
                    P.dma(dst[:, c, n0:n1], v[:, c, n0:n1], writes=[b_dst], q="pool")

        def ffn_phase(l, f, sub, xin, b_xin, xout, b_xout):
            with ExitStack() as st:
                w1 = sbt(st, "w1", [128, 8, 2 * DFF], BF16)
                w2 = sbt(st, "w2", [128, 22, D], BF16)
                b_w1, b_w2, b_ms = Buf(), Buf(), Buf()
                load_w_bf16(w1, w_in[l, f], 8, b_w1)
                load_w_bf16(w2, w_out[l, f], 22, b_w2, piece=1024)
                s1, sh, gp, lng, lnb = [sbt(st, "m%d" % i, [128, D]) for i in range(5)]
                load_bcast(sh, MODP[l, (sub * 3 + 0) * D:(sub * 3 + 1) * D], b_ms)
                load_bcast(s1, MODP[l, (sub * 3 + 1) * D:(sub * 3 + 2) * D], b_ms)
                load_bcast(gp, MODP[l, (sub * 3 + 2) * D:(sub * 3 + 3) * D], b_ms)
                P.dma(lng[:], ln_g[l, sub, :].partition_broadcast(128), writes=[b_ms])
                P.dma(lnb[:], ln_b[l, sub, :].partition_broadcast(128), writes=[b_ms])
                xs = [sbt(st, "xs%d" % i, [128, 2, D]) for i in range(1)]
                b_xs = [Buf(), Buf()]
                u = sbt(st, "u", [128, 2, D])
                b_u = [Buf(), Buf()]
                xT = sbt(st, "xT", [128, 8, 256], BF16)
                b_xT = Buf()
                aT = sbt(st, "aT", [128, 22, 256], BF16)
                b_aT = Buf()
                sg = [sbt(st, "sg%d" % i, [128, 256]) for i in range(2)]
                b_sg = [Buf(), Buf()]
                t1 = sbt(st, "t1", [128, D])
                r = sbt(st, "r", [128, D])
                small = sbt(st, "small", [128, 32])
                b_t1, b_r, b_small = Buf(), Buf(), Buf()
                for ti in range(NT if FFN_STOP > 0 else 0):
                    t0 = ti * 256
                    k2 = 0
                    make_uT(t0, xin, b_xin, xs[k2], b_xs[k2], u, b_u, xT, b_xT, s1, sh, b_ms)
                    if FFN_STOP < 2:
                        continue
                    for m in range(22):
                        k = 2 + (m % 2)
                        for half in range(2):
                            col = half * DFF + m * 128
                            for c in range(8):
                                P.op("pe", lambda e: e.matmul(pq[k][:, half * 256:(half + 1) * 256], lhsT=w1[:, c, col:col + 128], rhs=xT[:, c, :], start=(c == 0), stop=(c == 7)),
                                     reads=[b_xT, b_w1], writes=[b_pq[k]])
                        P.op("act", lambda e: e.activation(out=sg[m % 2][:], in_=pq[k][:, 0:256], func=AF.Silu), reads=[b_pq[k]], writes=[b_sg[m % 2]])
                        P.op("dve", lambda e: e.tensor_tensor(out=aT[:, m, :], in0=sg[m % 2][:], in1=pq[k][:, 256:512], op=ALU.mult),
                             reads=[b_sg[m % 2], b_pq[k]], writes=[b_aT])
                    for j in range(2):
                        for nh in range(2):
                            for m in range(22):
                                P.op("pe", lambda e: e.matmul(pw[:, 2 * j + nh, :], lhsT=aT[:, m, j * 128:(j + 1) * 128], rhs=w2[:, m, nh * 512:(nh + 1) * 512], start=(m == 0), stop=(m == 21)),
                                     reads=[b_aT, b_w2], writes=[b_pw[2 * j + nh]])
                    if FFN_STOP < 3:
                        continue
                    for j in range(2):
                        deepnorm_ln(j, xs[k2], b_xs[k2], gp, lng, lnb, b_ms, t1, r, b_t1, b_r, small, b_small, u, b_u[j])
                    P.dma(xout[t0:t0 + 256, :].rearrange("(j p) d -> p j d", p=128), u[:], reads=[b_u[0], b_u[1]], writes=[b_xout])
            P.barrier()

        def inproj_phase(l, xin, b_xin):
            with ExitStack() as st:
                wm = sbt(st, "wm", [128, 8, NC1], BF16)
                b_wm, b_ms = Buf(), Buf()
                load_w_bf16(wm, wm1[l], 8, b_wm, piece=1024)
                s1, sh = [sbt(st, "m%d" % i, [128, D]) for i in range(2)]
                load_bcast(sh, MODP[l, 3 * D:4 * D], b_ms)
                load_bcast(s1, MODP[l, 4 * D:5 * D], b_ms)
                xs = sbt(st, "xs", [128, 2, D])
                b_xs = Buf()
                u = sbt(st, "u", [128, 2, D])
                b_u = [Buf(), Buf()]
                xT = sbt(st, "xT", [128, 8, 256], BF16)
                b_xT = Buf()
                rc = sbt(st, "rc", [128, 2, 256])
                b_rc = Buf()
                ta = [sbt(st, "ta%d" % i, [128, 256]) for i in range(2)]
                tb = [sbt(st, "tb%d" % i, [128, 256]) for i in range(2)]
                b_ta, b_tb = [Buf(), Buf()], [Buf(), Buf()]
                stg = sbt(st, "stg", [128, NQK, 256], BF16)
                fgs = sbt(st, "fgs", [4, 256])
                vst = sbt(st, "vst", [128, 2, 1152], BF16)
                iws = sbt(st, "iws", [128, 2, 4])
                b_stg, b_fgs, b_vst, b_iws = Buf(), Buf(), Buf(), Buf()
                for ti in range(NT):
                    t0 = ti * 256
                    make_uT(t0, xin, b_xin, xs, b_xs, u, b_u, xT, b_xT, s1, sh, b_ms)
                    P.dma(rc[:], rope_in[:, :, t0:t0 + 256].rearrange("a p t -> p a t"), writes=[b_rc])
                    for ch in range(NQK):
                        if ch < NROPE:
                            k = 2 * (ch % 2)
                            for which, cc in ((0, ch), (1, NQK + ch)):
                                for c in range(8):
                                    P.op("pe", lambda e: e.matmul(pw[:, k + which, 0:256], lhsT=wm[:, c, cc * 128:(cc + 1) * 128], rhs=xT[:, c, :], start=(c == 0), stop=(c == 7)),
                                         reads=[b_xT, b_wm], writes=[b_pw[k + which]])
                            kk = ch % 2
                            P.op("dve", lambda e: e.tensor_tensor(out=ta[kk][:], in0=pw[:, k, 0:256], in1=rc[:, 0, :], op=ALU.mult), reads=[b_pw[k], b_rc], writes=[b_ta[kk]])
                            P.op("dve", lambda e: e.tensor_tensor(out=tb[kk][:], in0=pw[:, k + 1, 0:256], in1=rc[:, 1, :], op=ALU.mult), reads=[b_pw[k + 1], b_rc], writes=[b_tb[kk]])
                            P.op("pool", lambda e: e.tensor_tensor(out=stg[:, ch, :], in0=ta[kk][:], in1=tb[kk][:], op=ALU.add), reads=[b_ta[kk], b_tb[kk]], writes=[b_stg])
                        else:
                            k = ch % 4
                            for c in range(8):
                                P.op("pe", lambda e: e.matmul(pw[:, k, 0:256], lhsT=wm[:, c, ch * 128:(ch + 1) * 128], rhs=xT[:, c, :], start=(c == 0), stop=(c == 7)),
                                     reads=[b_xT, b_wm], writes=[b_pw[k]])
                            P.op("act", lambda e: e.activation(out=stg[:, ch, :], in_=pw[:, k, 0:256], func=AF.Copy), reads=[b_pw[k]], writes=[b_stg])
                    for c in range(8):
                        P.op("pe", lambda e: e.matmul(pw[0:4, 0, 0:256], lhsT=wm[:, c, FGCOL:FGCOL + 4], rhs=xT[:, c, :], start=(c == 0), stop=(c == 7)),
                             reads=[b_xT, b_wm], writes=[b_pw[0]])
                    P.op("act", lambda e: e.activation(out=fgs[:], in_=pw[0:4, 0, 0:256], func=AF.Copy), reads=[b_pw[0]], writes=[b_fgs])
                    P.dma(FGT[:, t0:t0 + 256], fgs[:], reads=[b_fgs], writes=[b_FGT])
                    P.dma(QKT[:, :, t0:t0 + 256].rearrange("c p t -> p c t"), stg[:], reads=[b_stg], writes=[b_QKT])
                    ki = 0
                    for j in range(2):
                        for (n0, n1) in ((0, 384), (384, 896), (896, 1156)):
                            k = 2 + (ki % 2)
                            ki += 1
                            for c in range(8):
                                P.op("pe", lambda e: e.matmul(pq[k][:, 0:n1 - n0], lhsT=xT[:, c, j * 128:(j + 1) * 128], rhs=wm[:, c, TMCOL + n0:TMCOL + n1], start=(c == 0), stop=(c == 7)),
                                     reads=[b_xT, b_wm], writes=[b_pq[k]])
                            if n1 <= 1152:
                                P.op("act", lambda e: e.activation(out=vst[:, j, n0:n1], in_=pq[k][:, 0:n1 - n0], func=AF.Copy), reads=[b_pq[k]], writes=[b_vst])
                            else:
                                P.op("act", lambda e: e.activation(out=vst[:, j, n0:1152], in_=pq[k][:, 0:1152 - n0], func=AF.Copy), reads=[b_pq[k]], writes=[b_vst])
                                P.op("dve", lambda e: e.tensor_copy(out=iws[:, j, :], in_=pq[k][:, 1152 - n0:1156 - n0]), reads=[b_pq[k]], writes=[b_iws])
                    P.dma(VTM[t0:t0 + 256, :].rearrange("(j p) n -> p j n", p=128), vst[:], reads=[b_vst], writes=[b_VTM])
                    P.dma(IWT[t0:t0 + 256, :].rearrange("(j p) n -> p j n", p=128), iws[:], reads=[b_iws], writes=[b_IWT])
            P.barrier()
            with ExitStack() as st:
                wg = sbt(st, "wg", [128, 8, 4096], BF16)
                b_wg, b_ms = Buf(), Buf()
                load_w_bf16(wg, wgate[l], 8, b_wg, piece=1024)
                s1, sh = [sbt(st, "m%d" % i, [128, D]) for i in range(2)]
                load_bcast(sh, MODP[l, 3 * D:4 * D], b_ms)
                load_bcast(s1, MODP[l, 4 * D:5 * D], b_ms)
                bg = sbt(st, "bg", [128, 32])
                P.dma(bg[:], b_gate[l].rearrange("(c p) -> p c", p=128), writes=[b_ms], allow_slow_non_contiguous=True)
                xs = sbt(st, "xs", [128, 2, D])
                b_xs = Buf()
                u = sbt(st, "u", [128, 2, D])
                b_u = [Buf(), Buf()]
                xT = sbt(st, "xT", [128, 8, 256], BF16)
                b_xT = Buf()
                gst = [sbt(st, "gst%d" % i, [128, 32, 256], BF16) for i in range(2)]
                b_gst = [Buf(), Buf()]
                for ti in range(NT):
                    t0 = ti * 256
                    g2 = ti % 2
                    make_uT(t0, xin, b_xin, xs, b_xs, u, b_u, xT, b_xT, s1, sh, b_ms)
                    for ch in range(32):
                        k = ch % 4
                        for c in range(8):
                            P.op("pe", lambda e: e.matmul(pw[:, k, 0:256], lhsT=wg[:, c, ch * 128:(ch + 1) * 128], rhs=xT[:, c, :], start=(c == 0), stop=(c == 7)),
                                 reads=[b_xT, b_wg], writes=[b_pw[k]])
                        P.op("act", lambda e: e.activation(out=gst[g2][:, ch, :], in_=pw[:, k, 0:256], func=AF.Sigmoid, bias=bg[:, ch:ch + 1]), reads=[b_pw[k], b_ms], writes=[b_gst[g2]])
                    P.dma(GT[:, :, t0:t0 + 256].rearrange("c p t -> p c t"), gst[g2][:], reads=[b_gst[g2]], writes=[b_GT])
            P.barrier()

        def attention_phase(l):
            with ExitStack() as sm:
                cumT = sbt(sm, "cumT", [128, NB * 4])
                Gb = sbt(sm, "Gb", [128, NB * 4])
                ones64 = sbt(sm, "ones64", [128, 64])
                b_cum = Buf()
                b_o64 = Buf()
                P.op("pool", lambda e: e.memset(ones64[:], 1.0), writes=[b_o64])
                with ExitStack() as st:
                    fg = sbt(st, "fg", [4, T])
                    sp = sbt(st, "sp", [4, T])
                    on4 = sbt(st, "on4", [4, T])
                    nb4 = sbt(st, "nb4", [4, 1])
                    b_fg, b_sp, b_on, b_nb = Buf(), Buf(), Buf(), Buf()
                    P.dma(fg[:], FGT, reads=[b_FGT], writes=[b_fg])
                    P.dma(nb4[:], b_forget[l].rearrange("(p a) -> p a", a=1), writes=[b_nb])
                    P.op("dve", lambda e: e.tensor_scalar_mul(out=nb4[:], in0=nb4[:], scalar1=-1.0), reads=[b_nb], writes=[b_nb])
                    P.op("pool", lambda e: e.memset(on4[:], 1.0), writes=[b_on])
                    P.op("act", lambda e: e.activation(out=sp[:], in_=fg[:], func=AF.Exp, scale=-1.0, bias=nb4[:, 0:1]), reads=[b_fg, b_nb], writes=[b_sp])
                    P.op("act", lambda e: e.activation(out=sp[:], in_=sp[:], func=AF.Ln, bias=1.0), reads=[b_sp], writes=[b_sp])
                    P.op("dve", lambda e: e.tensor_tensor_scan(out=fg[:], data0=on4[:], data1=sp[:], initial=0.0, op0=ALU.mult, op1=ALU.subtract),
                         reads=[b_on, b_sp], writes=[b_fg])
                    for blk in range(NB):
                        P.op("pe", lambda e: e.transpose(out=pq[0][:, blk * 4:(blk + 1) * 4], in_=fg[0:4, blk * 128:(blk + 1) * 128], identity=ident[0:4, 0:4]),
                             reads=[b_fg, b_c], writes=[b_pq[0]])
                    P.op("act", lambda e: e.activation(out=cumT[:], in_=pq[0][:, 0:NB * 4], func=AF.Copy), reads=[b_pq[0]], writes=[b_cum])
                    P.op("pe", lambda e: e.matmul(pq[1][:, 0:NB * 4], lhsT=E0, rhs=cumT[:], start=True, stop=True), reads=[b_cum, b_c], writes=[b_pq[1]])
                    P.op("act", lambda e: e.activation(out=Gb[:], in_=pq[1][:, 0:NB * 4], func=AF.Copy), reads=[b_pq[1]], writes=[b_cum])
                P.barrier()

                def load_kqv(st, qch, kch, vcol, nh, noq=False):
                    kt = sbt(st, "kt", [128, nh // 2, T], BF16)
                    qt = sbt(st, "qt", [128, nh // 2, 128 if noq else T], BF16)
                    va = sbt(st, "va", [128, NB, nh, 65], BF16)
                    b_k, b_q, b_v = Buf(), Buf(), Buf()
                    P.dma(kt[:], QKT[kch:kch + nh // 2].rearrange("c p t -> p c t"), reads=[b_QKT], writes=[b_k])
                    if not noq:
                        P.dma(qt[:], QKT[qch:qch + nh // 2].rearrange("c p t -> p c t"), reads=[b_QKT], writes=[b_q])
                    P.op("pool", lambda e: e.memset(va[:, :, :, 64:65], 1.0), writes=[b_v])
                    for h in range(nh):
                        P.dma(va[:, :, h, 0:64], VTM[:, vcol + h * 64:vcol + (h + 1) * 64].rearrange("(k p) e -> p k e", p=128), reads=[b_VTM], writes=[b_v])
                    return kt, qt, va, b_k, b_q, b_v

                Sk = [pqq[:, 0:2, 0:256], pqq[:, 2:4, 0:256]]
                b_Sk = [[b_pq[0], b_pq[1]], [b_pq[2], b_pq[3]]]
                po4 = pw[:, 0, :].rearrange("p (h t) -> p h t", h=4)
                b_po = b_pw[0]
                pbk = pw[:, 1, :]
                b_pb = b_pw[1]

                def sreg(k, h):
                    return pqq[:, 2 * k + h % 2, (h // 2) * 128:(h // 2 + 1) * 128]

                def qk4(k, kt, qt, b_k, b_q, i, j):
                    for h in range(4):
                        P.op("pe", lambda e: e.matmul(sreg(k, h), lhsT=kt[hr(h), h // 2, j * 128:(j + 1) * 128], rhs=qt[hr(h), h // 2, i * 128:(i + 1) * 128], start=True, stop=True),
                             reads=[b_k, b_q], writes=[b_pq[2 * k + h % 2]])

                def av4(pT, b_pT, va, b_v, j, first, last, M):
                    for h in range(4):
                        P.op("pe", lambda e: e.matmul(po4[0:M, h, :], lhsT=va[:, j, h, 0:M], rhs=pT[:, sl(h), :], start=(first and h == 0), stop=last, skip_group_check=True),
                             reads=[b_pT, b_v], writes=[b_po])

                def normalize_store(rrow, b_rr, rb, b_rb, yst, b_yst, ych, i):
                    P.op("dve", lambda e: e.reciprocal(out=rrow[64:65, :, :], in_=po4[64:65, :, :]), reads=[b_po], writes=[b_rr])
                    P.op("pe", lambda e: e.matmul(pbk[0:64, :], lhsT=ones64[64:65, :], rhs=rrow[64:65, :, :].rearrange("p a b -> p (a b)"), start=True, stop=True),
                         reads=[b_rr, b_o64], writes=[b_pb])
                    P.op("act", lambda e: e.activation(out=rb[0:64, :, :].rearrange("p a b -> p (a b)"), in_=pbk[0:64, :], func=AF.Copy), reads=[b_pb], writes=[b_rb])
                    P.op("dve", lambda e: e.tensor_tensor(out=yst[0:64, :, :], in0=po4[0:64, :, :], in1=rb[0:64, :, :], op=ALU.mult), reads=[b_po, b_rb], writes=[b_yst])
                    P.dma(YT[ych:ych + 2, :, i * 128:(i + 1) * 128].rearrange("c (h e) t -> e (c h) t", e=64), yst[0:64, :, :], reads=[b_yst], writes=[b_YT])

                def flat(t):
                    return t[:].rearrange("p a b -> p (a b)")

                with ExitStack() as st:
                  if ATT_STOP >= 1 and ATT_ONLY in (-1, 1):
                      kt, qt, va, b_k, b_q, b_v = load_kqv(st, 17, 19, 896, 4)
                      bm = [sbt(st, "bm%d" % i, [128, 4, NB]) for i in range(2)]
                      b_bm = [Buf(), Buf()]
                      tS = [sbt(st, "tS%d" % i, [128, 4, 128]) for i in range(2)]
                      b_tS = [Buf(), Buf()]
                      pT = [sbt(st, "pT%d" % i, [128, 4, 128], BF16) for i in range(2)]
                      b_pT = [Buf(), Buf()]
                      rrow = sbt(st, "rrow", [128, 4, 128])
                      rb = sbt(st, "rb", [64, 4, 128])
                      yst = sbt(st, "yst", [64, 4, 128], BF16)
                      b_rr, b_rb, b_yst = Buf(), Buf(), Buf()
                      cum3 = cumT[:].rearrange("p (k h) -> p k h", h=4)
                      pc = 0
                      for i in range(NB):
                          bi = i % 2
                          for h in range(4):
                              P.op("dve", lambda e: e.tensor_scalar(out=bm[bi][:, h, 0:i + 1], in0=cum3[:, 0:i + 1, h], scalar1=-1.0, scalar2=Gb[:, i * 4 + h:i * 4 + h + 1], op0=ALU.mult, op1=ALU.add),
                                   reads=[b_cum], writes=[b_bm[bi]])
                          qk4(0, kt, qt, b_k, b_q, i, 0)
                          for j in range(i + 1):
                              k = j % 2
                              if j < i:
                                  qk4((j + 1) % 2, kt, qt, b_k, b_q, i, j + 1)
                              kp = pc % 2
                              pc += 1
                              for h in (1, 3):
                                  P.op("dve", lambda e: e.tensor_scalar(out=tS[kp][:, sl(h), :], in0=sreg(k, h), scalar1=0.125, scalar2=bm[bi][:, h, j:j + 1], op0=ALU.mult, op1=ALU.add),
                                       reads=[b_pq[2 * k + h % 2], b_bm[bi]], writes=[b_tS[kp]])
                              for h in (0, 2):
                                  P.op("act", lambda e: e.activation(out=pT[kp][:, sl(h), :], in_=sreg(k, h), func=AF.Exp, scale=0.125, bias=bm[bi][:, h, j:j + 1]),
                                       reads=[b_pq[2 * k + h % 2], b_bm[bi]], writes=[b_pT[kp]])
                              P.op("act", lambda e: e.activation(out=pT[kp][:, 2:4, :], in_=tS[kp][:, 2:4, :], func=AF.Exp), reads=[b_tS[kp]], writes=[b_pT[kp]])
                              if j == i:
                                  P.op("pool", lambda e: e.tensor_tensor(out=pT[kp][:], in0=pT[kp][:], in1=LE4, op=ALU.mult), reads=[b_pT[kp], b_c], writes=[b_pT[kp]])
                              av4(pT[kp], b_pT[kp], va, b_v, j, j == 0, j == i, 65)
                          normalize_store(rrow, b_rr, rb, b_rb, yst, b_yst, 5, i)
                P.barrier()

                with ExitStack() as st:
                  if ATT_STOP >= 2 and ATT_ONLY in (-1, 2):
                      kt, qt, va, b_k, b_q, b_v = load_kqv(st, 13, 15, 640, 4)
                      ef = sbt(st, "ef", [128, 512])
                      spf = [sbt(st, "spf%d" % i, [128, 512]) for i in range(2)]
                      tt = sbt(st, "tt", [128, 512])
                      arg = sbt(st, "arg", [128, 512])
                      carry = sbt(st, "carry", [128, 512])
                      b_ef, b_tt, b_arg, b_carry = Buf(), Buf(), Buf(), Buf()
                      b_spf = [Buf(), Buf()]
                      zf = [sbt(st, "zf%d" % i, [128, 512]) for i in range(2)]
                      b_zf = [Buf(), Buf()]
                      shi = [sbt(st, "shi%d" % i, [128, 512], BF16) for i in range(2)]
                      slo = [sbt(st, "slo%d" % i, [128, 512], BF16) for i in range(2)]
                      b_shi, b_slo = [Buf(), Buf()], [Buf(), Buf()]
                      pT = [sbt(st, "pT%d" % i, [128, 4, 128], BF16) for i in range(2)]
                      b_pT = [Buf(), Buf()]
                      yst = sbt(st, "yst", [64, 4, 128], BF16)
                      b_yst = Buf()
                      LT4f = LT4.rearrange("p a b -> p (a b)")
                      pA2 = [pw[:, 2, :], pw[:, 1, :]]
                      b_pA2 = [b_pw[2], b_pw[1]]
                      pB = pw[:, 3, :]
                      b_pB = b_pw[3]
                      carry2 = [sbt(st, "carry2_%d" % i, [128, 512]) for i in range(2)]
                      b_carry2 = [Buf(), Buf()]
                      zc = sbt(st, "zc", [128, 512])
                      b_zc = Buf()

                      def c_front(i, j, k):
                          qk4(k, kt, qt, b_k, b_q, i, j)
                          P.op("act", lambda e: e.activation(out=v3(ef[:]), in_=Sk[k], func=AF.Exp, scale=0.125), reads=b_Sk[k], writes=[b_ef])
                          P.op("act", lambda e: e.activation(out=v3(zf[k][:]), in_=Sk[k], func=AF.Copy, scale=0.125), reads=b_Sk[k], writes=[b_zf[k]])
                          P.op("act", lambda e: e.activation(out=spf[k][:], in_=ef[:], func=AF.Ln, bias=1.0), reads=[b_ef], writes=[b_spf[k]])
                          if j == i:
                              P.op("dve", lambda e: e.tensor_tensor(out=spf[k][:], in0=spf[k][:], in1=LT4f, op=ALU.mult), reads=[b_spf[k], b_c], writes=[b_spf[k]])
                          P.op("dve", lambda e: e.tensor_copy(out=shi[k][:], in_=spf[k][:]), reads=[b_spf[k]], writes=[b_shi[k]])
                          P.op("pool", lambda e: e.tensor_tensor(out=slo[k][:], in0=spf[k][:], in1=shi[k][:], op=ALU.subtract), reads=[b_spf[k], b_shi[k]], writes=[b_slo[k]])

                      def c_back(i, j, k, n):
                          ka = n % 2
                          co, cn = n % 2, (n + 1) % 2
                          P.op("pool", lambda e: e.tensor_tensor(out=zc[:], in0=zf[k][:], in1=carry2[co][:], op=ALU.subtract), reads=[b_zf[k], b_carry2[co]], writes=[b_zc])
                          P.op("pe", lambda e: e.matmul(pA2[ka], lhsT=Ugeb, rhs=shi[k][:], start=True, stop=False), reads=[b_shi[k], b_c], writes=[b_pA2[ka]])
                          P.op("pe", lambda e: e.matmul(pA2[ka], lhsT=Ugeb, rhs=slo[k][:], start=False, stop=True), reads=[b_slo[k], b_c], writes=[b_pA2[ka]])
                          if j > 0:
                              P.op("pe", lambda e: e.matmul(pB, lhsT=onesb, rhs=shi[k][:], start=True, stop=False), reads=[b_shi[k], b_c], writes=[b_pB])
                              P.op("pe", lambda e: e.matmul(pB, lhsT=onesb, rhs=slo[k][:], start=False, stop=True), reads=[b_slo[k], b_c], writes=[b_pB])
                              P.op("dve", lambda e: e.tensor_tensor(out=carry2[cn][:], in0=pB, in1=carry2[co][:], op=ALU.add), reads=[b_pB, b_carry2[co]], writes=[b_carry2[cn]])
                          P.op("dve", lambda e: e.scalar_tensor_tensor(out=arg[:], in0=pA2[ka], scalar=-1.0, in1=zc[:], op0=ALU.mult, op1=ALU.add), reads=[b_pA2[ka], b_zc], writes=[b_arg])
                          P.op("act", lambda e: e.activation(out=flat(pT[k]), in_=arg[:], func=AF.Exp), reads=[b_arg], writes=[b_pT[k]])
                          if j == i:
                              P.op("pool", lambda e: e.tensor_tensor(out=pT[k][:], in0=pT[k][:], in1=LT4, op=ALU.mult), reads=[b_pT[k], b_c], writes=[b_pT[k]])

                      def c_av(i, j, k):
                          av4(pT[k], b_pT[k], va, b_v, j, j == i, j == 0, 64)

                      for i in range(NB):
                          P.op("pool", lambda e: e.memset(carry2[0][:], 0.0), writes=[b_carry2[0]])
                          pairs = list(range(i, -1, -1))
                          m_ = len(pairs)
                          c_front(i, pairs[0], 0)
                          if m_ > 1:
                              c_front(i, pairs[1], 1)
                          c_back(i, pairs[0], 0, 0)
                          for n, j in enumerate(pairs):
                              if n + 2 < m_:
                                  c_front(i, pairs[n + 2], (n + 2) % 2)
                              if n + 1 < m_:
                                  c_back(i, pairs[n + 1], (n + 1) % 2, n + 1)
                              c_av(i, j, n % 2)
                          P.op("act", lambda e: e.activation(out=yst[0:64, :, :], in_=po4[0:64, :, :], func=AF.Copy), reads=[b_po], writes=[b_yst])
                          P.dma(YT[3:5, :, i * 128:(i + 1) * 128].rearrange("c (h e) t -> e (c h) t", e=64), yst[0:64, :, :], reads=[b_yst], writes=[b_YT])
                P.barrier()

                with ExitStack() as st:
                  if ATT_STOP >= 3 and ATT_ONLY in (-1, 3):
                      kt, qt_full, va, b_k, b_q_full, b_v = load_kqv(st, 6, 8, 384, 4, noq=True)
                      qtb = [sbt(st, "qtb%d" % i, [128, 2, 128], BF16) for i in range(2)]
                      qib = [sbt(st, "qib%d" % i, [128, 2, 128], BF16) for i in range(2)]
                      b_qtb = [Buf(), Buf()]
                      kit = sbt(st, "kit", [128, T], BF16)
                      wi = sbt(st, "wi", [128, NB, 4])
                      b_qi = Buf()
                      P.dma(kit[:], QKT[12], reads=[b_QKT], writes=[b_qi])
                      P.dma(wi[:], IWT.rearrange("(k p) h -> p k h", p=128), reads=[b_IWT], writes=[b_qi])
                      sc = sbt(st, "sc", [128, T])
                      mk = [sbt(st, "mk%d" % i, [128, T], BF16) for i in range(2)]
                      rl = [sbt(st, "rl%d" % i, [128, 512]) for i in range(2)]
                      b_sc = Buf()
                      b_mk = [Buf(), Buf()]
                      b_jD = [Buf(), Buf()]
                      b_jA = [Buf(), Buf()]
                      b_rl = [Buf(), Buf()]
                      sm_ = sbt(st, "smallb", [128, 12])
                      b_sm = Buf()
                      b_sA = Buf()
                      b_nm = Buf()
                      lo, hi, w0, nmid, cnt, ge, mid, tcb = [sm_[:, a:a + 1] for a in range(8)]
                      sA = sm_[:, 8:9]
                      pT = [sbt(st, "pT%d" % i, [128, 4, 128], BF16) for i in range(2)]
                      b_pT = [Buf(), Buf()]
                      rrow = sbt(st, "rrow", [128, 4, 128])
                      rb = sbt(st, "rb", [64, 4, 128])
                      yst = sbt(st, "yst", [64, 4, 128], BF16)
                      b_rr, b_rb, b_yst = Buf(), Buf(), Buf()

                      pmk = pw[:, 2, :].bitcast(BF16)
                      b_pmk = b_pw[2]
                      ibank = [pw[:, 3, :], pw[:, 1, :]]
                      b_ibank = [b_pw[3], b_pw[1]]
                      midm = sm_[:, 9:10]
                      b_mm = Buf()

                      def b_tr(m, j):
                          for h in range(4):
                              P.op("pe", lambda e: e.transpose(out=pmk[:, h * 128:(h + 1) * 128], in_=mk[m][:, j * 128:(j + 1) * 128], identity=identb),
                                   reads=[b_mk[m], b_c], writes=[b_pmk])

                      def b_front(qb_, j, k):
                          qk4(k, kt, qtb[qb_], b_k, b_qtb[qb_], 0, j)
                          P.op("act", lambda e: e.activation(out=v3(flat(pT[k])), in_=Sk[k], func=AF.Exp, scale=0.125), reads=b_Sk[k], writes=[b_pT[k]])

                      def att_gen(i):
                          qb_ = i % 2
                          m = i % 2
                          b_tr(m, 0)
                          b_front(qb_, 0, 0)
                          for j in range(i + 1):
                              k = j % 2
                              if j < i:
                                  b_front(qb_, j + 1, (j + 1) % 2)
                              P.op("dve", lambda e: e.tensor_tensor(out=flat(pT[k]), in0=flat(pT[k]), in1=pmk[:, 0:512], op=ALU.mult),
                                   reads=[b_pT[k], b_pmk], writes=[b_pT[k]])
                              if j < i:
                                  b_tr(m, j + 1)
                              av4(pT[k], b_pT[k], va, b_v, j, j == 0, j == i, 65)
                              yield
                          normalize_store(rrow, b_rr, rb, b_rb, yst, b_yst, 1, i)
                          yield

                      def sel_gen(i):
                          n = 128 * (i + 1)
                          nchunk = (n + 511) // 512
                          qb_ = i % 2
                          m = i % 2
                          P.dma(qtb[qb_][:], QKT[6:8, :, i * 128:(i + 1) * 128].rearrange("c p t -> p c t"), reads=[b_QKT], writes=[b_qtb[qb_]])
                          P.dma(qib[qb_][:], QKT[10:12, :, i * 128:(i + 1) * 128].rearrange("c p t -> p c t"), reads=[b_QKT], writes=[b_qtb[qb_]])
                          for c in range(nchunk):
                              w = min(512, n - 512 * c)
                              for h in range(4):
                                  k = h % 2
                                  P.op("pe", lambda e: e.matmul(ibank[k][:, 0:w], lhsT=qib[qb_][hr(h), h // 2, :], rhs=kit[hr(h), c * 512:c * 512 + w], start=True, stop=True),
                                       reads=[b_qi, b_qtb[qb_]], writes=[b_ibank[k]])
                                  P.op("act", lambda e: e.activation(out=rl[k][:, 0:w], in_=ibank[k][:, 0:w], func=AF.Relu), reads=[b_ibank[k]], writes=[b_rl[k]])
                                  if h == 0:
                                      P.op("dve", lambda e: e.tensor_scalar(out=sc[:, c * 512:c * 512 + w], in0=rl[k][:, 0:w], scalar1=wi[:, i, 0:1], scalar2=None, op0=ALU.mult),
                                           reads=[b_rl[k], b_qi], writes=[b_sc])
                                  else:
                                      P.op("dve", lambda e: e.scalar_tensor_tensor(out=sc[:, c * 512:c * 512 + w], in0=rl[k][:, 0:w], scalar=wi[:, i, h:h + 1], in1=sc[:, c * 512:c * 512 + w], op0=ALU.mult, op1=ALU.add),
                                           reads=[b_rl[k], b_qi, b_sc], writes=[b_sc])
                              yield
                          P.op("dve", lambda e: e.tensor_reduce(out=lo, in_=sc[:, 0:n], axis=AX.X, op=ALU.min), reads=[b_sc], writes=[b_sm])
                          P.op("dve", lambda e: e.tensor_reduce(out=hi, in_=sc[:, 0:n], axis=AX.X, op=ALU.max), reads=[b_sc], writes=[b_sm])
                          P.op("dve", lambda e: e.tensor_tensor(out=sc[:, n - 128:n], in0=sc[:, n - 128:n], in1=negmask, op=ALU.add), reads=[b_sc, b_c], writes=[b_sc])
                          P.op("dve", lambda e: e.tensor_tensor(out=w0, in0=hi, in1=lo, op=ALU.subtract), reads=[b_sm], writes=[b_sm])
                          P.op("dve", lambda e: e.memset(mk[m][:, 0:2], 0.0), writes=[b_mk[m], b_jD[m], b_jA[m]])
                          P.op("dve", lambda e: e.scalar_tensor_tensor(out=mid, in0=w0, scalar=0.5, in1=lo, op0=ALU.mult, op1=ALU.add), reads=[b_sm], writes=[b_nm])
                          yield
                          nd = max(0, ((int(0.476 * n) - 460) // 64) * 64) if ND_ON else 0
                          na = n - nd
                          for it in range(1, NITER + 1):
                              f = 2.0 ** (-it)
                              P.op("act", lambda e: e.activation(out=mk[m][:, nd:n], in_=sc[:, nd:n], func=AF.Sign, bias=mid, scale=-1.0, accum_out=sA),
                                   reads=[b_sc, b_nm], writes=[b_jA[m], b_sA])
                              if nd > 0:
                                  P.op("dve", lambda e: e.tensor_scalar(out=mk[m][:, 0:nd], in0=sc[:, 0:nd], scalar1=mid, scalar2=None, op0=ALU.is_ge, op1=ALU.add, accum_out=cnt),
                                       reads=[b_sc, b_nm], writes=[b_jD[m], b_sm])
                              P.op("dve", lambda e: e.scalar_tensor_tensor(out=midm, in0=w0, scalar=-0.5 * f, in1=mid, op0=ALU.mult, op1=ALU.add), reads=[b_sm, b_nm], writes=[b_mm])
                              if nd > 0:
                                  P.op("dve", lambda e: e.scalar_tensor_tensor(out=cnt, in0=sA, scalar=-0.5, in1=cnt, op0=ALU.mult, op1=ALU.add), reads=[b_sA, b_sm], writes=[b_sm])
                                  P.op("dve", lambda e: e.tensor_scalar(out=ge, in0=cnt, scalar1=TOPK - 0.5 - na / 2.0, scalar2=f, op0=ALU.is_ge, op1=ALU.mult), reads=[b_sm], writes=[b_sm])
                              else:
                                  P.op("dve", lambda e: e.tensor_scalar(out=ge, in0=sA, scalar1=float(n - 2 * TOPK + 1), scalar2=f, op0=ALU.is_le, op1=ALU.mult), reads=[b_sA], writes=[b_sm])
                              P.op("dve", lambda e: e.scalar_tensor_tensor(out=mid, in0=ge, scalar=w0, in1=midm, op0=ALU.mult, op1=ALU.add), reads=[b_sm, b_mm], writes=[b_nm])
                              yield
                          P.op("dve", lambda e: e.scalar_tensor_tensor(out=lo, in0=w0, scalar=-(2.0 ** (-NITER)), in1=mid, op0=ALU.mult, op1=ALU.add), reads=[b_sm, b_nm], writes=[b_sm])
                          P.op("dve", lambda e: e.tensor_scalar(out=mk[m][:, 0:n], in0=sc[:, 0:n], scalar1=lo, scalar2=None, op0=ALU.is_ge), reads=[b_sc, b_sm], writes=[b_mk[m], b_jD[m], b_jA[m]])
                          yield

                      def merge(ga, na_, gs, ns_):
                          ia = is_ = 0
                          da = ga is None
                          ds = gs is None
                          while not (da and ds):
                              if not da and (ds or ia * ns_ <= is_ * na_):
                                  try:
                                      next(ga)
                                      ia += 1
                                  except StopIteration:
                                      da = True
                              else:
                                  try:
                                      next(gs)
                                      is_ += 1
                                  except StopIteration:
                                      ds = True

                      merge(None, 1, sel_gen(0), 1)
                      for i in range(NB):
                          gs = sel_gen(i + 1) if i + 1 < NB else None
                          ns_ = ((128 * (i + 2) + 511) // 512) + NITER + 2
                          merge(att_gen(i), i + 2, gs, ns_)
                P.barrier()


                with ExitStack() as st:
                  if ATT_STOP >= 4 and ATT_ONLY in (-1, 4):
                      accA = sbt(st, "accA", [65, 2, T])
                      b_acc = Buf()
                      kt = sbt(st, "kt", [128, T], BF16)
                      qt = sbt(st, "qt", [128, T], BF16)
                      va = sbt(st, "va", [128, NB, 2, 65], BF16)
                      b_k, b_q, b_v = Buf(), Buf(), Buf()
                      pT = [sbt(st, "pT%d" % i, [128, 4, 128], BF16) for i in range(2)]
                      b_pT = [Buf(), Buf()]
                      P.op("pool", lambda e: e.memset(va[:, :, :, 64:65], 1.0), writes=[b_v])
                      pc = 0
                      for g, d in enumerate((1, 4, 16)):
                          nbs = NB // d
                          P.dma(kt[:], QKT[3 + g], reads=[b_QKT], writes=[b_k])
                          P.dma(qt[:], QKT[g], reads=[b_QKT], writes=[b_q])
                          vsrc = VTM[:, g * 128:(g + 1) * 128].rearrange("(k p r) (h e) -> r p k h e", p=128, r=d, e=64)
                          for r_ in range(d):
                              for hh in range(2):
                                  P.dma(va[:, r_ * nbs:(r_ + 1) * nbs, hh, 0:64], vsrc[r_, :, :, hh, :], reads=[b_VTM], writes=[b_v])
                          for r_ in range(d):
                              for kb in range(nbs):
                                  k = pc % 2
                                  pc += 1

                                  def tok(kk):
                                      base = r_ + d * 128 * kk
                                      return slice(base, base + d * 127 + 1, d) if d > 1 else slice(base, base + 128)
                                  for hh in range(2):
                                      for wch in range(2):
                                          if kb == 0 and wch == 0:
                                              continue
                                          P.op("pe", lambda e: e.matmul(pqq[:, hh, wch * 128:(wch + 1) * 128], lhsT=kt[hr(hh), tok(kb - 1 + wch)], rhs=qt[hr(hh), tok(kb)], start=True, stop=True),
                                               reads=[b_k, b_q], writes=[b_pq[hh]])
                                  if kb == 0:
                                      for hh in range(2):
                                          P.op("act", lambda e: e.activation(out=pT[k][:, hh * 2 + 1, :], in_=pqq[:, hh, 128:256], func=AF.Exp, scale=0.125), reads=[b_pq[hh]], writes=[b_pT[k]])
                                          P.op("pool", lambda e: e.tensor_tensor(out=pT[k][:, hh * 2 + 1, :], in0=pT[k][:, hh * 2 + 1, :], in1=MA4[:, hh * 2 + 1, :], op=ALU.mult), reads=[b_pT[k], b_c], writes=[b_pT[k]])
                                  else:
                                      P.op("act", lambda e: e.activation(out=v3(pT[k][:].rearrange("p a b -> p (a b)")), in_=SS, func=AF.Exp, scale=0.125), reads=[b_pq[0], b_pq[1]], writes=[b_pT[k]])
                                      P.op("pool", lambda e: e.tensor_tensor(out=pT[k][:], in0=pT[k][:], in1=MA4, op=ALU.mult), reads=[b_pT[k], b_c], writes=[b_pT[k]])
                                  for hh in range(2):
                                      if kb > 0:
                                          P.op("pe", lambda e: e.matmul(pw[0:65, hh, 0:128], lhsT=va[:, r_ * nbs + kb - 1, hh, :], rhs=pT[k][:, hh * 2, :], start=True, stop=False),
                                               reads=[b_pT[k], b_v], writes=[b_pw[hh]])
                                      P.op("pe", lambda e: e.matmul(pw[0:65, hh, 0:128], lhsT=va[:, r_ * nbs + kb, hh, :], rhs=pT[k][:, hh * 2 + 1, :], start=(kb == 0), stop=True),
                                           reads=[b_pT[k], b_v], writes=[b_pw[hh]])
                                  dst = accA[0:65, :, tok(kb)]
                                  if g == 0:
                                      P.op("dve", lambda e: e.tensor_copy(out=dst, in_=pw[0:65, 0:2, 0:128]), reads=[b_pw[0], b_pw[1]], writes=[b_acc])
                                  else:
                                      P.op("dve", lambda e: e.tensor_tensor(out=dst, in0=pw[0:65, 0:2, 0:128], in1=dst, op=ALU.add), reads=[b_pw[0], b_pw[1], b_acc], writes=[b_acc])
                      rrow = sbt(st, "rrowA", [128, 2, 256])
                      ysa = sbt(st, "ysa", [64, 2, 256], BF16)
                      b_rr, b_ys = Buf(), Buf()
                      for ti in range(NT):
                          t0 = ti * 256
                          P.op("dve", lambda e: e.reciprocal(out=rrow[64:65, :, :], in_=accA[64:65, :, t0:t0 + 256]), reads=[b_acc], writes=[b_rr])
                          P.op("pe", lambda e: e.matmul(pq[2][0:64, :], lhsT=ones64[64:65, :], rhs=rrow[64:65, :, :].rearrange("p a b -> p (a b)"), start=True, stop=True),
                               reads=[b_rr, b_o64], writes=[b_pq[2]])
                          P.op("dve", lambda e: e.tensor_tensor(out=ysa[:], in0=accA[0:64, :, t0:t0 + 256], in1=pq[2][0:64, :].rearrange("p (a b) -> p a b", a=2), op=ALU.mult),
                               reads=[b_acc, b_pq[2]], writes=[b_ys])
                          P.dma(YT[0, :, t0:t0 + 256].rearrange("(h e) t -> e h t", e=64), ysa[:], reads=[b_ys], writes=[b_YT])
            P.barrier()

        def post_phase(l, xin, b_xin, xout, b_xout):
            with ExitStack() as st:
                wbr = sbt(st, "wbr", [128, 7, D], BF16)
                wo = sbt(st, "wo", [128, 8, D], BF16)
                b_wbr, b_wo, b_ms = Buf(), Buf(), Buf()
                load_w_bf16(wbr, w_branch[l], 7, b_wbr, piece=1024)
                load_w_bf16(wo, w_o[l], 8, b_wo, piece=1024)
                gp, lng, lnb = [sbt(st, "m%d" % i, [128, D]) for i in range(3)]
                load_bcast(gp, MODP[l, 5 * D:6 * D], b_ms)
                P.dma(lng[:], ln_g[l, 1, :].partition_broadcast(128), writes=[b_ms])
                P.dma(lnb[:], ln_b[l, 1, :].partition_broadcast(128), writes=[b_ms])
                xs = [sbt(st, "xs%d" % i, [128, 2, D]) for i in range(2)]
                b_xs = [Buf(), Buf()]
                yt = [sbt(st, "yt%d" % i, [128, 7, 256], BF16) for i in range(2)]
                gt = [sbt(st, "gt%d" % i, [128, 32, 256], BF16) for i in range(2)]
                b_yt, b_gt = [Buf(), Buf()], [Buf(), Buf()]
                mg = sbt(st, "mg", [128, 8, 256])
                mgb = sbt(st, "mgb", [128, 8, 256], BF16)
                tmp = [sbt(st, "tmp%d" % i, [128, 256]) for i in range(2)]
                b_mg, b_mgb = Buf(), Buf()
                b_tmp = [Buf(), Buf()]
                xo = sbt(st, "xo", [128, 2, D])
                b_xo = [Buf(), Buf()]
                t1 = sbt(st, "t1", [128, D])
                r = sbt(st, "r", [128, D])
                small = sbt(st, "small", [128, 32])
                b_t1, b_r, b_small = Buf(), Buf(), Buf()
                kch = ((0, 1), (1, 3), (3, 5), (5, 7))
                pc = 0
                for ti in range(NT):
                    t0 = ti * 256
                    k2 = ti % 2
                    P.dma(xs[k2][:], xin[t0:t0 + 256, :].rearrange("(j p) d -> p j d", p=128), reads=[b_xin], writes=[b_xs[k2]])
                    P.dma(yt[k2][:], YT[:, :, t0:t0 + 256].rearrange("c p t -> p c t"), reads=[b_YT], writes=[b_yt[k2]])
                    P.dma(gt[k2][:], GT[:, :, t0:t0 + 256].rearrange("c p t -> p c t"), reads=[b_GT], writes=[b_gt[k2]])
                    for fo in range(8):
                        for bi in range(4):
                            k = pc % 4
                            pc += 1
                            a, b = kch[bi]
                            for kc in range(a, b):
                                P.op("pe", lambda e: e.matmul(pq[k][:, 0:256], lhsT=wbr[:, kc, fo * 128:(fo + 1) * 128], rhs=yt[k2][:, kc, :], start=(kc == a), stop=(kc == b - 1)),
                                     reads=[b_wbr, b_yt[k2]], writes=[b_pq[k]])
                            if bi == 0:
                                P.op("dve", lambda e: e.tensor_tensor(out=mg[:, fo, :], in0=pq[k][:, 0:256], in1=gt[k2][:, bi * 8 + fo, :], op=ALU.mult), reads=[b_pq[k], b_gt[k2]], writes=[b_mg])
                            else:
                                kk = pc % 2
                                P.op("dve", lambda e: e.tensor_tensor(out=tmp[kk][:], in0=pq[k][:, 0:256], in1=gt[k2][:, bi * 8 + fo, :], op=ALU.mult), reads=[b_pq[k], b_gt[k2]], writes=[b_tmp[kk]])
                                if bi < 3:
                                    P.op("pool", lambda e: e.tensor_tensor(out=mg[:, fo, :], in0=mg[:, fo, :], in1=tmp[kk][:], op=ALU.add), reads=[b_mg, b_tmp[kk]], writes=[b_mg])
                                else:
                                    P.op("pool", lambda e: e.tensor_tensor(out=mgb[:, fo, :], in0=mg[:, fo, :], in1=tmp[kk][:], op=ALU.add), reads=[b_mg, b_tmp[kk]], writes=[b_mgb])
                    for j in range(2):
                        for nh in range(2):
                            for kc in range(8):
                                P.op("pe", lambda e: e.matmul(pw[:, 2 * j + nh, :], lhsT=mgb[:, kc, j * 128:(j + 1) * 128], rhs=wo[:, kc, nh * 512:(nh + 1) * 512], start=(kc == 0), stop=(kc == 7)),
                                     reads=[b_mgb, b_wo], writes=[b_pw[2 * j + nh]])
                    for j in range(2):
                        deepnorm_ln(j, xs[k2], b_xs[k2], gp, lng, lnb, b_ms, t1, r, b_t1, b_r, small, b_small, xo, b_xo[j])
                    P.dma(xout[t0:t0 + 256, :].rearrange("(j p) d -> p j d", p=128), xo[:], reads=[b_xo[0], b_xo[1]], writes=[b_xout])
            P.barrier()

        b_xin0 = Buf()
        b_y = Buf()
        dbg_stage = dbg if isinstance(dbg, int) and not isinstance(dbg, bool) else 99
        stages = 0
        cur, b_cur = x_in, b_xin0
        for l in range(2):
            last = (l == 1)
            if dbg_stage >= 1:
                ffn_phase(l, 0, 0, cur, b_cur, S1, b_S1)
            if dbg_stage >= 2:
                inproj_phase(l, S1, b_S1)
            if dbg_stage >= 3:
                attention_phase(l)
            if dbg_stage >= 4:
                post_phase(l, S1, b_S1, S2, b_S2)
            if dbg_stage >= 5:
                ffn_phase(l, 1, 2, S2, b_S2, y_out if last else S1, b_y if last else b_S1)
            cur, b_cur = S1, b_S1
            if dbg_stage < 99:
                break
        P.finish()
        build.stats = (P.ninst, P.nwait)
    return nc


def _consts(T):
    p = np.arange(128)
    cf = np.zeros((128, 5, 128), np.float32)
    cf[:, 0, :] = np.eye(128)
    cf[:, 1, :] = (p[:, None] >= p[None, :])
    cf[:, 2, :] = 1.0
    cf[0, 3, :] = 1.0
    cf[:, 4, :] = np.where(p[None, :] > p[:, None], -1e30, 0.0)
    cb = np.zeros((128, 15, 128), np.float32)
    cb[:, 13, :] = (p[:, None] >= p[None, :])
    cb[:, 14, :] = 1.0
    cb[:, 0, :] = np.eye(128)
    le = (p[:, None] <= p[None, :]).astype(np.float32)
    lt = (p[:, None] < p[None, :]).astype(np.float32)
    gev = (p[:, None] >= p[None, :]).astype(np.float32)
    for h in range(4):
        cb[:, 1 + h, :] = le
        cb[:, 5 + h, :] = lt
    for hh in range(2):
        cb[:, 9 + hh * 2 + 0, :] = gev
        cb[:, 9 + hh * 2 + 1, :] = le
    half = 8
    inv = 500000.0 ** (-(np.arange(half, dtype=np.float32) * (2.0 / 16)))
    ang = np.arange(T, dtype=np.float32)[None, :] * inv[:, None].astype(np.float32)
    cos, sin = np.cos(ang).astype(np.float32), np.sin(ang).astype(np.float32)
    C = np.ones((64, T), np.float32)
    S = np.zeros((64, T), np.float32)
    C[0:8], C[8:16] = cos, cos
    S[0:8], S[8:16] = -sin, sin
    rope = np.stack([np.concatenate([C, C], 0), np.concatenate([S, S], 0)], 0)
    return cf, cb.astype(ml_dtypes.bfloat16), rope


def _wm_cols():
    ar = np.arange
    chunks = []
    for g in range(3):
        chunks.append(ar(g * 128, (g + 1) * 128))
    for g in range(3):
        chunks.append(384 + ar(g * 128, (g + 1) * 128))
    for g in range(2):
        chunks.append(OFF_B + ar(g * 128, (g + 1) * 128))
    for g in range(2):
        chunks.append(OFF_B + 256 + ar(g * 128, (g + 1) * 128))
    for g in range(2):
        chunks.append(OFF_IQ + ar(g * 128, (g + 1) * 128))
    chunks.append(np.concatenate([OFF_IK + ar(64), OFF_IK + ar(64)]))
    for base in (OFF_C, OFF_C + 256, OFF_D, OFF_D + 256):
        for g in range(2):
            chunks.append(base + ar(g * 128, (g + 1) * 128))
    perm64 = np.arange(64)
    perm64[0:8] = np.arange(8, 16)
    perm64[8:16] = np.arange(0, 8)
    perm128 = np.concatenate([perm64, 64 + perm64])
    for ch in range(NROPE):
        chunks.append(chunks[ch][perm128])
    cols = np.concatenate(chunks + [OFF_FG + ar(4), 768 + ar(384), OFF_B + 512 + ar(256), OFF_C + 512 + ar(256),
                                    OFF_D + 512 + ar(256), OFF_IW + ar(4)])
    assert cols.shape[0] == NC1
    return cols


_NC_CACHE = {}


def _run(T, per_core, dbg=False):
    key = (T, dbg)
    if key not in _NC_CACHE:
        _NC_CACHE[key] = build(T, dbg)
    nc = _NC_CACHE[key]
    res = run_bass_kernel_spmd(nc, per_core, core_ids=list(range(len(per_core))))
    return res


def make_in_maps(T, x, c, ada_w, ada_b, ln_g, ln_b, ffn_w_in, ffn_w_out, mix_w_in, mix_b_gate, mix_b_forget,
                 mix_w_branch, mix_w_out):
    f = lambda a: np.ascontiguousarray(np.asarray(a, dtype=np.float32))
    cf, cb, rope = _consts(T)
    cols = _wm_cols()
    mw = np.asarray(mix_w_in, dtype=np.float32)
    shared = {
        "ada_w": f(ada_w), "ada_b": f(ada_b), "ln_g": f(ln_g), "ln_b": f(ln_b),
        "ffn_w_in": f(ffn_w_in), "ffn_w_out": f(ffn_w_out),
        "wm1": f(mw[:, :, cols]), "wgate": f(mw[:, :, OFF_GATE:OFF_GATE + 4096]),
        "b_gate": f(mix_b_gate), "b_forget": f(mix_b_forget), "w_branch": f(mix_w_branch), "w_o": f(mix_w_out),
        "cf": cf, "cb": cb, "rope": rope,
    }
    xs = np.asarray(x, dtype=np.float32)
    cs = np.asarray(c, dtype=np.float32)
    maps = []
    for b in range(xs.shape[0]):
        m = dict(shared)
        m["x"] = f(xs[b])
        m["c"] = f(cs[b])
        maps.append(m)
    return maps


def kernel(x, c, ada_w, ada_b, ln_g, ln_b, ffn_w_in, ffn_w_out, mix_w_in, mix_b_gate, mix_b_forget,
           mix_w_branch, mix_w_out):
    B, T, _ = np.asarray(x).shape
    maps = make_in_maps(T, x, c, ada_w, ada_b, ln_g, ln_b, ffn_w_in, ffn_w_out, mix_w_in, mix_b_gate,
                        mix_b_forget, mix_w_branch, mix_w_out)
    per_core = [maps[i] for i in range(B)]
    res = _run(T, per_core)
    out = np.stack([np.asarray(res.results[b]["y"], dtype=np.float32) for b in range(B)], axis=0)
    return out
```

```python
import numpy as np
import ml_dtypes
import concourse.bass as bass
import concourse.mybir as mybir
from concourse.bass_utils import run_bass_kernel_spmd
from contextlib import ExitStack

F32 = mybir.dt.float32
BF16 = mybir.dt.bfloat16
AF = mybir.ActivationFunctionType
ALU = mybir.AluOpType
AX = mybir.AxisListType

D = 1024
DFF = 2816
ALPHA = 4.0 ** 0.25
LN_EPS = 1e-5
NITER = 18
TOPK = 256
NDMA = 24
FFN_STOP = 9
ATT_STOP = 9
ATT_ONLY = -1
D_STOP = 9

A_QKV_W, B_QKV_W, IDX_Q_W, IDX_K_W, IDX_W_W, C_QKV_W, D_QKV_W, FG_W = 1152, 768, 256, 64, 4, 768, 768, 4
OFF_B = A_QKV_W
OFF_IQ = OFF_B + B_QKV_W
OFF_IK = OFF_IQ + IDX_Q_W
OFF_IW = OFF_IK + IDX_K_W
OFF_C = OFF_IW + IDX_W_W
OFF_D = OFF_C + C_QKV_W
OFF_FG = OFF_D + D_QKV_W
OFF_GATE = OFF_FG + FG_W

NFM = 34
NQK = 21
NROPE = 13
FGCOL = NFM * 128
TMCOL = FGCOL + 4
NTM = 1156
NC1 = TMCOL + NTM


class Buf:
    __slots__ = ("name", "w", "r")

    def __init__(self, name=""):
        self.name = name
        self.w = None
        self.r = {}


class Prog:
    def __init__(self, nc, es, sync_same=("act", "dve", "pool")):
        self.nc = nc
        self.eng = {"pe": nc.tensor, "act": nc.scalar, "dve": nc.vector, "pool": nc.gpsimd, "sp": nc.sync}
        self.sem = {k: es.enter_context(nc.semaphore("s_" + k)) for k in ("pe", "act", "dve", "pool")}
        self.cnt = {k: 0 for k in self.sem}
        self.seen = {k: {} for k in self.eng}
        self.dsem = [es.enter_context(nc.semaphore("d%d" % i)) for i in range(NDMA)]
        self.dcnt = [0] * NDMA
        self.drr = 0
        self.sync_same = set(sync_same)
        self.ninst = 0
        self.nwait = 0

    def _semh(self, key):
        return self.sem[key[1]] if key[0] == "e" else self.dsem[key[1]]

    def _deps(self, reads, writes):
        deps = {}
        for b in reads:
            if b.w is not None:
                k, v = b.w
                if deps.get(k, 0) < v:
                    deps[k] = v
        for b in writes:
            if b.w is not None:
                k, v = b.w
                if deps.get(k, 0) < v:
                    deps[k] = v
            for k, v in b.r.items():
                if deps.get(k, 0) < v:
                    deps[k] = v
        return deps

    def _waits(self, e, deps):
        seen = self.seen[e]
        for k, v in deps.items():
            if seen.get(k, 0) >= v:
                continue
            if k == ("e", e) and e not in self.sync_same:
                continue
            self.eng[e].wait_ge(self._semh(k), v)
            seen[k] = v
            self.nwait += 1

    def _mark(self, key, val, reads, writes):
        for b in reads:
            if b.r.get(key, 0) < val:
                b.r[key] = val
        for b in writes:
            b.w = (key, val)
            b.r = {}

    def op(self, e, fn, reads=(), writes=()):
        self._waits(e, self._deps(reads, writes))
        fn(self.eng[e]).then_inc(self.sem[e], 1)
        self.cnt[e] += 1
        self.ninst += 1
        self._mark(("e", e), self.cnt[e], reads, writes)

    def dma(self, out, in_, reads=(), writes=(), q="sp", **kw):
        i = self.drr
        self.drr = (i + 1) % NDMA
        deps = self._deps(reads, writes)
        if self.dcnt[i] > 0:
            deps[("d", i)] = self.dcnt[i]
        self._waits(q, deps)
        self.dcnt[i] += 16
        self.eng[q].dma_start(out=out, in_=in_, **kw).then_inc(self.dsem[i], 16)
        self.ninst += 1
        self._mark(("d", i), self.dcnt[i], reads, writes)

    def _all(self):
        deps = {("d", i): c for i, c in enumerate(self.dcnt) if c > 0}
        for k, c in self.cnt.items():
            if c > 0:
                deps[("e", k)] = c
        return deps

    def barrier(self):
        deps = self._all()
        for e in ("pe", "act", "dve", "pool", "sp"):
            self._waits(e, dict(deps))

    def finish(self):
        self._waits("sp", self._all())


def hr(h):
    return slice((h % 2) * 64, (h % 2) * 64 + 64)


def sl(h):
    return (h % 2) * 2 + h // 2


def v3(ap):
    return ap.rearrange("p (a b) -> p a b", a=2)


def build(T, dbg=False):
    NB = T // 128
    NT = T // 256
    nc = bass.Bass("TRN2", target_bir_lowering=False)

    def din(name, shape, dt=F32):
        return nc.dram_tensor(name, list(shape), dt, kind="ExternalInput").ap()

    def dscr(name, shape, dt=F32):
        if dbg:
            return nc.dram_tensor(name, list(shape), dt, kind="ExternalOutput").ap()
        return nc.dram_tensor(name, list(shape), dt).ap()

    x_in = din("x", [T, D])
    c_in = din("c", [D])
    ada_w = din("ada_w", [2, D, 9 * D])
    ada_b = din("ada_b", [2, 9 * D])
    ln_g = din("ln_g", [2, 3, D])
    ln_b = din("ln_b", [2, 3, D])
    w_in = din("ffn_w_in", [2, 2, D, 2 * DFF])
    w_out = din("ffn_w_out", [2, 2, DFF, D])
    wm1 = din("wm1", [2, D, NC1])
    wgate = din("wgate", [2, D, 4096])
    b_gate = din("b_gate", [2, 4096])
    b_forget = din("b_forget", [2, 4])
    w_branch = din("w_branch", [2, 896, D])
    w_o = din("w_o", [2, D, D])
    cf_in = din("cf", [128, 5, 128])
    cb_in = din("cb", [128, 15, 128], BF16)
    rope_in = din("rope", [2, 128, T])
    y_out = nc.dram_tensor("y", [T, D], F32, kind="ExternalOutput").ap()

    S1 = dscr("S1", [T, D])
    S2 = dscr("S2", [T, D])
    MODP = dscr("MODP", [2, 9 * D])
    QKT = dscr("QKT", [NQK, 128, T], BF16)
    FGT = dscr("FGT", [4, T])
    VTM = dscr("VTM", [T, 1152], BF16)
    IWT = dscr("IWT", [T, 4])
    GT = dscr("GT", [32, 128, T], BF16)
    YT = dscr("YT", [7, 128, T], BF16)
    b_S1, b_S2, b_MODP, b_QKT, b_FGT, b_VTM, b_IWT, b_GT, b_YT = [Buf() for _ in range(9)]

    with ExitStack() as es:
        P = Prog(nc, es)

        uid = [0]

        def sbt(st, name, shape, dt=F32):
            uid[0] += 1
            return st.enter_context(nc.sbuf_tensor("%s_%d" % (name, uid[0]), list(shape), dt))

        pqq = es.enter_context(nc.psum_tensor("pqq", [128, 4, 512], F32))
        pq = [pqq[:, i, :] for i in range(4)]
        SS = pqq[:, 0:2, 0:256]
        pw = es.enter_context(nc.psum_tensor("pw", [128, 4, 512], F32))
        b_pq = [Buf() for _ in range(4)]
        b_pw = [Buf() for _ in range(4)]

        cf = sbt(es, "cf", [128, 5, 128])
        cb = sbt(es, "cb", [128, 15, 128], BF16)
        b_c = Buf()
        P.dma(cf[:], cf_in, writes=[b_c])
        P.dma(cb[:], cb_in, writes=[b_c])
        ident, Uge, ones, E0, negmask = [cf[:, i, :] for i in range(5)]
        identb = cb[:, 0, :]
        LE4 = cb[:, 1:5, :]
        LT4 = cb[:, 5:9, :]
        MA4 = cb[:, 9:13, :]
        Ugeb = cb[:, 13, :]
        onesb = cb[:, 14, :]

        with ExitStack() as st:
            condT = sbt(st, "condT", [128, 8])
            modrow = sbt(st, "modrow", [1, 9 * D])
            brow = sbt(st, "brow", [1, 9 * D])
            wa = [sbt(st, "wa%d" % i, [128, 8, 512]) for i in range(2)]
            b_cond, b_mod, b_brow = Buf(), Buf(), Buf()
            b_wa = [Buf(), Buf()]
            P.dma(condT[:], c_in.rearrange("(c p) -> p c", p=128), writes=[b_cond], allow_slow_non_contiguous=True)
            P.op("act", lambda e: e.activation(out=condT[:], in_=condT[:], func=AF.Silu), reads=[b_cond], writes=[b_cond])
            for l in range(2):
                P.dma(brow[:], ada_b[l:l + 1, :], writes=[b_brow])
                for blk in range(18):
                    k = blk % 2
                    P.dma(wa[k][:], ada_w[l, :, blk * 512:(blk + 1) * 512].rearrange("(c p) n -> p c n", p=128), writes=[b_wa[k]])
                    for c in range(8):
                        P.op("pe", lambda e: e.matmul(pq[k][0:1, :], lhsT=condT[:, c:c + 1], rhs=wa[k][:, c, :], start=(c == 0), stop=(c == 7)),
                             reads=[b_cond, b_wa[k]], writes=[b_pq[k]])
                    P.op("dve", lambda e: e.tensor_tensor(out=modrow[:, blk * 512:(blk + 1) * 512], in0=pq[k][0:1, :], in1=brow[:, blk * 512:(blk + 1) * 512], op=ALU.add),
                         reads=[b_pq[k], b_brow], writes=[b_mod])
                for s in range(3):
                    sc_ = modrow[:, (s * 3 + 1) * D:(s * 3 + 2) * D]
                    gt_ = modrow[:, (s * 3 + 2) * D:(s * 3 + 3) * D]
                    P.op("dve", lambda e: e.tensor_scalar_add(out=sc_, in0=sc_, scalar1=1.0), reads=[b_mod], writes=[b_mod])
                    f = 1.0 if s == 1 else 0.5
                    P.op("dve", lambda e: e.tensor_scalar(out=gt_, in0=gt_, scalar1=1.0, scalar2=f, op0=ALU.add, op1=ALU.mult), reads=[b_mod], writes=[b_mod])
                P.dma(MODP[l:l + 1, :], modrow[:], reads=[b_mod], writes=[b_MODP])
        P.barrier()

        def load_bcast(tile, src_row, buf):
            P.dma(tile[:], src_row.partition_broadcast(128), reads=[b_MODP], writes=[buf])

        def make_uT(t0, xin, b_xin, xs, b_xs, u, b_u, xT, b_xT, s1, sh, b_ms):
            P.dma(xs[:], xin[t0:t0 + 256, :].rearrange("(j p) d -> p j d", p=128), reads=[b_xin], writes=[b_xs])
            for j in range(2):
                P.op("dve", lambda e: e.tensor_tensor(out=u[:, j, :], in0=xs[:, j, :], in1=s1[:], op=ALU.mult), reads=[b_xs, b_ms], writes=[b_u[j]])
                P.op("pool", lambda e: e.tensor_tensor(out=u[:, j, :], in0=u[:, j, :], in1=sh[:], op=ALU.add), reads=[b_u[j], b_ms], writes=[b_u[j]])
                for c in range(8):
                    P.op("pe", lambda e: e.transpose(out=pq[c // 4][:, (c % 4) * 128:(c % 4 + 1) * 128], in_=u[:, j, c * 128:(c + 1) * 128], identity=ident),
                         reads=[b_u[j], b_c], writes=[b_pq[c // 4]])
                for hh in range(2):
                    P.op("act", lambda e: e.activation(out=xT[:, hh * 4:(hh + 1) * 4, j * 128:(j + 1) * 128],
                                                       in_=pq[hh][:].rearrange("p (c n) -> p c n", c=4), func=AF.Copy),
                         reads=[b_pq[hh]], writes=[b_xT])

        def deepnorm_ln(j, xs, b_xs, gp, lng, lnb, b_ms, t1, r, b_t1, b_r, small, b_small, xo, b_xo):
            pwj = pw[:, 2 * j:2 * j + 2, :]
            P.op("dve", lambda e: e.tensor_tensor(out=t1[:].rearrange("p (a b) -> p a b", a=2), in0=pwj, in1=gp[:].rearrange("p (a b) -> p a b", a=2), op=ALU.mult),
                 reads=[b_pw[2 * j], b_pw[2 * j + 1], b_ms], writes=[b_t1])
            P.op("dve", lambda e: e.scalar_tensor_tensor(out=r[:], in0=xs[:, j, :], scalar=ALPHA, in1=t1[:], op0=ALU.mult, op1=ALU.add),
                 reads=[b_xs, b_t1], writes=[b_r])
            st6 = small[:, 0:12].rearrange("p (a b) -> p a b", a=2)
            for k in range(2):
                P.op("dve", lambda e: e.bn_stats(out=st6[:, k, :], in_=r[:, k * 512:(k + 1) * 512]), reads=[b_r], writes=[b_small])
            mv = small[:, 12:14]
            P.op("dve", lambda e: e.bn_aggr(out=mv, in_=st6), reads=[b_small], writes=[b_small])
            P.op("dve", lambda e: e.tensor_scalar_add(out=small[:, 14:15], in0=small[:, 13:14], scalar1=LN_EPS), reads=[b_small], writes=[b_small])
            P.op("act", lambda e: e.activation(out=small[:, 15:16], in_=small[:, 14:15], func=AF.Sqrt), reads=[b_small], writes=[b_small])
            P.op("dve", lambda e: e.reciprocal(out=small[:, 16:17], in_=small[:, 15:16]), reads=[b_small], writes=[b_small])
            P.op("dve", lambda e: e.tensor_scalar(out=small[:, 17:18], in0=small[:, 12:13], scalar1=small[:, 16:17], scalar2=-1.0, op0=ALU.mult, op1=ALU.mult),
                 reads=[b_small], writes=[b_small])
            P.op("act", lambda e: e.activation(out=t1[:], in_=r[:], func=AF.Identity, scale=small[:, 16:17], bias=small[:, 17:18]),
                 reads=[b_r, b_small], writes=[b_t1])
            P.op("dve", lambda e: e.tensor_tensor(out=t1[:], in0=t1[:], in1=lng[:], op=ALU.mult), reads=[b_t1, b_ms], writes=[b_t1])
            P.op("pool", lambda e: e.tensor_tensor(out=xo[:, j, :], in0=t1[:], in1=lnb[:], op=ALU.add), reads=[b_t1, b_ms], writes=[b_xo])

        def load_w_bf16(dst, src, kchunks, b_dst, piece=1408):
            N = src.shape[1]
            v = src.rearrange("(c p) n -> p c n", p=128)
            for c in range(kchunks):
                for n0 in range(0, N, piece):
                    n1 = min(N, n0 + piece)
                    P.dma(dst[:, c, n0:n1], v[:, c, n0:n1], writes=[b_dst], q="pool")

        def ffn_phase(l, f, sub, xin, b_xin, xout, b_xout):
            with ExitStack() as st:
                w1 = sbt(st, "w1", [128, 8, 2 * DFF], BF16)
                w2 = sbt(st, "w2", [128, 22, D], BF16)
                b_w1, b_w2, b_ms = Buf(), Buf(), Buf()
                load_w_bf16(w1, w_in[l, f], 8, b_w1)
                load_w_bf16(w2, w_out[l, f], 22, b_w2, piece=1024)
                s1, sh, gp, lng, lnb = [sbt(st, "m%d" % i, [128, D]) for i in range(5)]
                load_bcast(sh, MODP[l, (sub * 3 + 0) * D:(sub * 3 + 1) * D], b_ms)
                load_bcast(s1, MODP[l, (sub * 3 + 1) * D:(sub * 3 + 2) * D], b_ms)
                load_bcast(gp, MODP[l, (sub * 3 + 2) * D:(sub * 3 + 3) * D], b_ms)
                P.dma(lng[:], ln_g[l, sub, :].partition_broadcast(128), writes=[b_ms])
                P.dma(lnb[:], ln_b[l, sub, :].partition_broadcast(128), writes=[b_ms])
                xs = [sbt(st, "xs%d" % i, [128, 2, D]) for i in range(2)]
                b_xs = [[Buf(), Buf()], [Buf(), Buf()]]
                xo = sbt(st, "xo", [128, 2, D])
                b_xo = [Buf(), Buf()]
                xT = sbt(st, "xT", [128, 8, 256], BF16)
                b_xT = Buf()
                aT = sbt(st, "aT", [128, 22, 256], BF16)
                b_aT = Buf()
                sg = [sbt(st, "sg%d" % i, [128, 256]) for i in range(2)]
                b_sg = [Buf(), Buf()]
                r = sbt(st, "r", [128, D])
                small = sbt(st, "small", [128, 32])
                b_r, b_small = Buf(), Buf()

                def xtile(ap, ti):
                    return ap[ti * 256:ti * 256 + 256, :].rearrange("(j p) d -> p j d", p=128)

                def f_load(ti):
                    P.dma(xs[ti % 2][:], xtile(xin, ti), reads=[b_xin], writes=b_xs[ti % 2])

                def f_mod(ti):
                    k2 = ti % 2
                    for j in range(2):
                        P.op("dve", lambda e: e.tensor_tensor(out=xs[k2][:, j, :], in0=xs[k2][:, j, :], in1=s1[:], op=ALU.mult), reads=[b_xs[k2][j], b_ms], writes=[b_xs[k2][j]])
                        P.op("pool", lambda e: e.tensor_tensor(out=xs[k2][:, j, :], in0=xs[k2][:, j, :], in1=sh[:], op=ALU.add), reads=[b_xs[k2][j], b_ms], writes=[b_xs[k2][j]])

                def f_trans(ti):
                    k2 = ti % 2
                    for j in range(2):
                        for c in range(8):
                            P.op("pe", lambda e: e.transpose(out=pq[c // 4][:, (c % 4) * 128:(c % 4 + 1) * 128], in_=xs[k2][:, j, c * 128:(c + 1) * 128], identity=ident),
                                 reads=[b_xs[k2][j], b_c], writes=[b_pq[c // 4]])
                        for hh in range(2):
                            P.op("act", lambda e: e.activation(out=xT[:, hh * 4:(hh + 1) * 4, j * 128:(j + 1) * 128],
                                                               in_=pq[hh][:].rearrange("p (c n) -> p c n", c=4), func=AF.Copy),
                                 reads=[b_pq[hh]], writes=[b_xT])

                def f_ln(j):
                    xo_j = xo[:, j, :]
                    pwj = pw[:, 2 * j:2 * j + 2, :]
                    P.op("dve", lambda e: e.tensor_tensor(out=r[:].rearrange("p (a b) -> p a b", a=2), in0=pwj, in1=gp[:].rearrange("p (a b) -> p a b", a=2), op=ALU.mult),
                         reads=[b_pw[2 * j], b_pw[2 * j + 1], b_ms], writes=[b_r])
                    P.op("dve", lambda e: e.scalar_tensor_tensor(out=r[:], in0=xo_j, scalar=ALPHA, in1=r[:], op0=ALU.mult, op1=ALU.add), reads=[b_xo[j], b_r], writes=[b_r])
                    st6 = small[:, 0:12].rearrange("p (a b) -> p a b", a=2)
                    for k in range(2):
                        P.op("dve", lambda e: e.bn_stats(out=st6[:, k, :], in_=r[:, k * 512:(k + 1) * 512]), reads=[b_r], writes=[b_small])
                    P.op("dve", lambda e: e.bn_aggr(out=small[:, 12:14], in_=st6), reads=[b_small], writes=[b_small])
                    P.op("dve", lambda e: e.tensor_scalar_add(out=small[:, 14:15], in0=small[:, 13:14], scalar1=LN_EPS), reads=[b_small], writes=[b_small])
                    P.op("act", lambda e: e.activation(out=small[:, 15:16], in_=small[:, 14:15], func=AF.Sqrt), reads=[b_small], writes=[b_small])
                    P.op("dve", lambda e: e.reciprocal(out=small[:, 16:17], in_=small[:, 15:16]), reads=[b_small], writes=[b_small])
                    P.op("dve", lambda e: e.tensor_scalar(out=small[:, 17:18], in0=small[:, 12:13], scalar1=small[:, 16:17], scalar2=-1.0, op0=ALU.mult, op1=ALU.mult),
                         reads=[b_small], writes=[b_small])
                    P.op("act", lambda e: e.activation(out=xo_j, in_=r[:], func=AF.Identity, scale=small[:, 16:17], bias=small[:, 17:18]),
                         reads=[b_r, b_small], writes=[b_xo[j]])
                    P.op("dve", lambda e: e.tensor_tensor(out=xo_j, in0=xo_j, in1=lng[:], op=ALU.mult), reads=[b_xo[j], b_ms], writes=[b_xo[j]])
                    P.op("pool", lambda e: e.tensor_tensor(out=xo_j, in0=xo_j, in1=lnb[:], op=ALU.add), reads=[b_xo[j], b_ms], writes=[b_xo[j]])

                f_load(0)
                f_mod(0)
                f_trans(0)
                for ti in range(NT):
                    if ti + 1 < NT:
                        f_load(ti + 1)
                    for m in range(22):
                        k = 2 + (m % 2)
                        for half in range(2):
                            col = half * DFF + m * 128
                            for c in range(8):
                                P.op("pe", lambda e: e.matmul(pq[k][:, half * 256:(half + 1) * 256], lhsT=w1[:, c, col:col + 128], rhs=xT[:, c, :], start=(c == 0), stop=(c == 7)),
                                     reads=[b_xT, b_w1], writes=[b_pq[k]])
                        P.op("act", lambda e: e.activation(out=sg[m % 2][:], in_=pq[k][:, 0:256], func=AF.Silu), reads=[b_pq[k]], writes=[b_sg[m % 2]])
                        P.op("dve", lambda e: e.tensor_tensor(out=aT[:, m, :], in0=sg[m % 2][:], in1=pq[k][:, 256:512], op=ALU.mult),
                             reads=[b_sg[m % 2], b_pq[k]], writes=[b_aT])
                    if ti + 1 < NT:
                        f_mod(ti + 1)
                    for j in range(2):
                        for nh in range(2):
                            for m in range(22):
                                P.op("pe", lambda e: e.matmul(pw[:, 2 * j + nh, :], lhsT=aT[:, m, j * 128:(j + 1) * 128], rhs=w2[:, m, nh * 512:(nh + 1) * 512], start=(m == 0), stop=(m == 21)),
                                     reads=[b_aT, b_w2], writes=[b_pw[2 * j + nh]])
                    P.dma(xo[:], xtile(xin, ti), reads=[b_xin], writes=b_xo)
                    if ti + 1 < NT:
                        f_trans(ti + 1)
                    for j in range(2):
                        f_ln(j)
                    P.dma(xtile(xout, ti), xo[:], reads=b_xo, writes=[b_xout])
            P.barrier()

        def inproj_phase(l, xin, b_xin):
            with ExitStack() as st:
                wm = sbt(st, "wm", [128, 8, NC1], BF16)
                b_wm, b_ms = Buf(), Buf()
                load_w_bf16(wm, wm1[l], 8, b_wm, piece=1024)
                s1, sh = [sbt(st, "m%d" % i, [128, D]) for i in range(2)]
                load_bcast(sh, MODP[l, 3 * D:4 * D], b_ms)
                load_bcast(s1, MODP[l, 4 * D:5 * D], b_ms)
                xs = sbt(st, "xs", [128, 2, D])
                b_xs = Buf()
                u = sbt(st, "u", [128, 2, D])
                b_u = [Buf(), Buf()]
                xT = sbt(st, "xT", [128, 8, 256], BF16)
                b_xT = Buf()
                rc = sbt(st, "rc", [128, 2, 256])
                b_rc = Buf()
                ta = [sbt(st, "ta%d" % i, [128, 256]) for i in range(2)]
                tb = [sbt(st, "tb%d" % i, [128, 256]) for i in range(2)]
                b_ta, b_tb = [Buf(), Buf()], [Buf(), Buf()]
                stg = sbt(st, "stg", [128, NQK, 256], BF16)
                fgs = sbt(st, "fgs", [4, 256])
                vst = sbt(st, "vst", [128, 2, 1152], BF16)
                iws = sbt(st, "iws", [128, 2, 4])
                b_stg, b_fgs, b_vst, b_iws = Buf(), Buf(), Buf(), Buf()
                for ti in range(NT):
                    t0 = ti * 256
                    make_uT(t0, xin, b_xin, xs, b_xs, u, b_u, xT, b_xT, s1, sh, b_ms)
                    P.dma(rc[:], rope_in[:, :, t0:t0 + 256].rearrange("a p t -> p a t"), writes=[b_rc])
                    for ch in range(NQK):
                        if ch < NROPE:
                            k = 2 * (ch % 2)
                            for which, cc in ((0, ch), (1, NQK + ch)):
                                for c in range(8):
                                    P.op("pe", lambda e: e.matmul(pw[:, k + which, 0:256], lhsT=wm[:, c, cc * 128:(cc + 1) * 128], rhs=xT[:, c, :], start=(c == 0), stop=(c == 7)),
                                         reads=[b_xT, b_wm], writes=[b_pw[k + which]])
                            kk = ch % 2
                            P.op("dve", lambda e: e.tensor_tensor(out=ta[kk][:], in0=pw[:, k, 0:256], in1=rc[:, 0, :], op=ALU.mult), reads=[b_pw[k], b_rc], writes=[b_ta[kk]])
                            P.op("dve", lambda e: e.tensor_tensor(out=tb[kk][:], in0=pw[:, k + 1, 0:256], in1=rc[:, 1, :], op=ALU.mult), reads=[b_pw[k + 1], b_rc], writes=[b_tb[kk]])
                            P.op("pool", lambda e: e.tensor_tensor(out=stg[:, ch, :], in0=ta[kk][:], in1=tb[kk][:], op=ALU.add), reads=[b_ta[kk], b_tb[kk]], writes=[b_stg])
                        else:
                            k = ch % 4
                            for c in range(8):
                                P.op("pe", lambda e: e.matmul(pw[:, k, 0:256], lhsT=wm[:, c, ch * 128:(ch + 1) * 128], rhs=xT[:, c, :], start=(c == 0), stop=(c == 7)),
                                     reads=[b_xT, b_wm], writes=[b_pw[k]])
                            P.op("act", lambda e: e.activation(out=stg[:, ch, :], in_=pw[:, k, 0:256], func=AF.Copy), reads=[b_pw[k]], writes=[b_stg])
                    for c in range(8):
                        P.op("pe", lambda e: e.matmul(pw[0:4, 0, 0:256], lhsT=wm[:, c, FGCOL:FGCOL + 4], rhs=xT[:, c, :], start=(c == 0), stop=(c == 7)),
                             reads=[b_xT, b_wm], writes=[b_pw[0]])
                    P.op("act", lambda e: e.activation(out=fgs[:], in_=pw[0:4, 0, 0:256], func=AF.Copy), reads=[b_pw[0]], writes=[b_fgs])
                    P.dma(FGT[:, t0:t0 + 256], fgs[:], reads=[b_fgs], writes=[b_FGT])
                    P.dma(QKT[:, :, t0:t0 + 256].rearrange("c p t -> p c t"), stg[:], reads=[b_stg], writes=[b_QKT])
                    ki = 0
                    for j in range(2):
                        for (n0, n1) in ((0, 384), (384, 896), (896, 1156)):
                            k = 2 + (ki % 2)
                            ki += 1
                            for c in range(8):
                                P.op("pe", lambda e: e.matmul(pq[k][:, 0:n1 - n0], lhsT=xT[:, c, j * 128:(j + 1) * 128], rhs=wm[:, c, TMCOL + n0:TMCOL + n1], start=(c == 0), stop=(c == 7)),
                                     reads=[b_xT, b_wm], writes=[b_pq[k]])
                            if n1 <= 1152:
                                P.op("act", lambda e: e.activation(out=vst[:, j, n0:n1], in_=pq[k][:, 0:n1 - n0], func=AF.Copy), reads=[b_pq[k]], writes=[b_vst])
                            else:
                                P.op("act", lambda e: e.activation(out=vst[:, j, n0:1152], in_=pq[k][:, 0:1152 - n0], func=AF.Copy), reads=[b_pq[k]], writes=[b_vst])
                                P.op("dve", lambda e: e.tensor_copy(out=iws[:, j, :], in_=pq[k][:, 1152 - n0:1156 - n0]), reads=[b_pq[k]], writes=[b_iws])
                    P.dma(VTM[t0:t0 + 256, :].rearrange("(j p) n -> p j n", p=128), vst[:], reads=[b_vst], writes=[b_VTM])
                    P.dma(IWT[t0:t0 + 256, :].rearrange("(j p) n -> p j n", p=128), iws[:], reads=[b_iws], writes=[b_IWT])
            P.barrier()
            with ExitStack() as st:
                wg = sbt(st, "wg", [128, 8, 4096], BF16)
                b_wg, b_ms = Buf(), Buf()
                load_w_bf16(wg, wgate[l], 8, b_wg, piece=1024)
                s1, sh = [sbt(st, "m%d" % i, [128, D]) for i in range(2)]
                load_bcast(sh, MODP[l, 3 * D:4 * D], b_ms)
                load_bcast(s1, MODP[l, 4 * D:5 * D], b_ms)
                bg = sbt(st, "bg", [128, 32])
                P.dma(bg[:], b_gate[l].rearrange("(c p) -> p c", p=128), writes=[b_ms], allow_slow_non_contiguous=True)
                xs = sbt(st, "xs", [128, 2, D])
                b_xs = Buf()
                u = sbt(st, "u", [128, 2, D])
                b_u = [Buf(), Buf()]
                xT = sbt(st, "xT", [128, 8, 256], BF16)
                b_xT = Buf()
                gst = [sbt(st, "gst%d" % i, [128, 32, 256], BF16) for i in range(2)]
                b_gst = [Buf(), Buf()]
                for ti in range(NT):
                    t0 = ti * 256
                    g2 = ti % 2
                    make_uT(t0, xin, b_xin, xs, b_xs, u, b_u, xT, b_xT, s1, sh, b_ms)
                    for ch in range(32):
                        k = ch % 4
                        for c in range(8):
                            P.op("pe", lambda e: e.matmul(pw[:, k, 0:256], lhsT=wg[:, c, ch * 128:(ch + 1) * 128], rhs=xT[:, c, :], start=(c == 0), stop=(c == 7)),
                                 reads=[b_xT, b_wg], writes=[b_pw[k]])
                        P.op("act", lambda e: e.activation(out=gst[g2][:, ch, :], in_=pw[:, k, 0:256], func=AF.Sigmoid, bias=bg[:, ch:ch + 1]), reads=[b_pw[k], b_ms], writes=[b_gst[g2]])
                    P.dma(GT[:, :, t0:t0 + 256].rearrange("c p t -> p c t"), gst[g2][:], reads=[b_gst[g2]], writes=[b_GT])
            P.barrier()

        def attention_phase(l):
            with ExitStack() as sm:
                cumT = sbt(sm, "cumT", [128, NB * 4])
                Gb = sbt(sm, "Gb", [128, NB * 4])
                ones64 = sbt(sm, "ones64", [128, 64])
                b_cum = Buf()
                b_o64 = Buf()
                P.op("pool", lambda e: e.memset(ones64[:], 1.0), writes=[b_o64])
                with ExitStack() as st:
                    fg = sbt(st, "fg", [4, T])
                    sp = sbt(st, "sp", [4, T])
                    on4 = sbt(st, "on4", [4, T])
                    nb4 = sbt(st, "nb4", [4, 1])
                    b_fg, b_sp, b_on, b_nb = Buf(), Buf(), Buf(), Buf()
                    P.dma(fg[:], FGT, reads=[b_FGT], writes=[b_fg])
                    P.dma(nb4[:], b_forget[l].rearrange("(p a) -> p a", a=1), writes=[b_nb])
                    P.op("dve", lambda e: e.tensor_scalar_mul(out=nb4[:], in0=nb4[:], scalar1=-1.0), reads=[b_nb], writes=[b_nb])
                    P.op("pool", lambda e: e.memset(on4[:], 1.0), writes=[b_on])
                    P.op("act", lambda e: e.activation(out=sp[:], in_=fg[:], func=AF.Exp, scale=-1.0, bias=nb4[:, 0:1]), reads=[b_fg, b_nb], writes=[b_sp])
                    P.op("act", lambda e: e.activation(out=sp[:], in_=sp[:], func=AF.Ln, bias=1.0), reads=[b_sp], writes=[b_sp])
                    P.op("dve", lambda e: e.tensor_tensor_scan(out=fg[:], data0=on4[:], data1=sp[:], initial=0.0, op0=ALU.mult, op1=ALU.subtract),
                         reads=[b_on, b_sp], writes=[b_fg])
                    for blk in range(NB):
                        P.op("pe", lambda e: e.transpose(out=pq[0][:, blk * 4:(blk + 1) * 4], in_=fg[0:4, blk * 128:(blk + 1) * 128], identity=ident[0:4, 0:4]),
                             reads=[b_fg, b_c], writes=[b_pq[0]])
                    P.op("act", lambda e: e.activation(out=cumT[:], in_=pq[0][:, 0:NB * 4], func=AF.Copy), reads=[b_pq[0]], writes=[b_cum])
                    P.op("pe", lambda e: e.matmul(pq[1][:, 0:NB * 4], lhsT=E0, rhs=cumT[:], start=True, stop=True), reads=[b_cum, b_c], writes=[b_pq[1]])
                    P.op("act", lambda e: e.activation(out=Gb[:], in_=pq[1][:, 0:NB * 4], func=AF.Copy), reads=[b_pq[1]], writes=[b_cum])
                P.barrier()

                def load_kqv(st, qch, kch, vcol, nh, noq=False):
                    kt = sbt(st, "kt", [128, nh // 2, T], BF16)
                    qt = sbt(st, "qt", [128, nh // 2, 128 if noq else T], BF16)
                    va = sbt(st, "va", [128, NB, nh, 65], BF16)
                    b_k, b_q, b_v = Buf(), Buf(), Buf()
                    P.dma(kt[:], QKT[kch:kch + nh // 2].rearrange("c p t -> p c t"), reads=[b_QKT], writes=[b_k])
                    if not noq:
                        P.dma(qt[:], QKT[qch:qch + nh // 2].rearrange("c p t -> p c t"), reads=[b_QKT], writes=[b_q])
                    P.op("pool", lambda e: e.memset(va[:, :, :, 64:65], 1.0), writes=[b_v])
                    for h in range(nh):
                        P.dma(va[:, :, h, 0:64], VTM[:, vcol + h * 64:vcol + (h + 1) * 64].rearrange("(k p) e -> p k e", p=128), reads=[b_VTM], writes=[b_v])
                    return kt, qt, va, b_k, b_q, b_v

                Sk = [pqq[:, 0:2, 0:256], pqq[:, 2:4, 0:256]]
                b_Sk = [[b_pq[0], b_pq[1]], [b_pq[2], b_pq[3]]]
                po4 = pw[:, 0, :].rearrange("p (h t) -> p h t", h=4)
                b_po = b_pw[0]
                pbk = pw[:, 1, :]
                b_pb = b_pw[1]

                def sreg(k, h):
                    return pqq[:, 2 * k + h % 2, (h // 2) * 128:(h // 2 + 1) * 128]

                def qk4(k, kt, qt, b_k, b_q, i, j):
                    for h in range(4):
                        P.op("pe", lambda e: e.matmul(sreg(k, h), lhsT=kt[hr(h), h // 2, j * 128:(j + 1) * 128], rhs=qt[hr(h), h // 2, i * 128:(i + 1) * 128], start=True, stop=True),
                             reads=[b_k, b_q], writes=[b_pq[2 * k + h % 2]])

                def av4(pT, b_pT, va, b_v, j, first, last, M):
                    for h in range(4):
                        P.op("pe", lambda e: e.matmul(po4[0:M, h, :], lhsT=va[:, j, h, 0:M], rhs=pT[:, sl(h), :], start=(first and h == 0), stop=last, skip_group_check=True),
                             reads=[b_pT, b_v], writes=[b_po])

                def normalize_store(rrow, b_rr, rb, b_rb, yst, b_yst, ych, i):
                    P.op("dve", lambda e: e.reciprocal(out=rrow[64:65, :, :], in_=po4[64:65, :, :]), reads=[b_po], writes=[b_rr])
                    P.op("pe", lambda e: e.matmul(pbk[0:64, :], lhsT=ones64[64:65, :], rhs=rrow[64:65, :, :].rearrange("p a b -> p (a b)"), start=True, stop=True),
                         reads=[b_rr, b_o64], writes=[b_pb])
                    P.op("act", lambda e: e.activation(out=rb[0:64, :, :].rearrange("p a b -> p (a b)"), in_=pbk[0:64, :], func=AF.Copy), reads=[b_pb], writes=[b_rb])
                    P.op("dve", lambda e: e.tensor_tensor(out=yst[0:64, :, :], in0=po4[0:64, :, :], in1=rb[0:64, :, :], op=ALU.mult), reads=[b_po, b_rb], writes=[b_yst])
                    P.dma(YT[ych:ych + 2, :, i * 128:(i + 1) * 128].rearrange("c (h e) t -> e (c h) t", e=64), yst[0:64, :, :], reads=[b_yst], writes=[b_YT])

                def flat(t):
                    return t[:].rearrange("p a b -> p (a b)")

                with ExitStack() as st:
                  if ATT_STOP >= 1 and ATT_ONLY in (-1, 1):
                      kt, qt, va, b_k, b_q, b_v = load_kqv(st, 17, 19, 896, 4)
                      bm = [sbt(st, "bm%d" % i, [128, 4, NB]) for i in range(2)]
                      b_bm = [Buf(), Buf()]
                      tS = [sbt(st, "tS%d" % i, [128, 4, 128]) for i in range(2)]
                      b_tS = [Buf(), Buf()]
                      pT = [sbt(st, "pT%d" % i, [128, 4, 128], BF16) for i in range(2)]
                      b_pT = [Buf(), Buf()]
                      rrow = sbt(st, "rrow", [128, 4, 128])
                      rb = sbt(st, "rb", [64, 4, 128])
                      yst = sbt(st, "yst", [64, 4, 128], BF16)
                      b_rr, b_rb, b_yst = Buf(), Buf(), Buf()
                      cum3 = cumT[:].rearrange("p (k h) -> p k h", h=4)
                      pc = 0
                      for i in range(NB):
                          bi = i % 2
                          for h in range(4):
                              P.op("dve", lambda e: e.tensor_scalar(out=bm[bi][:, h, 0:i + 1], in0=cum3[:, 0:i + 1, h], scalar1=-1.0, scalar2=Gb[:, i * 4 + h:i * 4 + h + 1], op0=ALU.mult, op1=ALU.add),
                                   reads=[b_cum], writes=[b_bm[bi]])
                          qk4(0, kt, qt, b_k, b_q, i, 0)
                          for j in range(i + 1):
                              k = j % 2
                              if j < i:
                                  qk4((j + 1) % 2, kt, qt, b_k, b_q, i, j + 1)
                              kp = pc % 2
                              pc += 1
                              for h in (1, 3):
                                  P.op("dve", lambda e: e.tensor_scalar(out=tS[kp][:, sl(h), :], in0=sreg(k, h), scalar1=0.125, scalar2=bm[bi][:, h, j:j + 1], op0=ALU.mult, op1=ALU.add),
                                       reads=[b_pq[2 * k + h % 2], b_bm[bi]], writes=[b_tS[kp]])
                              for h in (0, 2):
                                  P.op("act", lambda e: e.activation(out=pT[kp][:, sl(h), :], in_=sreg(k, h), func=AF.Exp, scale=0.125, bias=bm[bi][:, h, j:j + 1]),
                                       reads=[b_pq[2 * k + h % 2], b_bm[bi]], writes=[b_pT[kp]])
                              P.op("act", lambda e: e.activation(out=pT[kp][:, 2:4, :], in_=tS[kp][:, 2:4, :], func=AF.Exp), reads=[b_tS[kp]], writes=[b_pT[kp]])
                              if j == i:
                                  P.op("pool", lambda e: e.tensor_tensor(out=pT[kp][:], in0=pT[kp][:], in1=LE4, op=ALU.mult), reads=[b_pT[kp], b_c], writes=[b_pT[kp]])
                              av4(pT[kp], b_pT[kp], va, b_v, j, j == 0, j == i, 65)
                          normalize_store(rrow, b_rr, rb, b_rb, yst, b_yst, 5, i)
                P.barrier()

                with ExitStack() as st:
                  if ATT_STOP >= 2 and ATT_ONLY in (-1, 2):
                      kt, qt, va, b_k, b_q, b_v = load_kqv(st, 13, 15, 640, 4)
                      ef = sbt(st, "ef", [128, 512])
                      spf = [sbt(st, "spf%d" % i, [128, 512]) for i in range(2)]
                      tt = sbt(st, "tt", [128, 512])
                      arg = sbt(st, "arg", [128, 512])
                      carry = sbt(st, "carry", [128, 512])
                      b_ef, b_tt, b_arg, b_carry = Buf(), Buf(), Buf(), Buf()
                      b_spf = [Buf(), Buf()]
                      shi = [sbt(st, "shi%d" % i, [128, 512], BF16) for i in range(2)]
                      slo = [sbt(st, "slo%d" % i, [128, 512], BF16) for i in range(2)]
                      b_shi, b_slo = [Buf(), Buf()], [Buf(), Buf()]
                      pT = [sbt(st, "pT%d" % i, [128, 4, 128], BF16) for i in range(2)]
                      b_pT = [Buf(), Buf()]
                      yst = sbt(st, "yst", [64, 4, 128], BF16)
                      b_yst = Buf()
                      LT4f = LT4.rearrange("p a b -> p (a b)")
                      pA, pB = pw[:, 2, :], pw[:, 3, :]
                      b_pA, b_pB = b_pw[2], b_pw[3]

                      def c_front(i, j, k):
                          qk4(k, kt, qt, b_k, b_q, i, j)
                          P.op("act", lambda e: e.activation(out=v3(ef[:]), in_=Sk[k], func=AF.Exp, scale=0.125), reads=b_Sk[k], writes=[b_ef])
                          P.op("act", lambda e: e.activation(out=spf[k][:], in_=ef[:], func=AF.Ln, bias=1.0), reads=[b_ef], writes=[b_spf[k]])
                          if j == i:
                              P.op("dve", lambda e: e.tensor_tensor(out=spf[k][:], in0=spf[k][:], in1=LT4f, op=ALU.mult), reads=[b_spf[k], b_c], writes=[b_spf[k]])
                          P.op("dve", lambda e: e.tensor_copy(out=shi[k][:], in_=spf[k][:]), reads=[b_spf[k]], writes=[b_shi[k]])
                          P.op("pool", lambda e: e.tensor_tensor(out=slo[k][:], in0=spf[k][:], in1=shi[k][:], op=ALU.subtract), reads=[b_spf[k], b_shi[k]], writes=[b_slo[k]])

                      def c_back(i, j, k):
                          P.op("pe", lambda e: e.matmul(pA, lhsT=Ugeb, rhs=shi[k][:], start=True, stop=False), reads=[b_shi[k], b_c], writes=[b_pA])
                          P.op("pe", lambda e: e.matmul(pA, lhsT=Ugeb, rhs=slo[k][:], start=False, stop=True), reads=[b_slo[k], b_c], writes=[b_pA])
                          if j > 0:
                              P.op("pe", lambda e: e.matmul(pB, lhsT=onesb, rhs=shi[k][:], start=True, stop=False), reads=[b_shi[k], b_c], writes=[b_pB])
                              P.op("pe", lambda e: e.matmul(pB, lhsT=onesb, rhs=slo[k][:], start=False, stop=True), reads=[b_slo[k], b_c], writes=[b_pB])
                          P.op("dve", lambda e: e.tensor_tensor(out=tt[:], in0=pA, in1=carry[:], op=ALU.add), reads=[b_pA, b_carry], writes=[b_tt])
                          P.op("dve", lambda e: e.scalar_tensor_tensor(out=v3(arg[:]), in0=Sk[k], scalar=0.125, in1=v3(tt[:]), op0=ALU.mult, op1=ALU.subtract),
                               reads=b_Sk[k] + [b_tt], writes=[b_arg])
                          if j > 0:
                              P.op("dve", lambda e: e.tensor_tensor(out=carry[:], in0=pB, in1=carry[:], op=ALU.add), reads=[b_pB, b_carry], writes=[b_carry])
                          P.op("act", lambda e: e.activation(out=flat(pT[k]), in_=arg[:], func=AF.Exp), reads=[b_arg], writes=[b_pT[k]])
                          if j == i:
                              P.op("pool", lambda e: e.tensor_tensor(out=pT[k][:], in0=pT[k][:], in1=LT4, op=ALU.mult), reads=[b_pT[k], b_c], writes=[b_pT[k]])

                      def c_av(i, j, k):
                          av4(pT[k], b_pT[k], va, b_v, j, j == i, j == 0, 64)

                      for i in range(NB):
                          P.op("pool", lambda e: e.memset(carry[:], 0.0), writes=[b_carry])
                          pairs = list(range(i, -1, -1))
                          m_ = len(pairs)
                          c_front(i, pairs[0], 0)
                          if m_ > 1:
                              c_front(i, pairs[1], 1)
                          c_back(i, pairs[0], 0)
                          for n, j in enumerate(pairs):
                              if n + 2 < m_:
                                  c_front(i, pairs[n + 2], (n + 2) % 2)
                              if n + 1 < m_:
                                  c_back(i, pairs[n + 1], (n + 1) % 2)
                              c_av(i, j, n % 2)
                          P.op("act", lambda e: e.activation(out=yst[0:64, :, :], in_=po4[0:64, :, :], func=AF.Copy), reads=[b_po], writes=[b_yst])
                          P.dma(YT[3:5, :, i * 128:(i + 1) * 128].rearrange("c (h e) t -> e (c h) t", e=64), yst[0:64, :, :], reads=[b_yst], writes=[b_YT])
                P.barrier()

                with ExitStack() as st:
                  if ATT_STOP >= 3 and ATT_ONLY in (-1, 3):
                      kt, qt_full, va, b_k, b_q_full, b_v = load_kqv(st, 6, 8, 384, 4, noq=True)
                      qtb = [sbt(st, "qtb%d" % i, [128, 2, 128], BF16) for i in range(2)]
                      qib = [sbt(st, "qib%d" % i, [128, 2, 128], BF16) for i in range(2)]
                      b_qtb = [Buf(), Buf()]
                      kit = sbt(st, "kit", [128, T], BF16)
                      wi = sbt(st, "wi", [128, NB, 4])
                      b_qi = Buf()
                      P.dma(kit[:], QKT[12], reads=[b_QKT], writes=[b_qi])
                      P.dma(wi[:], IWT.rearrange("(k p) h -> p k h", p=128), reads=[b_IWT], writes=[b_qi])
                      sc = sbt(st, "sc", [128, T])
                      mk = [sbt(st, "mk%d" % i, [128, T], BF16) for i in range(2)]
                      rl = [sbt(st, "rl%d" % i, [128, 512]) for i in range(2)]
                      b_sc = Buf()
                      b_mk = [Buf(), Buf()]
                      b_jD = [Buf(), Buf()]
                      b_jA = [Buf(), Buf()]
                      b_rl = [Buf(), Buf()]
                      sm_ = sbt(st, "smallb", [128, 12])
                      b_sm = Buf()
                      b_sA = Buf()
                      b_nm = Buf()
                      lo, hi, w0, nmid, cnt, ge, mid, tcb = [sm_[:, a:a + 1] for a in range(8)]
                      sA = sm_[:, 8:9]
                      pT = [sbt(st, "pT%d" % i, [128, 4, 128], BF16) for i in range(2)]
                      b_pT = [Buf(), Buf()]
                      rrow = sbt(st, "rrow", [128, 4, 128])
                      rb = sbt(st, "rb", [64, 4, 128])
                      yst = sbt(st, "yst", [64, 4, 128], BF16)
                      b_rr, b_rb, b_yst = Buf(), Buf(), Buf()
                      pm = [pw[:, 2, :].bitcast(BF16), pw[:, 3, :].bitcast(BF16)]
                      b_pm = [b_pw[2], b_pw[3]]

                      def b_front(qb_, m, j, k):
                          for h in range(4):
                              P.op("pe", lambda e: e.transpose(out=pm[k][:, h * 128:(h + 1) * 128], in_=mk[m][:, j * 128:(j + 1) * 128], identity=identb),
                                   reads=[b_mk[m], b_c], writes=[b_pm[k]])
                          qk4(k, kt, qtb[qb_], b_k, b_qtb[qb_], 0, j)
                          P.op("act", lambda e: e.activation(out=v3(flat(pT[k])), in_=Sk[k], func=AF.Exp, scale=0.125), reads=b_Sk[k], writes=[b_pT[k]])

                      def b_back(i, j, k):
                          P.op("dve", lambda e: e.tensor_tensor(out=flat(pT[k]), in0=flat(pT[k]), in1=pm[k][:, 0:512], op=ALU.mult),
                               reads=[b_pT[k], b_pm[k]], writes=[b_pT[k]])
                          av4(pT[k], b_pT[k], va, b_v, j, j == 0, j == i, 65)

                      for i in range(NB):
                          n = 128 * (i + 1)
                          nchunk = (n + 511) // 512
                          qb_ = i % 2
                          m = i % 2
                          P.dma(qtb[qb_][:], QKT[6:8, :, i * 128:(i + 1) * 128].rearrange("c p t -> p c t"), reads=[b_QKT], writes=[b_qtb[qb_]])
                          P.dma(qib[qb_][:], QKT[10:12, :, i * 128:(i + 1) * 128].rearrange("c p t -> p c t"), reads=[b_QKT], writes=[b_qtb[qb_]])
                          for c in range(nchunk):
                              w = min(512, n - 512 * c)
                              for h in range(4):
                                  k = h % 2
                                  P.op("pe", lambda e: e.matmul(pq[k][:, 0:w], lhsT=qib[qb_][hr(h), h // 2, :], rhs=kit[hr(h), c * 512:c * 512 + w], start=True, stop=True),
                                       reads=[b_qi, b_qtb[qb_]], writes=[b_pq[k]])
                                  P.op("act", lambda e: e.activation(out=rl[k][:, 0:w], in_=pq[k][:, 0:w], func=AF.Relu), reads=[b_pq[k]], writes=[b_rl[k]])
                                  if h == 0:
                                      P.op("dve", lambda e: e.tensor_scalar(out=sc[:, c * 512:c * 512 + w], in0=rl[k][:, 0:w], scalar1=wi[:, i, 0:1], scalar2=None, op0=ALU.mult),
                                           reads=[b_rl[k], b_qi], writes=[b_sc])
                                  else:
                                      P.op("dve", lambda e: e.scalar_tensor_tensor(out=sc[:, c * 512:c * 512 + w], in0=rl[k][:, 0:w], scalar=wi[:, i, h:h + 1], in1=sc[:, c * 512:c * 512 + w], op0=ALU.mult, op1=ALU.add),
                                           reads=[b_rl[k], b_qi, b_sc], writes=[b_sc])
                          P.op("dve", lambda e: e.tensor_reduce(out=lo, in_=sc[:, 0:n], axis=AX.X, op=ALU.min), reads=[b_sc], writes=[b_sm])
                          P.op("dve", lambda e: e.tensor_reduce(out=hi, in_=sc[:, 0:n], axis=AX.X, op=ALU.max), reads=[b_sc], writes=[b_sm])
                          P.op("dve", lambda e: e.tensor_tensor(out=sc[:, n - 128:n], in0=sc[:, n - 128:n], in1=negmask, op=ALU.add), reads=[b_sc, b_c], writes=[b_sc])
                          P.op("dve", lambda e: e.tensor_tensor(out=w0, in0=hi, in1=lo, op=ALU.subtract), reads=[b_sm], writes=[b_sm])
                          P.op("dve", lambda e: e.memset(mk[m][:, 0:2], 0.0), writes=[b_mk[m], b_jD[m], b_jA[m]])
                          nd = max(0, ((int(0.457 * n) - 924) // 64) * 64)
                          na = n - nd
                          for it in range(1, NITER + 1):
                              f = 2.0 ** (-it)
                              P.op("dve", lambda e: e.scalar_tensor_tensor(out=mid, in0=w0, scalar=f, in1=lo, op0=ALU.mult, op1=ALU.add), reads=[b_sm], writes=[b_nm])
                              P.op("act", lambda e: e.activation(out=mk[m][:, nd:n], in_=sc[:, nd:n], func=AF.Sign, bias=mid, scale=-1.0, accum_out=sA),
                                   reads=[b_sc, b_nm], writes=[b_jA[m], b_sA])
                              if nd > 0:
                                  P.op("dve", lambda e: e.tensor_scalar(out=mk[m][:, 0:nd], in0=sc[:, 0:nd], scalar1=mid, scalar2=None, op0=ALU.is_ge, op1=ALU.add, accum_out=cnt),
                                       reads=[b_sc, b_nm], writes=[b_jD[m], b_sm])
                                  P.op("dve", lambda e: e.scalar_tensor_tensor(out=cnt, in0=sA, scalar=-0.5, in1=cnt, op0=ALU.mult, op1=ALU.add), reads=[b_sA, b_sm], writes=[b_sm])
                                  P.op("dve", lambda e: e.tensor_scalar(out=ge, in0=cnt, scalar1=TOPK - 0.5 - na / 2.0, scalar2=f, op0=ALU.is_ge, op1=ALU.mult), reads=[b_sm], writes=[b_sm])
                              else:
                                  P.op("dve", lambda e: e.tensor_scalar(out=ge, in0=sA, scalar1=float(n - 2 * TOPK + 1), scalar2=f, op0=ALU.is_le, op1=ALU.mult), reads=[b_sA], writes=[b_sm])
                              P.op("dve", lambda e: e.scalar_tensor_tensor(out=lo, in0=ge, scalar=w0, in1=lo, op0=ALU.mult, op1=ALU.add), reads=[b_sm], writes=[b_sm])
                          P.op("dve", lambda e: e.tensor_scalar(out=mk[m][:, 0:n], in0=sc[:, 0:n], scalar1=lo, scalar2=None, op0=ALU.is_ge), reads=[b_sc, b_sm], writes=[b_mk[m], b_jD[m], b_jA[m]])
                          b_front(qb_, m, 0, 0)
                          for j in range(i + 1):
                              if j < i:
                                  b_front(qb_, m, j + 1, (j + 1) % 2)
                              b_back(i, j, j % 2)
                          normalize_store(rrow, b_rr, rb, b_rb, yst, b_yst, 1, i)
                P.barrier()


                with ExitStack() as st:
                  if ATT_STOP >= 4 and ATT_ONLY in (-1, 4):
                      accA = sbt(st, "accA", [65, 2, T])
                      b_acc = Buf()
                      kt = sbt(st, "kt", [128, T], BF16)
                      qt = sbt(st, "qt", [128, T], BF16)
                      va = sbt(st, "va", [128, NB, 2, 65], BF16)
                      b_k, b_q, b_v = Buf(), Buf(), Buf()
                      pT = [sbt(st, "pT%d" % i, [128, 4, 128], BF16) for i in range(2)]
                      b_pT = [Buf(), Buf()]
                      P.op("pool", lambda e: e.memset(va[:, :, :, 64:65], 1.0), writes=[b_v])
                      pc = 0
                      for g, d in enumerate((1, 4, 16)):
                          nbs = NB // d
                          P.dma(kt[:], QKT[3 + g], reads=[b_QKT], writes=[b_k])
                          P.dma(qt[:], QKT[g], reads=[b_QKT], writes=[b_q])
                          vsrc = VTM[:, g * 128:(g + 1) * 128].rearrange("(k p r) (h e) -> r p k h e", p=128, r=d, e=64)
                          for r_ in range(d):
                              for hh in range(2):
                                  P.dma(va[:, r_ * nbs:(r_ + 1) * nbs, hh, 0:64], vsrc[r_, :, :, hh, :], reads=[b_VTM], writes=[b_v])
                          for r_ in range(d):
                              for kb in range(nbs):
                                  k = pc % 2
                                  pc += 1

                                  def tok(kk):
                                      base = r_ + d * 128 * kk
                                      return slice(base, base + d * 127 + 1, d) if d > 1 else slice(base, base + 128)
                                  for hh in range(2):
                                      for wch in range(2):
                                          if kb == 0 and wch == 0:
                                              continue
                                          P.op("pe", lambda e: e.matmul(pqq[:, hh, wch * 128:(wch + 1) * 128], lhsT=kt[hr(hh), tok(kb - 1 + wch)], rhs=qt[hr(hh), tok(kb)], start=True, stop=True),
                                               reads=[b_k, b_q], writes=[b_pq[hh]])
                                  if kb == 0:
                                      for hh in range(2):
                                          P.op("act", lambda e: e.activation(out=pT[k][:, hh * 2 + 1, :], in_=pqq[:, hh, 128:256], func=AF.Exp, scale=0.125), reads=[b_pq[hh]], writes=[b_pT[k]])
                                          P.op("pool", lambda e: e.tensor_tensor(out=pT[k][:, hh * 2 + 1, :], in0=pT[k][:, hh * 2 + 1, :], in1=MA4[:, hh * 2 + 1, :], op=ALU.mult), reads=[b_pT[k], b_c], writes=[b_pT[k]])
                                  else:
                                      P.op("act", lambda e: e.activation(out=v3(pT[k][:].rearrange("p a b -> p (a b)")), in_=SS, func=AF.Exp, scale=0.125), reads=[b_pq[0], b_pq[1]], writes=[b_pT[k]])
                                      P.op("pool", lambda e: e.tensor_tensor(out=pT[k][:], in0=pT[k][:], in1=MA4, op=ALU.mult), reads=[b_pT[k], b_c], writes=[b_pT[k]])
                                  for hh in range(2):
                                      if kb > 0:
                                          P.op("pe", lambda e: e.matmul(pw[0:65, hh, 0:128], lhsT=va[:, r_ * nbs + kb - 1, hh, :], rhs=pT[k][:, hh * 2, :], start=True, stop=False),
                                               reads=[b_pT[k], b_v], writes=[b_pw[hh]])
                                      P.op("pe", lambda e: e.matmul(pw[0:65, hh, 0:128], lhsT=va[:, r_ * nbs + kb, hh, :], rhs=pT[k][:, hh * 2 + 1, :], start=(kb == 0), stop=True),
                                           reads=[b_pT[k], b_v], writes=[b_pw[hh]])
                                  dst = accA[0:65, :, tok(kb)]
                                  if g == 0:
                                      P.op("dve", lambda e: e.tensor_copy(out=dst, in_=pw[0:65, 0:2, 0:128]), reads=[b_pw[0], b_pw[1]], writes=[b_acc])
                                  else:
                                      P.op("dve", lambda e: e.tensor_tensor(out=dst, in0=pw[0:65, 0:2, 0:128], in1=dst, op=ALU.add), reads=[b_pw[0], b_pw[1], b_acc], writes=[b_acc])
                      rrow = sbt(st, "rrowA", [128, 2, 256])
                      ysa = sbt(st, "ysa", [64, 2, 256], BF16)
                      b_rr, b_ys = Buf(), Buf()
                      for ti in range(NT):
                          t0 = ti * 256
                          P.op("dve", lambda e: e.reciprocal(out=rrow[64:65, :, :], in_=accA[64:65, :, t0:t0 + 256]), reads=[b_acc], writes=[b_rr])
                          P.op("pe", lambda e: e.matmul(pq[2][0:64, :], lhsT=ones64[64:65, :], rhs=rrow[64:65, :, :].rearrange("p a b -> p (a b)"), start=True, stop=True),
                               reads=[b_rr, b_o64], writes=[b_pq[2]])
                          P.op("dve", lambda e: e.tensor_tensor(out=ysa[:], in0=accA[0:64, :, t0:t0 + 256], in1=pq[2][0:64, :].rearrange("p (a b) -> p a b", a=2), op=ALU.mult),
                               reads=[b_acc, b_pq[2]], writes=[b_ys])
                          P.dma(YT[0, :, t0:t0 + 256].rearrange("(h e) t -> e h t", e=64), ysa[:], reads=[b_ys], writes=[b_YT])
            P.barrier()

        def post_phase(l, xin, b_xin, xout, b_xout):
            with ExitStack() as st:
                wbr = sbt(st, "wbr", [128, 7, D], BF16)
                wo = sbt(st, "wo", [128, 8, D], BF16)
                b_wbr, b_wo, b_ms = Buf(), Buf(), Buf()
                load_w_bf16(wbr, w_branch[l], 7, b_wbr, piece=1024)
                load_w_bf16(wo, w_o[l], 8, b_wo, piece=1024)
                gp, lng, lnb = [sbt(st, "m%d" % i, [128, D]) for i in range(3)]
                load_bcast(gp, MODP[l, 5 * D:6 * D], b_ms)
                P.dma(lng[:], ln_g[l, 1, :].partition_broadcast(128), writes=[b_ms])
                P.dma(lnb[:], ln_b[l, 1, :].partition_broadcast(128), writes=[b_ms])
                xs = [sbt(st, "xs%d" % i, [128, 2, D]) for i in range(2)]
                b_xs = [Buf(), Buf()]
                yt = [sbt(st, "yt%d" % i, [128, 7, 256], BF16) for i in range(2)]
                gt = [sbt(st, "gt%d" % i, [128, 32, 256], BF16) for i in range(2)]
                b_yt, b_gt = [Buf(), Buf()], [Buf(), Buf()]
                mg = sbt(st, "mg", [128, 8, 256])
                mgb = sbt(st, "mgb", [128, 8, 256], BF16)
                tmp = [sbt(st, "tmp%d" % i, [128, 256]) for i in range(2)]
                b_mg, b_mgb = Buf(), Buf()
                b_tmp = [Buf(), Buf()]
                xo = sbt(st, "xo", [128, 2, D])
                b_xo = [Buf(), Buf()]
                t1 = sbt(st, "t1", [128, D])
                r = sbt(st, "r", [128, D])
                small = sbt(st, "small", [128, 32])
                b_t1, b_r, b_small = Buf(), Buf(), Buf()
                kch = ((0, 1), (1, 3), (3, 5), (5, 7))
                pc = 0
                for ti in range(NT):
                    t0 = ti * 256
                    k2 = ti % 2
                    P.dma(xs[k2][:], xin[t0:t0 + 256, :].rearrange("(j p) d -> p j d", p=128), reads=[b_xin], writes=[b_xs[k2]])
                    P.dma(yt[k2][:], YT[:, :, t0:t0 + 256].rearrange("c p t -> p c t"), reads=[b_YT], writes=[b_yt[k2]])
                    P.dma(gt[k2][:], GT[:, :, t0:t0 + 256].rearrange("c p t -> p c t"), reads=[b_GT], writes=[b_gt[k2]])
                    for fo in range(8):
                        for bi in range(4):
                            k = pc % 4
                            pc += 1
                            a, b = kch[bi]
                            for kc in range(a, b):
                                P.op("pe", lambda e: e.matmul(pq[k][:, 0:256], lhsT=wbr[:, kc, fo * 128:(fo + 1) * 128], rhs=yt[k2][:, kc, :], start=(kc == a), stop=(kc == b - 1)),
                                     reads=[b_wbr, b_yt[k2]], writes=[b_pq[k]])
                            if bi == 0:
                                P.op("dve", lambda e: e.tensor_tensor(out=mg[:, fo, :], in0=pq[k][:, 0:256], in1=gt[k2][:, bi * 8 + fo, :], op=ALU.mult), reads=[b_pq[k], b_gt[k2]], writes=[b_mg])
                            else:
                                kk = pc % 2
                                P.op("dve", lambda e: e.tensor_tensor(out=tmp[kk][:], in0=pq[k][:, 0:256], in1=gt[k2][:, bi * 8 + fo, :], op=ALU.mult), reads=[b_pq[k], b_gt[k2]], writes=[b_tmp[kk]])
                                if bi < 3:
                                    P.op("pool", lambda e: e.tensor_tensor(out=mg[:, fo, :], in0=mg[:, fo, :], in1=tmp[kk][:], op=ALU.add), reads=[b_mg, b_tmp[kk]], writes=[b_mg])
                                else:
                                    P.op("pool", lambda e: e.tensor_tensor(out=mgb[:, fo, :], in0=mg[:, fo, :], in1=tmp[kk][:], op=ALU.add), reads=[b_mg, b_tmp[kk]], writes=[b_mgb])
                    for j in range(2):
                        for nh in range(2):
                            for kc in range(8):
                                P.op("pe", lambda e: e.matmul(pw[:, 2 * j + nh, :], lhsT=mgb[:, kc, j * 128:(j + 1) * 128], rhs=wo[:, kc, nh * 512:(nh + 1) * 512], start=(kc == 0), stop=(kc == 7)),
                                     reads=[b_mgb, b_wo], writes=[b_pw[2 * j + nh]])
                    for j in range(2):
                        deepnorm_ln(j, xs[k2], b_xs[k2], gp, lng, lnb, b_ms, t1, r, b_t1, b_r, small, b_small, xo, b_xo[j])
                    P.dma(xout[t0:t0 + 256, :].rearrange("(j p) d -> p j d", p=128), xo[:], reads=[b_xo[0], b_xo[1]], writes=[b_xout])
            P.barrier()

        b_xin0 = Buf()
        b_y = Buf()
        dbg_stage = dbg if isinstance(dbg, int) and not isinstance(dbg, bool) else 99
        stages = 0
        cur, b_cur = x_in, b_xin0
        for l in range(2):
            last = (l == 1)
            if dbg_stage >= 1:
                ffn_phase(l, 0, 0, cur, b_cur, S1, b_S1)
            if dbg_stage >= 2:
                inproj_phase(l, S1, b_S1)
            if dbg_stage >= 3:
                attention_phase(l)
            if dbg_stage >= 4:
                post_phase(l, S1, b_S1, S2, b_S2)
            if dbg_stage >= 5:
                ffn_phase(l, 1, 2, S2, b_S2, y_out if last else S1, b_y if last else b_S1)
            cur, b_cur = S1, b_S1
            if dbg_stage < 99:
                break
        P.finish()
        build.stats = (P.ninst, P.nwait)
    return nc


def _consts(T):
    p = np.arange(128)
    cf = np.zeros((128, 5, 128), np.float32)
    cf[:, 0, :] = np.eye(128)
    cf[:, 1, :] = (p[:, None] >= p[None, :])
    cf[:, 2, :] = 1.0
    cf[0, 3, :] = 1.0
    cf[:, 4, :] = np.where(p[None, :] > p[:, None], -1e30, 0.0)
    cb = np.zeros((128, 15, 128), np.float32)
    cb[:, 13, :] = (p[:, None] >= p[None, :])
    cb[:, 14, :] = 1.0
    cb[:, 0, :] = np.eye(128)
    le = (p[:, None] <= p[None, :]).astype(np.float32)
    lt = (p[:, None] < p[None, :]).astype(np.float32)
    gev = (p[:, None] >= p[None, :]).astype(np.float32)
    for h in range(4):
        cb[:, 1 + h, :] = le
        cb[:, 5 + h, :] = lt
    for hh in range(2):
        cb[:, 9 + hh * 2 + 0, :] = gev
        cb[:, 9 + hh * 2 + 1, :] = le
    half = 8
    inv = 500000.0 ** (-(np.arange(half, dtype=np.float32) * (2.0 / 16)))
    ang = np.arange(T, dtype=np.float32)[None, :] * inv[:, None].astype(np.float32)
    cos, sin = np.cos(ang).astype(np.float32), np.sin(ang).astype(np.float32)
    C = np.ones((64, T), np.float32)
    S = np.zeros((64, T), np.float32)
    C[0:8], C[8:16] = cos, cos
    S[0:8], S[8:16] = -sin, sin
    rope = np.stack([np.concatenate([C, C], 0), np.concatenate([S, S], 0)], 0)
    return cf, cb.astype(ml_dtypes.bfloat16), rope


def _wm_cols():
    ar = np.arange
    chunks = []
    for g in range(3):
        chunks.append(ar(g * 128, (g + 1) * 128))
    for g in range(3):
        chunks.append(384 + ar(g * 128, (g + 1) * 128))
    for g in range(2):
        chunks.append(OFF_B + ar(g * 128, (g + 1) * 128))
    for g in range(2):
        chunks.append(OFF_B + 256 + ar(g * 128, (g + 1) * 128))
    for g in range(2):
        chunks.append(OFF_IQ + ar(g * 128, (g + 1) * 128))
    chunks.append(np.concatenate([OFF_IK + ar(64), OFF_IK + ar(64)]))
    for base in (OFF_C, OFF_C + 256, OFF_D, OFF_D + 256):
        for g in range(2):
            chunks.append(base + ar(g * 128, (g + 1) * 128))
    perm64 = np.arange(64)
    perm64[0:8] = np.arange(8, 16)
    perm64[8:16] = np.arange(0, 8)
    perm128 = np.concatenate([perm64, 64 + perm64])
    for ch in range(NROPE):
        chunks.append(chunks[ch][perm128])
    cols = np.concatenate(chunks + [OFF_FG + ar(4), 768 + ar(384), OFF_B + 512 + ar(256), OFF_C + 512 + ar(256),
                                    OFF_D + 512 + ar(256), OFF_IW + ar(4)])
    assert cols.shape[0] == NC1
    return cols


_NC_CACHE = {}


def _run(T, per_core, dbg=False):
    key = (T, dbg)
    if key not in _NC_CACHE:
        _NC_CACHE[key] = build(T, dbg)
    nc = _NC_CACHE[key]
    res = run_bass_kernel_spmd(nc, per_core, core_ids=list(range(len(per_core))))
    return res


def make_in_maps(T, x, c, ada_w, ada_b, ln_g, ln_b, ffn_w_in, ffn_w_out, mix_w_in, mix_b_gate, mix_b_forget,
                 mix_w_branch, mix_w_out):
    f = lambda a: np.ascontiguousarray(np.asarray(a, dtype=np.float32))
    cf, cb, rope = _consts(T)
    cols = _wm_cols()
    mw = np.asarray(mix_w_in, dtype=np.float32)
    shared = {
        "ada_w": f(ada_w), "ada_b": f(ada_b), "ln_g": f(ln_g), "ln_b": f(ln_b),
        "ffn_w_in": f(ffn_w_in), "ffn_w_out": f(ffn_w_out),
        "wm1": f(mw[:, :, cols]), "wgate": f(mw[:, :, OFF_GATE:OFF_GATE + 4096]),
        "b_gate": f(mix_b_gate), "b_forget": f(mix_b_forget), "w_branch": f(mix_w_branch), "w_o": f(mix_w_out),
        "cf": cf, "cb": cb, "rope": rope,
    }
    xs = np.asarray(x, dtype=np.float32)
    cs = np.asarray(c, dtype=np.float32)
    maps = []
    for b in range(xs.shape[0]):
        m = dict(shared)
        m["x"] = f(xs[b])
        m["c"] = f(cs[b])
        maps.append(m)
    return maps


def kernel(x, c, ada_w, ada_b, ln_g, ln_b, ffn_w_in, ffn_w_out, mix_w_in, mix_b_gate, mix_b_forget,
           mix_w_branch, mix_w_out):
    B, T, _ = np.asarray(x).shape
    maps = make_in_maps(T, x, c, ada_w, ada_b, ln_g, ln_b, ffn_w_in, ffn_w_out, mix_w_in, mix_b_gate,
                        mix_b_forget, mix_w_branch, mix_w_out)
    per_core = [maps[i] for i in range(B)]
    res = _run(T, per_core)
    out = np.stack([np.asarray(res.results[b]["y"], dtype=np.float32) for b in range(B)], axis=0)
    return out
```

```python
import numpy as np
import ml_dtypes
import concourse.bass as bass
import concourse.mybir as mybir
from concourse.bass_utils import run_bass_kernel_spmd
from contextlib import ExitStack

F32 = mybir.dt.float32
BF16 = mybir.dt.bfloat16
AF = mybir.ActivationFunctionType
ALU = mybir.AluOpType
AX = mybir.AxisListType

D = 1024
DFF = 2816
ALPHA = 4.0 ** 0.25
LN_EPS = 1e-5
NITER = 18
TOPK = 256
NDMA = 24
FFN_STOP = 9
ATT_STOP = 9
ATT_ONLY = -1
D_STOP = 9

A_QKV_W, B_QKV_W, IDX_Q_W, IDX_K_W, IDX_W_W, C_QKV_W, D_QKV_W, FG_W = 1152, 768, 256, 64, 4, 768, 768, 4
OFF_B = A_QKV_W
OFF_IQ = OFF_B + B_QKV_W
OFF_IK = OFF_IQ + IDX_Q_W
OFF_IW = OFF_IK + IDX_K_W
OFF_C = OFF_IW + IDX_W_W
OFF_D = OFF_C + C_QKV_W
OFF_FG = OFF_D + D_QKV_W
OFF_GATE = OFF_FG + FG_W

NFM = 34
NQK = 21
NROPE = 13
FGCOL = NFM * 128
TMCOL = FGCOL + 4
NTM = 1156
NC1 = TMCOL + NTM


class Buf:
    __slots__ = ("name", "w", "r")

    def __init__(self, name=""):
        self.name = name
        self.w = None
        self.r = {}


class Prog:
    def __init__(self, nc, es, sync_same=("act", "dve", "pool")):
        self.nc = nc
        self.eng = {"pe": nc.tensor, "act": nc.scalar, "dve": nc.vector, "pool": nc.gpsimd, "sp": nc.sync}
        self.sem = {k: es.enter_context(nc.semaphore("s_" + k)) for k in ("pe", "act", "dve", "pool")}
        self.cnt = {k: 0 for k in self.sem}
        self.seen = {k: {} for k in self.eng}
        self.dsem = [es.enter_context(nc.semaphore("d%d" % i)) for i in range(NDMA)]
        self.dcnt = [0] * NDMA
        self.drr = 0
        self.sync_same = set(sync_same)
        self.ninst = 0
        self.nwait = 0

    def _semh(self, key):
        return self.sem[key[1]] if key[0] == "e" else self.dsem[key[1]]

    def _deps(self, reads, writes):
        deps = {}
        for b in reads:
            if b.w is not None:
                k, v = b.w
                if deps.get(k, 0) < v:
                    deps[k] = v
        for b in writes:
            if b.w is not None:
                k, v = b.w
                if deps.get(k, 0) < v:
                    deps[k] = v
            for k, v in b.r.items():
                if deps.get(k, 0) < v:
                    deps[k] = v
        return deps

    def _waits(self, e, deps):
        seen = self.seen[e]
        for k, v in deps.items():
            if seen.get(k, 0) >= v:
                continue
            if k == ("e", e) and e not in self.sync_same:
                continue
            self.eng[e].wait_ge(self._semh(k), v)
            seen[k] = v
            self.nwait += 1

    def _mark(self, key, val, reads, writes):
        for b in reads:
            if b.r.get(key, 0) < val:
                b.r[key] = val
        for b in writes:
            b.w = (key, val)
            b.r = {}

    def op(self, e, fn, reads=(), writes=()):
        self._waits(e, self._deps(reads, writes))
        fn(self.eng[e]).then_inc(self.sem[e], 1)
        self.cnt[e] += 1
        self.ninst += 1
        self._mark(("e", e), self.cnt[e], reads, writes)

    def dma(self, out, in_, reads=(), writes=(), q="sp", **kw):
        i = self.drr
        self.drr = (i + 1) % NDMA
        deps = self._deps(reads, writes)
        if self.dcnt[i] > 0:
            deps[("d", i)] = self.dcnt[i]
        self._waits(q, deps)
        self.dcnt[i] += 16
        self.eng[q].dma_start(out=out, in_=in_, **kw).then_inc(self.dsem[i], 16)
        self.ninst += 1
        self._mark(("d", i), self.dcnt[i], reads, writes)

    def _all(self):
        deps = {("d", i): c for i, c in enumerate(self.dcnt) if c > 0}
        for k, c in self.cnt.items():
            if c > 0:
                deps[("e", k)] = c
        return deps

    def barrier(self):
        deps = self._all()
        for e in ("pe", "act", "dve", "pool", "sp"):
            self._waits(e, dict(deps))

    def finish(self):
        self._waits("sp", self._all())


def hr(h):
    return slice((h % 2) * 64, (h % 2) * 64 + 64)


def sl(h):
    return (h % 2) * 2 + h // 2


def v3(ap):
    return ap.rearrange("p (a b) -> p a b", a=2)


def build(T, dbg=False):
    NB = T // 128
    NT = T // 256
    nc = bass.Bass("TRN2", target_bir_lowering=False)

    def din(name, shape, dt=F32):
        return nc.dram_tensor(name, list(shape), dt, kind="ExternalInput").ap()

    def dscr(name, shape, dt=F32):
        if dbg:
            return nc.dram_tensor(name, list(shape), dt, kind="ExternalOutput").ap()
        return nc.dram_tensor(name, list(shape), dt).ap()

    x_in = din("x", [T, D])
    c_in = din("c", [D])
    ada_w = din("ada_w", [2, D, 9 * D])
    ada_b = din("ada_b", [2, 9 * D])
    ln_g = din("ln_g", [2, 3, D])
    ln_b = din("ln_b", [2, 3, D])
    w_in = din("ffn_w_in", [2, 2, D, 2 * DFF])
    w_out = din("ffn_w_out", [2, 2, DFF, D])
    wm1 = din("wm1", [2, D, NC1])
    wgate = din("wgate", [2, D, 4096])
    b_gate = din("b_gate", [2, 4096])
    b_forget = din("b_forget", [2, 4])
    w_branch = din("w_branch", [2, 896, D])
    w_o = din("w_o", [2, D, D])
    cf_in = din("cf", [128, 5, 128])
    cb_in = din("cb", [128, 15, 128], BF16)
    rope_in = din("rope", [2, 128, T])
    y_out = nc.dram_tensor("y", [T, D], F32, kind="ExternalOutput").ap()

    S1 = dscr("S1", [T, D])
    S2 = dscr("S2", [T, D])
    MODP = dscr("MODP", [2, 9 * D])
    QKT = dscr("QKT", [NQK, 128, T], BF16)
    FGT = dscr("FGT", [4, T])
    VTM = dscr("VTM", [T, 1152], BF16)
    IWT = dscr("IWT", [T, 4])
    GT = dscr("GT", [32, 128, T], BF16)
    YT = dscr("YT", [7, 128, T], BF16)
    b_S1, b_S2, b_MODP, b_QKT, b_FGT, b_VTM, b_IWT, b_GT, b_YT = [Buf() for _ in range(9)]

    with ExitStack() as es:
        P = Prog(nc, es)

        uid = [0]

        def sbt(st, name, shape, dt=F32):
            uid[0] += 1
            return st.enter_context(nc.sbuf_tensor("%s_%d" % (name, uid[0]), list(shape), dt))

        pqq = es.enter_context(nc.psum_tensor("pqq", [128, 4, 512], F32))
        pq = [pqq[:, i, :] for i in range(4)]
        SS = pqq[:, 0:2, 0:256]
        pw = es.enter_context(nc.psum_tensor("pw", [128, 4, 512], F32))
        b_pq = [Buf() for _ in range(4)]
        b_pw = [Buf() for _ in range(4)]

        cf = sbt(es, "cf", [128, 5, 128])
        cb = sbt(es, "cb", [128, 15, 128], BF16)
        b_c = Buf()
        P.dma(cf[:], cf_in, writes=[b_c])
        P.dma(cb[:], cb_in, writes=[b_c])
        ident, Uge, ones, E0, negmask = [cf[:, i, :] for i in range(5)]
        identb = cb[:, 0, :]
        LE4 = cb[:, 1:5, :]
        LT4 = cb[:, 5:9, :]
        MA4 = cb[:, 9:13, :]
        Ugeb = cb[:, 13, :]
        onesb = cb[:, 14, :]

        with ExitStack() as st:
            condT = sbt(st, "condT", [128, 8])
            modrow = sbt(st, "modrow", [1, 9 * D])
            brow = sbt(st, "brow", [1, 9 * D])
            wa = [sbt(st, "wa%d" % i, [128, 8, 512]) for i in range(2)]
            b_cond, b_mod, b_brow = Buf(), Buf(), Buf()
            b_wa = [Buf(), Buf()]
            P.dma(condT[:], c_in.rearrange("(c p) -> p c", p=128), writes=[b_cond], allow_slow_non_contiguous=True)
            P.op("act", lambda e: e.activation(out=condT[:], in_=condT[:], func=AF.Silu), reads=[b_cond], writes=[b_cond])
            for l in range(2):
                P.dma(brow[:], ada_b[l:l + 1, :], writes=[b_brow])
                for blk in range(18):
                    k = blk % 2
                    P.dma(wa[k][:], ada_w[l, :, blk * 512:(blk + 1) * 512].rearrange("(c p) n -> p c n", p=128), writes=[b_wa[k]])
                    for c in range(8):
                        P.op("pe", lambda e: e.matmul(pq[k][0:1, :], lhsT=condT[:, c:c + 1], rhs=wa[k][:, c, :], start=(c == 0), stop=(c == 7)),
                             reads=[b_cond, b_wa[k]], writes=[b_pq[k]])
                    P.op("dve", lambda e: e.tensor_tensor(out=modrow[:, blk * 512:(blk + 1) * 512], in0=pq[k][0:1, :], in1=brow[:, blk * 512:(blk + 1) * 512], op=ALU.add),
                         reads=[b_pq[k], b_brow], writes=[b_mod])
                for s in range(3):
                    sc_ = modrow[:, (s * 3 + 1) * D:(s * 3 + 2) * D]
                    gt_ = modrow[:, (s * 3 + 2) * D:(s * 3 + 3) * D]
                    P.op("dve", lambda e: e.tensor_scalar_add(out=sc_, in0=sc_, scalar1=1.0), reads=[b_mod], writes=[b_mod])
                    f = 1.0 if s == 1 else 0.5
                    P.op("dve", lambda e: e.tensor_scalar(out=gt_, in0=gt_, scalar1=1.0, scalar2=f, op0=ALU.add, op1=ALU.mult), reads=[b_mod], writes=[b_mod])
                P.dma(MODP[l:l + 1, :], modrow[:], reads=[b_mod], writes=[b_MODP])
        P.barrier()

        def load_bcast(tile, src_row, buf):
            P.dma(tile[:], src_row.partition_broadcast(128), reads=[b_MODP], writes=[buf])

        def make_uT(t0, xin, b_xin, xs, b_xs, u, b_u, xT, b_xT, s1, sh, b_ms):
            P.dma(xs[:], xin[t0:t0 + 256, :].rearrange("(j p) d -> p j d", p=128), reads=[b_xin], writes=[b_xs])
            for j in range(2):
                P.op("dve", lambda e: e.tensor_tensor(out=u[:, j, :], in0=xs[:, j, :], in1=s1[:], op=ALU.mult), reads=[b_xs, b_ms], writes=[b_u[j]])
                P.op("pool", lambda e: e.tensor_tensor(out=u[:, j, :], in0=u[:, j, :], in1=sh[:], op=ALU.add), reads=[b_u[j], b_ms], writes=[b_u[j]])
                for c in range(8):
                    P.op("pe", lambda e: e.transpose(out=pq[c // 4][:, (c % 4) * 128:(c % 4 + 1) * 128], in_=u[:, j, c * 128:(c + 1) * 128], identity=ident),
                         reads=[b_u[j], b_c], writes=[b_pq[c // 4]])
                for hh in range(2):
                    P.op("act", lambda e: e.activation(out=xT[:, hh * 4:(hh + 1) * 4, j * 128:(j + 1) * 128],
                                                       in_=pq[hh][:].rearrange("p (c n) -> p c n", c=4), func=AF.Copy),
                         reads=[b_pq[hh]], writes=[b_xT])

        def make_uT_a(t0, xin, b_xin, xs, b_xs, u, b_u, s1, sh, b_ms):
            P.dma(xs[:], xin[t0:t0 + 256, :].rearrange("(j p) d -> p j d", p=128), reads=[b_xin], writes=[b_xs])
            for j in range(2):
                P.op("dve", lambda e: e.tensor_tensor(out=u[:, j, :], in0=xs[:, j, :], in1=s1[:], op=ALU.mult), reads=[b_xs, b_ms], writes=[b_u[j]])
                P.op("pool", lambda e: e.tensor_tensor(out=u[:, j, :], in0=u[:, j, :], in1=sh[:], op=ALU.add), reads=[b_u[j], b_ms], writes=[b_u[j]])

        def make_uT_b(u, b_u, xT, b_xT):
            for j in range(2):
                for c in range(8):
                    P.op("pe", lambda e: e.transpose(out=pq[c // 4][:, (c % 4) * 128:(c % 4 + 1) * 128], in_=u[:, j, c * 128:(c + 1) * 128], identity=ident),
                         reads=[b_u[j], b_c], writes=[b_pq[c // 4]])
                for hh in range(2):
                    P.op("act", lambda e: e.activation(out=xT[:, hh * 4:(hh + 1) * 4, j * 128:(j + 1) * 128],
                                                       in_=pq[hh][:].rearrange("p (c n) -> p c n", c=4), func=AF.Copy),
                         reads=[b_pq[hh]], writes=[b_xT])

        def deepnorm_ln(j, xs, b_xs, gp, lng, lnb, b_ms, t1, r, b_t1, b_r, small, b_small, xo, b_xo):
            pwj = pw[:, 2 * j:2 * j + 2, :]
            P.op("dve", lambda e: e.tensor_tensor(out=t1[:].rearrange("p (a b) -> p a b", a=2), in0=pwj, in1=gp[:].rearrange("p (a b) -> p a b", a=2), op=ALU.mult),
                 reads=[b_pw[2 * j], b_pw[2 * j + 1], b_ms], writes=[b_t1])
            P.op("dve", lambda e: e.scalar_tensor_tensor(out=r[:], in0=xs[:, j, :], scalar=ALPHA, in1=t1[:], op0=ALU.mult, op1=ALU.add),
                 reads=[b_xs, b_t1], writes=[b_r])
            st6 = small[:, 0:12].rearrange("p (a b) -> p a b", a=2)
            for k in range(2):
                P.op("dve", lambda e: e.bn_stats(out=st6[:, k, :], in_=r[:, k * 512:(k + 1) * 512]), reads=[b_r], writes=[b_small])
            mv = small[:, 12:14]
            P.op("dve", lambda e: e.bn_aggr(out=mv, in_=st6), reads=[b_small], writes=[b_small])
            P.op("dve", lambda e: e.tensor_scalar_add(out=small[:, 14:15], in0=small[:, 13:14], scalar1=LN_EPS), reads=[b_small], writes=[b_small])
            P.op("act", lambda e: e.activation(out=small[:, 15:16], in_=small[:, 14:15], func=AF.Sqrt), reads=[b_small], writes=[b_small])
            P.op("dve", lambda e: e.reciprocal(out=small[:, 16:17], in_=small[:, 15:16]), reads=[b_small], writes=[b_small])
            P.op("dve", lambda e: e.tensor_scalar(out=small[:, 17:18], in0=small[:, 12:13], scalar1=small[:, 16:17], scalar2=-1.0, op0=ALU.mult, op1=ALU.mult),
                 reads=[b_small], writes=[b_small])
            P.op("act", lambda e: e.activation(out=t1[:], in_=r[:], func=AF.Identity, scale=small[:, 16:17], bias=small[:, 17:18]),
                 reads=[b_r, b_small], writes=[b_t1])
            P.op("dve", lambda e: e.tensor_tensor(out=t1[:], in0=t1[:], in1=lng[:], op=ALU.mult), reads=[b_t1, b_ms], writes=[b_t1])
            P.op("pool", lambda e: e.tensor_tensor(out=xo[:, j, :], in0=t1[:], in1=lnb[:], op=ALU.add), reads=[b_t1, b_ms], writes=[b_xo])

        def load_w_bf16(dst, src, kchunks, b_dst, piece=1408):
            N = src.shape[1]
            v = src.rearrange("(c p) n -> p c n", p=128)
            for c in range(kchunks):
                for n0 in range(0, N, piece):
                    n1 = min(N, n0 + piece)
                    P.dma(dst[:, c, n0:n1], v[:, c, n0:n1], writes=[b_dst], q="pool")

        def ffn_phase(l, f, sub, xin, b_xin, xout, b_xout):
            with ExitStack() as st:
                w1 = sbt(st, "w1", [128, 8, 2 * DFF], BF16)
                w2 = sbt(st, "w2", [128, 22, D], BF16)
                b_w1, b_w2, b_ms = Buf(), Buf(), Buf()
                load_w_bf16(w1, w_in[l, f], 8, b_w1)
                load_w_bf16(w2, w_out[l, f], 22, b_w2, piece=1024)
                s1, sh, gp, lng, lnb = [sbt(st, "m%d" % i, [128, D]) for i in range(5)]
                load_bcast(sh, MODP[l, (sub * 3 + 0) * D:(sub * 3 + 1) * D], b_ms)
                load_bcast(s1, MODP[l, (sub * 3 + 1) * D:(sub * 3 + 2) * D], b_ms)
                load_bcast(gp, MODP[l, (sub * 3 + 2) * D:(sub * 3 + 3) * D], b_ms)
                P.dma(lng[:], ln_g[l, sub, :].partition_broadcast(128), writes=[b_ms])
                P.dma(lnb[:], ln_b[l, sub, :].partition_broadcast(128), writes=[b_ms])
                xs = [sbt(st, "xs%d" % i, [128, 2, D]) for i in range(2)]
                b_xs = [[Buf(), Buf()], [Buf(), Buf()]]
                xo = sbt(st, "xo", [128, 2, D])
                b_xo = [Buf(), Buf()]
                xT = sbt(st, "xT", [128, 8, 256], BF16)
                b_xT = Buf()
                aT = sbt(st, "aT", [128, 22, 256], BF16)
                b_aT = Buf()
                sg = [sbt(st, "sg%d" % i, [128, 256]) for i in range(2)]
                b_sg = [Buf(), Buf()]
                r = sbt(st, "r", [128, D])
                small = sbt(st, "small", [128, 32])
                b_r, b_small = Buf(), Buf()

                def xtile(ap, ti):
                    return ap[ti * 256:ti * 256 + 256, :].rearrange("(j p) d -> p j d", p=128)

                def f_load(ti):
                    P.dma(xs[ti % 2][:], xtile(xin, ti), reads=[b_xin], writes=b_xs[ti % 2])

                def f_mod(ti):
                    k2 = ti % 2
                    for j in range(2):
                        P.op("dve", lambda e: e.tensor_tensor(out=xs[k2][:, j, :], in0=xs[k2][:, j, :], in1=s1[:], op=ALU.mult), reads=[b_xs[k2][j], b_ms], writes=[b_xs[k2][j]])
                        P.op("pool", lambda e: e.tensor_tensor(out=xs[k2][:, j, :], in0=xs[k2][:, j, :], in1=sh[:], op=ALU.add), reads=[b_xs[k2][j], b_ms], writes=[b_xs[k2][j]])

                def f_trans(ti):
                    k2 = ti % 2
                    for j in range(2):
                        for c in range(8):
                            P.op("pe", lambda e: e.transpose(out=pq[c // 4][:, (c % 4) * 128:(c % 4 + 1) * 128], in_=xs[k2][:, j, c * 128:(c + 1) * 128], identity=ident),
                                 reads=[b_xs[k2][j], b_c], writes=[b_pq[c // 4]])
                        for hh in range(2):
                            P.op("act", lambda e: e.activation(out=xT[:, hh * 4:(hh + 1) * 4, j * 128:(j + 1) * 128],
                                                               in_=pq[hh][:].rearrange("p (c n) -> p c n", c=4), func=AF.Copy),
                                 reads=[b_pq[hh]], writes=[b_xT])

                def f_ln(j):
                    xo_j = xo[:, j, :]
                    pwj = pw[:, 2 * j:2 * j + 2, :]
                    P.op("dve", lambda e: e.tensor_tensor(out=r[:].rearrange("p (a b) -> p a b", a=2), in0=pwj, in1=gp[:].rearrange("p (a b) -> p a b", a=2), op=ALU.mult),
                         reads=[b_pw[2 * j], b_pw[2 * j + 1], b_ms], writes=[b_r])
                    P.op("dve", lambda e: e.scalar_tensor_tensor(out=r[:], in0=xo_j, scalar=ALPHA, in1=r[:], op0=ALU.mult, op1=ALU.add), reads=[b_xo[j], b_r], writes=[b_r])
                    st6 = small[:, 0:12].rearrange("p (a b) -> p a b", a=2)
                    for k in range(2):
                        P.op("dve", lambda e: e.bn_stats(out=st6[:, k, :], in_=r[:, k * 512:(k + 1) * 512]), reads=[b_r], writes=[b_small])
                    P.op("dve", lambda e: e.bn_aggr(out=small[:, 12:14], in_=st6), reads=[b_small], writes=[b_small])
                    P.op("dve", lambda e: e.tensor_scalar_add(out=small[:, 14:15], in0=small[:, 13:14], scalar1=LN_EPS), reads=[b_small], writes=[b_small])
                    P.op("act", lambda e: e.activation(out=small[:, 15:16], in_=small[:, 14:15], func=AF.Sqrt), reads=[b_small], writes=[b_small])
                    P.op("dve", lambda e: e.reciprocal(out=small[:, 16:17], in_=small[:, 15:16]), reads=[b_small], writes=[b_small])
                    P.op("dve", lambda e: e.tensor_scalar(out=small[:, 17:18], in0=small[:, 12:13], scalar1=small[:, 16:17], scalar2=-1.0, op0=ALU.mult, op1=ALU.mult),
                         reads=[b_small], writes=[b_small])
                    P.op("act", lambda e: e.activation(out=xo_j, in_=r[:], func=AF.Identity, scale=small[:, 16:17], bias=small[:, 17:18]),
                         reads=[b_r, b_small], writes=[b_xo[j]])
                    P.op("dve", lambda e: e.tensor_tensor(out=xo_j, in0=xo_j, in1=lng[:], op=ALU.mult), reads=[b_xo[j], b_ms], writes=[b_xo[j]])
                    P.op("pool", lambda e: e.tensor_tensor(out=xo_j, in0=xo_j, in1=lnb[:], op=ALU.add), reads=[b_xo[j], b_ms], writes=[b_xo[j]])

                f_load(0)
                f_mod(0)
                f_trans(0)
                for ti in range(NT):
                    if ti + 1 < NT:
                        f_load(ti + 1)
                    for m in range(22):
                        k = 2 + (m % 2)
                        for half in range(2):
                            col = half * DFF + m * 128
                            for c in range(8):
                                P.op("pe", lambda e: e.matmul(pq[k][:, half * 256:(half + 1) * 256], lhsT=w1[:, c, col:col + 128], rhs=xT[:, c, :], start=(c == 0), stop=(c == 7)),
                                     reads=[b_xT, b_w1], writes=[b_pq[k]])
                        P.op("act", lambda e: e.activation(out=sg[m % 2][:], in_=pq[k][:, 0:256], func=AF.Silu), reads=[b_pq[k]], writes=[b_sg[m % 2]])
                        P.op("dve", lambda e: e.tensor_tensor(out=aT[:, m, :], in0=sg[m % 2][:], in1=pq[k][:, 256:512], op=ALU.mult),
                             reads=[b_sg[m % 2], b_pq[k]], writes=[b_aT])
                    if ti + 1 < NT:
                        f_mod(ti + 1)
                    for j in range(2):
                        for nh in range(2):
                            for m in range(22):
                                P.op("pe", lambda e: e.matmul(pw[:, 2 * j + nh, :], lhsT=aT[:, m, j * 128:(j + 1) * 128], rhs=w2[:, m, nh * 512:(nh + 1) * 512], start=(m == 0), stop=(m == 21)),
                                     reads=[b_aT, b_w2], writes=[b_pw[2 * j + nh]])
                    P.dma(xo[:], xtile(xin, ti), reads=[b_xin], writes=b_xo)
                    if ti + 1 < NT:
                        f_trans(ti + 1)
                    for j in range(2):
                        f_ln(j)
                    P.dma(xtile(xout, ti), xo[:], reads=b_xo, writes=[b_xout])
            P.barrier()

        def inproj_phase(l, xin, b_xin):
            with ExitStack() as st:
                wm = sbt(st, "wm", [128, 8, NC1], BF16)
                b_wm, b_ms = Buf(), Buf()
                load_w_bf16(wm, wm1[l], 8, b_wm, piece=1024)
                s1, sh = [sbt(st, "m%d" % i, [128, D]) for i in range(2)]
                load_bcast(sh, MODP[l, 3 * D:4 * D], b_ms)
                load_bcast(s1, MODP[l, 4 * D:5 * D], b_ms)
                xs2 = [sbt(st, "xs%d" % i, [128, 2, D]) for i in range(2)]
                b_xs2 = [Buf(), Buf()]
                u2 = [sbt(st, "u%d" % i, [128, 2, D]) for i in range(2)]
                b_u2 = [[Buf(), Buf()], [Buf(), Buf()]]
                xT2 = [sbt(st, "xT%d" % i, [128, 8, 256], BF16) for i in range(2)]
                b_xT2 = [Buf(), Buf()]
                rc = sbt(st, "rc", [128, 2, 256])
                b_rc = Buf()
                ta = [sbt(st, "ta%d" % i, [128, 256]) for i in range(2)]
                tb = [sbt(st, "tb%d" % i, [128, 256]) for i in range(2)]
                b_ta, b_tb = [Buf(), Buf()], [Buf(), Buf()]
                stg = sbt(st, "stg", [128, NQK, 256], BF16)
                fgs = sbt(st, "fgs", [4, 256])
                vst = sbt(st, "vst", [128, 2, 1152], BF16)
                iws = sbt(st, "iws", [128, 2, 4])
                b_stg, b_fgs, b_vst, b_iws = Buf(), Buf(), Buf(), Buf()
                make_uT_a(0, xin, b_xin, xs2[0], b_xs2[0], u2[0], b_u2[0], s1, sh, b_ms)
                make_uT_b(u2[0], b_u2[0], xT2[0], b_xT2[0])
                for ti in range(NT):
                    t0 = ti * 256
                    xT, b_xT = xT2[ti % 2], b_xT2[ti % 2]
                    n2 = (ti + 1) % 2
                    P.dma(rc[:], rope_in[:, :, t0:t0 + 256].rearrange("a p t -> p a t"), writes=[b_rc])
                    if ti + 1 < NT:
                        make_uT_a(t0 + 256, xin, b_xin, xs2[n2], b_xs2[n2], u2[n2], b_u2[n2], s1, sh, b_ms)
                    for ch in range(NQK):
                        if ti + 1 < NT and ch == 16:
                            make_uT_b(u2[n2], b_u2[n2], xT2[n2], b_xT2[n2])
                        if ch < NROPE:
                            k = 2 * (ch % 2)
                            for which, cc in ((0, ch), (1, NQK + ch)):
                                for c in range(8):
                                    P.op("pe", lambda e: e.matmul(pw[:, k + which, 0:256], lhsT=wm[:, c, cc * 128:(cc + 1) * 128], rhs=xT[:, c, :], start=(c == 0), stop=(c == 7)),
                                         reads=[b_xT, b_wm], writes=[b_pw[k + which]])
                            kk = ch % 2
                            P.op("dve", lambda e: e.tensor_tensor(out=ta[kk][:], in0=pw[:, k, 0:256], in1=rc[:, 0, :], op=ALU.mult), reads=[b_pw[k], b_rc], writes=[b_ta[kk]])
                            P.op("dve", lambda e: e.tensor_tensor(out=tb[kk][:], in0=pw[:, k + 1, 0:256], in1=rc[:, 1, :], op=ALU.mult), reads=[b_pw[k + 1], b_rc], writes=[b_tb[kk]])
                            P.op("pool", lambda e: e.tensor_tensor(out=stg[:, ch, :], in0=ta[kk][:], in1=tb[kk][:], op=ALU.add), reads=[b_ta[kk], b_tb[kk]], writes=[b_stg])
                        else:
                            k = ch % 4
                            for c in range(8):
                                P.op("pe", lambda e: e.matmul(pw[:, k, 0:256], lhsT=wm[:, c, ch * 128:(ch + 1) * 128], rhs=xT[:, c, :], start=(c == 0), stop=(c == 7)),
                                     reads=[b_xT, b_wm], writes=[b_pw[k]])
                            P.op("act", lambda e: e.activation(out=stg[:, ch, :], in_=pw[:, k, 0:256], func=AF.Copy), reads=[b_pw[k]], writes=[b_stg])
                    for c in range(8):
                        P.op("pe", lambda e: e.matmul(pw[0:4, 0, 0:256], lhsT=wm[:, c, FGCOL:FGCOL + 4], rhs=xT[:, c, :], start=(c == 0), stop=(c == 7)),
                             reads=[b_xT, b_wm], writes=[b_pw[0]])
                    P.op("act", lambda e: e.activation(out=fgs[:], in_=pw[0:4, 0, 0:256], func=AF.Copy), reads=[b_pw[0]], writes=[b_fgs])
                    P.dma(FGT[:, t0:t0 + 256], fgs[:], reads=[b_fgs], writes=[b_FGT])
                    P.dma(QKT[:, :, t0:t0 + 256].rearrange("c p t -> p c t"), stg[:], reads=[b_stg], writes=[b_QKT])
                    ki = 0
                    for j in range(2):
                        for (n0, n1) in ((0, 384), (384, 896), (896, 1156)):
                            k = 2 + (ki % 2)
                            ki += 1
                            for c in range(8):
                                P.op("pe", lambda e: e.matmul(pq[k][:, 0:n1 - n0], lhsT=xT[:, c, j * 128:(j + 1) * 128], rhs=wm[:, c, TMCOL + n0:TMCOL + n1], start=(c == 0), stop=(c == 7)),
                                     reads=[b_xT, b_wm], writes=[b_pq[k]])
                            if n1 <= 1152:
                                P.op("act", lambda e: e.activation(out=vst[:, j, n0:n1], in_=pq[k][:, 0:n1 - n0], func=AF.Copy), reads=[b_pq[k]], writes=[b_vst])
                            else:
                                P.op("act", lambda e: e.activation(out=vst[:, j, n0:1152], in_=pq[k][:, 0:1152 - n0], func=AF.Copy), reads=[b_pq[k]], writes=[b_vst])
                                P.op("dve", lambda e: e.tensor_copy(out=iws[:, j, :], in_=pq[k][:, 1152 - n0:1156 - n0]), reads=[b_pq[k]], writes=[b_iws])
                    P.dma(VTM[t0:t0 + 256, :].rearrange("(j p) n -> p j n", p=128), vst[:], reads=[b_vst], writes=[b_VTM])
                    P.dma(IWT[t0:t0 + 256, :].rearrange("(j p) n -> p j n", p=128), iws[:], reads=[b_iws], writes=[b_IWT])
            P.barrier()
            with ExitStack() as st:
                wg = sbt(st, "wg", [128, 8, 4096], BF16)
                b_wg, b_ms = Buf(), Buf()
                load_w_bf16(wg, wgate[l], 8, b_wg, piece=1024)
                s1, sh = [sbt(st, "m%d" % i, [128, D]) for i in range(2)]
                load_bcast(sh, MODP[l, 3 * D:4 * D], b_ms)
                load_bcast(s1, MODP[l, 4 * D:5 * D], b_ms)
                bg = sbt(st, "bg", [128, 32])
                P.dma(bg[:], b_gate[l].rearrange("(c p) -> p c", p=128), writes=[b_ms], allow_slow_non_contiguous=True)
                xs2 = [sbt(st, "xs%d" % i, [128, 2, D]) for i in range(2)]
                b_xs2 = [Buf(), Buf()]
                u2 = [sbt(st, "u%d" % i, [128, 2, D]) for i in range(2)]
                b_u2 = [[Buf(), Buf()], [Buf(), Buf()]]
                xT2 = [sbt(st, "xT%d" % i, [128, 8, 256], BF16) for i in range(2)]
                b_xT2 = [Buf(), Buf()]
                gst = [sbt(st, "gst%d" % i, [128, 32, 256], BF16) for i in range(2)]
                b_gst = [Buf(), Buf()]
                make_uT_a(0, xin, b_xin, xs2[0], b_xs2[0], u2[0], b_u2[0], s1, sh, b_ms)
                make_uT_b(u2[0], b_u2[0], xT2[0], b_xT2[0])
                for ti in range(NT):
                    t0 = ti * 256
                    g2 = ti % 2
                    xT, b_xT = xT2[g2], b_xT2[g2]
                    n2 = (ti + 1) % 2
                    for ch in range(32):
                        k = ch % 4
                        if ti + 1 < NT and ch == 0:
                            make_uT_a(t0 + 256, xin, b_xin, xs2[n2], b_xs2[n2], u2[n2], b_u2[n2], s1, sh, b_ms)
                        if ti + 1 < NT and ch == 24:
                            make_uT_b(u2[n2], b_u2[n2], xT2[n2], b_xT2[n2])
                        for c in range(8):
                            P.op("pe", lambda e: e.matmul(pw[:, k, 0:256], lhsT=wg[:, c, ch * 128:(ch + 1) * 128], rhs=xT[:, c, :], start=(c == 0), stop=(c == 7)),
                                 reads=[b_xT, b_wg], writes=[b_pw[k]])
                        P.op("act", lambda e: e.activation(out=gst[g2][:, ch, :], in_=pw[:, k, 0:256], func=AF.Sigmoid, bias=bg[:, ch:ch + 1]), reads=[b_pw[k], b_ms], writes=[b_gst[g2]])
                    P.dma(GT[:, :, t0:t0 + 256].rearrange("c p t -> p c t"), gst[g2][:], reads=[b_gst[g2]], writes=[b_GT])
            P.barrier()

        def attention_phase(l):
            with ExitStack() as sm:
                cumT = sbt(sm, "cumT", [128, NB * 4])
                Gb = sbt(sm, "Gb", [128, NB * 4])
                ones64 = sbt(sm, "ones64", [128, 64])
                b_cum = Buf()
                b_o64 = Buf()
                P.op("pool", lambda e: e.memset(ones64[:], 1.0), writes=[b_o64])
                with ExitStack() as st:
                    fg = sbt(st, "fg", [4, T])
                    sp = sbt(st, "sp", [4, T])
                    on4 = sbt(st, "on4", [4, T])
                    nb4 = sbt(st, "nb4", [4, 1])
                    b_fg, b_sp, b_on, b_nb = Buf(), Buf(), Buf(), Buf()
                    P.dma(fg[:], FGT, reads=[b_FGT], writes=[b_fg])
                    P.dma(nb4[:], b_forget[l].rearrange("(p a) -> p a", a=1), writes=[b_nb])
                    P.op("dve", lambda e: e.tensor_scalar_mul(out=nb4[:], in0=nb4[:], scalar1=-1.0), reads=[b_nb], writes=[b_nb])
                    P.op("pool", lambda e: e.memset(on4[:], 1.0), writes=[b_on])
                    P.op("act", lambda e: e.activation(out=sp[:], in_=fg[:], func=AF.Exp, scale=-1.0, bias=nb4[:, 0:1]), reads=[b_fg, b_nb], writes=[b_sp])
                    P.op("act", lambda e: e.activation(out=sp[:], in_=sp[:], func=AF.Ln, bias=1.0), reads=[b_sp], writes=[b_sp])
                    P.op("dve", lambda e: e.tensor_tensor_scan(out=fg[:], data0=on4[:], data1=sp[:], initial=0.0, op0=ALU.mult, op1=ALU.subtract),
                         reads=[b_on, b_sp], writes=[b_fg])
                    for blk in range(NB):
                        P.op("pe", lambda e: e.transpose(out=pq[0][:, blk * 4:(blk + 1) * 4], in_=fg[0:4, blk * 128:(blk + 1) * 128], identity=ident[0:4, 0:4]),
                             reads=[b_fg, b_c], writes=[b_pq[0]])
                    P.op("act", lambda e: e.activation(out=cumT[:], in_=pq[0][:, 0:NB * 4], func=AF.Copy), reads=[b_pq[0]], writes=[b_cum])
                    P.op("pe", lambda e: e.matmul(pq[1][:, 0:NB * 4], lhsT=E0, rhs=cumT[:], start=True, stop=True), reads=[b_cum, b_c], writes=[b_pq[1]])
                    P.op("act", lambda e: e.activation(out=Gb[:], in_=pq[1][:, 0:NB * 4], func=AF.Copy), reads=[b_pq[1]], writes=[b_cum])
                P.barrier()

                def load_kqv(st, qch, kch, vcol, nh, noq=False):
                    kt = sbt(st, "kt", [128, nh // 2, T], BF16)
                    qt = sbt(st, "qt", [128, nh // 2, 128 if noq else T], BF16)
                    va = sbt(st, "va", [128, NB, nh, 65], BF16)
                    b_k, b_q, b_v = Buf(), Buf(), Buf()
                    P.dma(kt[:], QKT[kch:kch + nh // 2].rearrange("c p t -> p c t"), reads=[b_QKT], writes=[b_k])
                    if not noq:
                        P.dma(qt[:], QKT[qch:qch + nh // 2].rearrange("c p t -> p c t"), reads=[b_QKT], writes=[b_q])
                    P.op("pool", lambda e: e.memset(va[:, :, :, 64:65], 1.0), writes=[b_v])
                    for h in range(nh):
                        P.dma(va[:, :, h, 0:64], VTM[:, vcol + h * 64:vcol + (h + 1) * 64].rearrange("(k p) e -> p k e", p=128), reads=[b_VTM], writes=[b_v])
                    return kt, qt, va, b_k, b_q, b_v

                Sk = [pqq[:, 0:2, 0:256], pqq[:, 2:4, 0:256]]
                b_Sk = [[b_pq[0], b_pq[1]], [b_pq[2], b_pq[3]]]
                po4 = pw[:, 0, :].rearrange("p (h t) -> p h t", h=4)
                b_po = b_pw[0]
                pbk = pw[:, 1, :]
                b_pb = b_pw[1]

                def sreg(k, h):
                    return pqq[:, 2 * k + h % 2, (h // 2) * 128:(h // 2 + 1) * 128]

                def qk4(k, kt, qt, b_k, b_q, i, j):
                    for h in range(4):
                        P.op("pe", lambda e: e.matmul(sreg(k, h), lhsT=kt[hr(h), h // 2, j * 128:(j + 1) * 128], rhs=qt[hr(h), h // 2, i * 128:(i + 1) * 128], start=True, stop=True),
                             reads=[b_k, b_q], writes=[b_pq[2 * k + h % 2]])

                def av4(pT, b_pT, va, b_v, j, first, last, M):
                    for h in range(4):
                        P.op("pe", lambda e: e.matmul(po4[0:M, h, :], lhsT=va[:, j, h, 0:M], rhs=pT[:, sl(h), :], start=(first and h == 0), stop=last, skip_group_check=True),
                             reads=[b_pT, b_v], writes=[b_po])

                def normalize_store(rrow, b_rr, rb, b_rb, yst, b_yst, ych, i):
                    P.op("dve", lambda e: e.reciprocal(out=rrow[64:65, :, :], in_=po4[64:65, :, :]), reads=[b_po], writes=[b_rr])
                    P.op("pe", lambda e: e.matmul(pbk[0:64, :], lhsT=ones64[64:65, :], rhs=rrow[64:65, :, :].rearrange("p a b -> p (a b)"), start=True, stop=True),
                         reads=[b_rr, b_o64], writes=[b_pb])
                    P.op("act", lambda e: e.activation(out=rb[0:64, :, :].rearrange("p a b -> p (a b)"), in_=pbk[0:64, :], func=AF.Copy), reads=[b_pb], writes=[b_rb])
                    P.op("dve", lambda e: e.tensor_tensor(out=yst[0:64, :, :], in0=po4[0:64, :, :], in1=rb[0:64, :, :], op=ALU.mult), reads=[b_po, b_rb], writes=[b_yst])
                    P.dma(YT[ych:ych + 2, :, i * 128:(i + 1) * 128].rearrange("c (h e) t -> e (c h) t", e=64), yst[0:64, :, :], reads=[b_yst], writes=[b_YT])

                def flat(t):
                    return t[:].rearrange("p a b -> p (a b)")

                with ExitStack() as st:
                  if ATT_STOP >= 1 and ATT_ONLY in (-1, 1):
                      kt, qt, va, b_k, b_q, b_v = load_kqv(st, 17, 19, 896, 4)
                      bm = [sbt(st, "bm%d" % i, [128, 4, NB]) for i in range(2)]
                      b_bm = [Buf(), Buf()]
                      tS = [sbt(st, "tS%d" % i, [128, 4, 128]) for i in range(2)]
                      b_tS = [Buf(), Buf()]
                      pT = [sbt(st, "pT%d" % i, [128, 4, 128], BF16) for i in range(2)]
                      b_pT = [Buf(), Buf()]
                      rrow = sbt(st, "rrow", [128, 4, 128])
                      rb = sbt(st, "rb", [64, 4, 128])
                      yst = sbt(st, "yst", [64, 4, 128], BF16)
                      b_rr, b_rb, b_yst = Buf(), Buf(), Buf()
                      cum3 = cumT[:].rearrange("p (k h) -> p k h", h=4)
                      pc = 0
                      for i in range(NB):
                          bi = i % 2
                          for h in range(4):
                              P.op("dve", lambda e: e.tensor_scalar(out=bm[bi][:, h, 0:i + 1], in0=cum3[:, 0:i + 1, h], scalar1=-1.0, scalar2=Gb[:, i * 4 + h:i * 4 + h + 1], op0=ALU.mult, op1=ALU.add),
                                   reads=[b_cum], writes=[b_bm[bi]])
                          qk4(0, kt, qt, b_k, b_q, i, 0)
                          for j in range(i + 1):
                              k = j % 2
                              if j < i:
                                  qk4((j + 1) % 2, kt, qt, b_k, b_q, i, j + 1)
                              kp = pc % 2
                              pc += 1
                              for h in (1, 3):
                                  P.op("dve", lambda e: e.tensor_scalar(out=tS[kp][:, sl(h), :], in0=sreg(k, h), scalar1=0.125, scalar2=bm[bi][:, h, j:j + 1], op0=ALU.mult, op1=ALU.add),
                                       reads=[b_pq[2 * k + h % 2], b_bm[bi]], writes=[b_tS[kp]])
                              for h in (0, 2):
                                  P.op("act", lambda e: e.activation(out=pT[kp][:, sl(h), :], in_=sreg(k, h), func=AF.Exp, scale=0.125, bias=bm[bi][:, h, j:j + 1]),
                                       reads=[b_pq[2 * k + h % 2], b_bm[bi]], writes=[b_pT[kp]])
                              P.op("act", lambda e: e.activation(out=pT[kp][:, 2:4, :], in_=tS[kp][:, 2:4, :], func=AF.Exp), reads=[b_tS[kp]], writes=[b_pT[kp]])
                              if j == i:
                                  P.op("pool", lambda e: e.tensor_tensor(out=pT[kp][:], in0=pT[kp][:], in1=LE4, op=ALU.mult), reads=[b_pT[kp], b_c], writes=[b_pT[kp]])
                              av4(pT[kp], b_pT[kp], va, b_v, j, j == 0, j == i, 65)
                          normalize_store(rrow, b_rr, rb, b_rb, yst, b_yst, 5, i)
                P.barrier()

                with ExitStack() as st:
                  if ATT_STOP >= 2 and ATT_ONLY in (-1, 2):
                      kt, qt, va, b_k, b_q, b_v = load_kqv(st, 13, 15, 640, 4)
                      ef = sbt(st, "ef", [128, 512])
                      spf = [sbt(st, "spf%d" % i, [128, 512]) for i in range(2)]
                      tt = sbt(st, "tt", [128, 512])
                      arg = sbt(st, "arg", [128, 512])
                      carry = sbt(st, "carry", [128, 512])
                      b_ef, b_tt, b_arg, b_carry = Buf(), Buf(), Buf(), Buf()
                      b_spf = [Buf(), Buf()]
                      shi = [sbt(st, "shi%d" % i, [128, 512], BF16) for i in range(2)]
                      slo = [sbt(st, "slo%d" % i, [128, 512], BF16) for i in range(2)]
                      b_shi, b_slo = [Buf(), Buf()], [Buf(), Buf()]
                      pT = [sbt(st, "pT%d" % i, [128, 4, 128], BF16) for i in range(2)]
                      b_pT = [Buf(), Buf()]
                      yst = sbt(st, "yst", [64, 4, 128], BF16)
                      b_yst = Buf()
                      LT4f = LT4.rearrange("p a b -> p (a b)")
                      pA, pB = pw[:, 2, :], pw[:, 3, :]
                      b_pA, b_pB = b_pw[2], b_pw[3]

                      def c_front(i, j, k):
                          qk4(k, kt, qt, b_k, b_q, i, j)
                          P.op("act", lambda e: e.activation(out=v3(ef[:]), in_=Sk[k], func=AF.Exp, scale=0.125), reads=b_Sk[k], writes=[b_ef])
                          P.op("act", lambda e: e.activation(out=spf[k][:], in_=ef[:], func=AF.Ln, bias=1.0), reads=[b_ef], writes=[b_spf[k]])
                          if j == i:
                              P.op("dve", lambda e: e.tensor_tensor(out=spf[k][:], in0=spf[k][:], in1=LT4f, op=ALU.mult), reads=[b_spf[k], b_c], writes=[b_spf[k]])
                          P.op("dve", lambda e: e.tensor_copy(out=shi[k][:], in_=spf[k][:]), reads=[b_spf[k]], writes=[b_shi[k]])
                          P.op("pool", lambda e: e.tensor_tensor(out=slo[k][:], in0=spf[k][:], in1=shi[k][:], op=ALU.subtract), reads=[b_spf[k], b_shi[k]], writes=[b_slo[k]])

                      def c_back(i, j, k):
                          P.op("pe", lambda e: e.matmul(pA, lhsT=Ugeb, rhs=shi[k][:], start=True, stop=False), reads=[b_shi[k], b_c], writes=[b_pA])
                          P.op("pe", lambda e: e.matmul(pA, lhsT=Ugeb, rhs=slo[k][:], start=False, stop=True), reads=[b_slo[k], b_c], writes=[b_pA])
                          if j > 0:
                              P.op("pe", lambda e: e.matmul(pB, lhsT=onesb, rhs=shi[k][:], start=True, stop=False), reads=[b_shi[k], b_c], writes=[b_pB])
                              P.op("pe", lambda e: e.matmul(pB, lhsT=onesb, rhs=slo[k][:], start=False, stop=True), reads=[b_slo[k], b_c], writes=[b_pB])
                          P.op("dve", lambda e: e.tensor_tensor(out=tt[:], in0=pA, in1=carry[:], op=ALU.add), reads=[b_pA, b_carry], writes=[b_tt])
                          P.op("dve", lambda e: e.scalar_tensor_tensor(out=v3(arg[:]), in0=Sk[k], scalar=0.125, in1=v3(tt[:]), op0=ALU.mult, op1=ALU.subtract),
                               reads=b_Sk[k] + [b_tt], writes=[b_arg])
                          if j > 0:
                              P.op("dve", lambda e: e.tensor_tensor(out=carry[:], in0=pB, in1=carry[:], op=ALU.add), reads=[b_pB, b_carry], writes=[b_carry])
                          P.op("act", lambda e: e.activation(out=flat(pT[k]), in_=arg[:], func=AF.Exp), reads=[b_arg], writes=[b_pT[k]])
                          if j == i:
                              P.op("pool", lambda e: e.tensor_tensor(out=pT[k][:], in0=pT[k][:], in1=LT4, op=ALU.mult), reads=[b_pT[k], b_c], writes=[b_pT[k]])

                      def c_av(i, j, k):
                          av4(pT[k], b_pT[k], va, b_v, j, j == i, j == 0, 64)

                      for i in range(NB):
                          P.op("pool", lambda e: e.memset(carry[:], 0.0), writes=[b_carry])
                          pairs = list(range(i, -1, -1))
                          m_ = len(pairs)
                          c_front(i, pairs[0], 0)
                          if m_ > 1:
                              c_front(i, pairs[1], 1)
                          c_back(i, pairs[0], 0)
                          for n, j in enumerate(pairs):
                              if n + 2 < m_:
                                  c_front(i, pairs[n + 2], (n + 2) % 2)
                              if n + 1 < m_:
                                  c_back(i, pairs[n + 1], (n + 1) % 2)
                              c_av(i, j, n % 2)
                          P.op("act", lambda e: e.activation(out=yst[0:64, :, :], in_=po4[0:64, :, :], func=AF.Copy), reads=[b_po], writes=[b_yst])
                          P.dma(YT[3:5, :, i * 128:(i + 1) * 128].rearrange("c (h e) t -> e (c h) t", e=64), yst[0:64, :, :], reads=[b_yst], writes=[b_YT])
                P.barrier()

                with ExitStack() as st:
                  if ATT_STOP >= 3 and ATT_ONLY in (-1, 3):
                      kt, qt_full, va, b_k, b_q_full, b_v = load_kqv(st, 6, 8, 384, 4, noq=True)
                      qtb = [sbt(st, "qtb%d" % i, [128, 2, 128], BF16) for i in range(2)]
                      qib = [sbt(st, "qib%d" % i, [128, 2, 128], BF16) for i in range(2)]
                      b_qtb = [Buf(), Buf()]
                      kit = sbt(st, "kit", [128, T], BF16)
                      wi = sbt(st, "wi", [128, NB, 4])
                      b_qi = Buf()
                      P.dma(kit[:], QKT[12], reads=[b_QKT], writes=[b_qi])
                      P.dma(wi[:], IWT.rearrange("(k p) h -> p k h", p=128), reads=[b_IWT], writes=[b_qi])
                      sc = sbt(st, "sc", [128, T])
                      mk = [sbt(st, "mk%d" % i, [128, T], BF16) for i in range(2)]
                      rl = [sbt(st, "rl%d" % i, [128, 512]) for i in range(2)]
                      b_sc = Buf()
                      b_mk = [Buf(), Buf()]
                      b_jD = [Buf(), Buf()]
                      b_jA = [Buf(), Buf()]
                      b_rl = [Buf(), Buf()]
                      sm_ = sbt(st, "smallb", [128, 12])
                      b_sm = Buf()
                      b_sA = Buf()
                      b_nm = Buf()
                      lo, hi, w0, nmid, cnt, ge, mid, tcb = [sm_[:, a:a + 1] for a in range(8)]
                      sA = sm_[:, 8:9]
                      pT = [sbt(st, "pT%d" % i, [128, 4, 128], BF16) for i in range(2)]
                      b_pT = [Buf(), Buf()]
                      rrow = sbt(st, "rrow", [128, 4, 128])
                      rb = sbt(st, "rb", [64, 4, 128])
                      yst = sbt(st, "yst", [64, 4, 128], BF16)
                      b_rr, b_rb, b_yst = Buf(), Buf(), Buf()
                      pm = [pw[:, 2, :].bitcast(BF16), pw[:, 3, :].bitcast(BF16)]
                      b_pm = [b_pw[2], b_pw[3]]

                      def b_front(qb_, m, j, k):
                          for h in range(4):
                              P.op("pe", lambda e: e.transpose(out=pm[k][:, h * 128:(h + 1) * 128], in_=mk[m][:, j * 128:(j + 1) * 128], identity=identb),
                                   reads=[b_mk[m], b_c], writes=[b_pm[k]])
                          qk4(k, kt, qtb[qb_], b_k, b_qtb[qb_], 0, j)
                          P.op("act", lambda e: e.activation(out=v3(flat(pT[k])), in_=Sk[k], func=AF.Exp, scale=0.125), reads=b_Sk[k], writes=[b_pT[k]])

                      def b_back(i, j, k):
                          P.op("dve", lambda e: e.tensor_tensor(out=flat(pT[k]), in0=flat(pT[k]), in1=pm[k][:, 0:512], op=ALU.mult),
                               reads=[b_pT[k], b_pm[k]], writes=[b_pT[k]])
                          av4(pT[k], b_pT[k], va, b_v, j, j == 0, j == i, 65)

                      for i in range(NB):
                          n = 128 * (i + 1)
                          nchunk = (n + 511) // 512
                          qb_ = i % 2
                          m = i % 2
                          P.dma(qtb[qb_][:], QKT[6:8, :, i * 128:(i + 1) * 128].rearrange("c p t -> p c t"), reads=[b_QKT], writes=[b_qtb[qb_]])
                          P.dma(qib[qb_][:], QKT[10:12, :, i * 128:(i + 1) * 128].rearrange("c p t -> p c t"), reads=[b_QKT], writes=[b_qtb[qb_]])
                          for c in range(nchunk):
                              w = min(512, n - 512 * c)
                              for h in range(4):
                                  k = h % 2
                                  P.op("pe", lambda e: e.matmul(pq[k][:, 0:w], lhsT=qib[qb_][hr(h), h // 2, :], rhs=kit[hr(h), c * 512:c * 512 + w], start=True, stop=True),
                                       reads=[b_qi, b_qtb[qb_]], writes=[b_pq[k]])
                                  P.op("act", lambda e: e.activation(out=rl[k][:, 0:w], in_=pq[k][:, 0:w], func=AF.Relu), reads=[b_pq[k]], writes=[b_rl[k]])
                                  if h == 0:
                                      P.op("dve", lambda e: e.tensor_scalar(out=sc[:, c * 512:c * 512 + w], in0=rl[k][:, 0:w], scalar1=wi[:, i, 0:1], scalar2=None, op0=ALU.mult),
                                           reads=[b_rl[k], b_qi], writes=[b_sc])
                                  else:
                                      P.op("dve", lambda e: e.scalar_tensor_tensor(out=sc[:, c * 512:c * 512 + w], in0=rl[k][:, 0:w], scalar=wi[:, i, h:h + 1], in1=sc[:, c * 512:c * 512 + w], op0=ALU.mult, op1=ALU.add),
                                           reads=[b_rl[k], b_qi, b_sc], writes=[b_sc])
                          P.op("dve", lambda e: e.tensor_reduce(out=lo, in_=sc[:, 0:n], axis=AX.X, op=ALU.min), reads=[b_sc], writes=[b_sm])
                          P.op("dve", lambda e: e.tensor_reduce(out=hi, in_=sc[:, 0:n], axis=AX.X, op=ALU.max), reads=[b_sc], writes=[b_sm])
                          P.op("dve", lambda e: e.tensor_tensor(out=sc[:, n - 128:n], in0=sc[:, n - 128:n], in1=negmask, op=ALU.add), reads=[b_sc, b_c], writes=[b_sc])
                          P.op("dve", lambda e: e.tensor_tensor(out=w0, in0=hi, in1=lo, op=ALU.subtract), reads=[b_sm], writes=[b_sm])
                          P.op("dve", lambda e: e.memset(mk[m][:, 0:2], 0.0), writes=[b_mk[m], b_jD[m], b_jA[m]])
                          nd = max(0, ((int(0.457 * n) - 924) // 64) * 64)
                          na = n - nd
                          for it in range(1, NITER + 1):
                              f = 2.0 ** (-it)
                              P.op("dve", lambda e: e.scalar_tensor_tensor(out=mid, in0=w0, scalar=f, in1=lo, op0=ALU.mult, op1=ALU.add), reads=[b_sm], writes=[b_nm])
                              P.op("act", lambda e: e.activation(out=mk[m][:, nd:n], in_=sc[:, nd:n], func=AF.Sign, bias=mid, scale=-1.0, accum_out=sA),
                                   reads=[b_sc, b_nm], writes=[b_jA[m], b_sA])
                              if nd > 0:
                                  P.op("dve", lambda e: e.tensor_scalar(out=mk[m][:, 0:nd], in0=sc[:, 0:nd], scalar1=mid, scalar2=None, op0=ALU.is_ge, op1=ALU.add, accum_out=cnt),
                                       reads=[b_sc, b_nm], writes=[b_jD[m], b_sm])
                                  P.op("dve", lambda e: e.scalar_tensor_tensor(out=cnt, in0=sA, scalar=-0.5, in1=cnt, op0=ALU.mult, op1=ALU.add), reads=[b_sA, b_sm], writes=[b_sm])
                                  P.op("dve", lambda e: e.tensor_scalar(out=ge, in0=cnt, scalar1=TOPK - 0.5 - na / 2.0, scalar2=f, op0=ALU.is_ge, op1=ALU.mult), reads=[b_sm], writes=[b_sm])
                              else:
                                  P.op("dve", lambda e: e.tensor_scalar(out=ge, in0=sA, scalar1=float(n - 2 * TOPK + 1), scalar2=f, op0=ALU.is_le, op1=ALU.mult), reads=[b_sA], writes=[b_sm])
                              P.op("dve", lambda e: e.scalar_tensor_tensor(out=lo, in0=ge, scalar=w0, in1=lo, op0=ALU.mult, op1=ALU.add), reads=[b_sm], writes=[b_sm])
                          P.op("dve", lambda e: e.tensor_scalar(out=mk[m][:, 0:n], in0=sc[:, 0:n], scalar1=lo, scalar2=None, op0=ALU.is_ge), reads=[b_sc, b_sm], writes=[b_mk[m], b_jD[m], b_jA[m]])
                          b_front(qb_, m, 0, 0)
                          for j in range(i + 1):
                              if j < i:
                                  b_front(qb_, m, j + 1, (j + 1) % 2)
                              b_back(i, j, j % 2)
                          normalize_store(rrow, b_rr, rb, b_rb, yst, b_yst, 1, i)
                P.barrier()


                with ExitStack() as st:
                  if ATT_STOP >= 4 and ATT_ONLY in (-1, 4):
                      accA = sbt(st, "accA", [65, 2, T])
                      b_acc = Buf()
                      kt = sbt(st, "kt", [128, T], BF16)
                      qt = sbt(st, "qt", [128, T], BF16)
                      va = sbt(st, "va", [128, NB, 2, 65], BF16)
                      b_k, b_q, b_v = Buf(), Buf(), Buf()
                      pT = [sbt(st, "pT%d" % i, [128, 4, 128], BF16) for i in range(2)]
                      b_pT = [Buf(), Buf()]
                      P.op("pool", lambda e: e.memset(va[:, :, :, 64:65], 1.0), writes=[b_v])
                      pc = 0
                      for g, d in enumerate((1, 4, 16)):
                          nbs = NB // d
                          P.dma(kt[:], QKT[3 + g], reads=[b_QKT], writes=[b_k])
                          P.dma(qt[:], QKT[g], reads=[b_QKT], writes=[b_q])
                          vsrc = VTM[:, g * 128:(g + 1) * 128].rearrange("(k p r) (h e) -> r p k h e", p=128, r=d, e=64)
                          for r_ in range(d):
                              for hh in range(2):
                                  P.dma(va[:, r_ * nbs:(r_ + 1) * nbs, hh, 0:64], vsrc[r_, :, :, hh, :], reads=[b_VTM], writes=[b_v])
                          for r_ in range(d):
                              for kb in range(nbs):
                                  k = pc % 2
                                  pc += 1

                                  def tok(kk):
                                      base = r_ + d * 128 * kk
                                      return slice(base, base + d * 127 + 1, d) if d > 1 else slice(base, base + 128)
                                  for hh in range(2):
                                      for wch in range(2):
                                          if kb == 0 and wch == 0:
                                              continue
                                          P.op("pe", lambda e: e.matmul(pqq[:, hh, wch * 128:(wch + 1) * 128], lhsT=kt[hr(hh), tok(kb - 1 + wch)], rhs=qt[hr(hh), tok(kb)], start=True, stop=True),
                                               reads=[b_k, b_q], writes=[b_pq[hh]])
                                  if kb == 0:
                                      for hh in range(2):
                                          P.op("act", lambda e: e.activation(out=pT[k][:, hh * 2 + 1, :], in_=pqq[:, hh, 128:256], func=AF.Exp, scale=0.125), reads=[b_pq[hh]], writes=[b_pT[k]])
                                          P.op("pool", lambda e: e.tensor_tensor(out=pT[k][:, hh * 2 + 1, :], in0=pT[k][:, hh * 2 + 1, :], in1=MA4[:, hh * 2 + 1, :], op=ALU.mult), reads=[b_pT[k], b_c], writes=[b_pT[k]])
                                  else:
                                      P.op("act", lambda e: e.activation(out=v3(pT[k][:].rearrange("p a b -> p (a b)")), in_=SS, func=AF.Exp, scale=0.125), reads=[b_pq[0], b_pq[1]], writes=[b_pT[k]])
                                      P.op("pool", lambda e: e.tensor_tensor(out=pT[k][:], in0=pT[k][:], in1=MA4, op=ALU.mult), reads=[b_pT[k], b_c], writes=[b_pT[k]])
                                  for hh in range(2):
                                      if kb > 0:
                                          P.op("pe", lambda e: e.matmul(pw[0:65, hh, 0:128], lhsT=va[:, r_ * nbs + kb - 1, hh, :], rhs=pT[k][:, hh * 2, :], start=True, stop=False),
                                               reads=[b_pT[k], b_v], writes=[b_pw[hh]])
                                      P.op("pe", lambda e: e.matmul(pw[0:65, hh, 0:128], lhsT=va[:, r_ * nbs + kb, hh, :], rhs=pT[k][:, hh * 2 + 1, :], start=(kb == 0), stop=True),
                                           reads=[b_pT[k], b_v], writes=[b_pw[hh]])
                                  dst = accA[0:65, :, tok(kb)]
                                  if g == 0:
                                      P.op("dve", lambda e: e.tensor_copy(out=dst, in_=pw[0:65, 0:2, 0:128]), reads=[b_pw[0], b_pw[1]], writes=[b_acc])
                                  else:
                                      P.op("dve", lambda e: e.tensor_tensor(out=dst, in0=pw[0:65, 0:2, 0:128], in1=dst, op=ALU.add), reads=[b_pw[0], b_pw[1], b_acc], writes=[b_acc])
                      rrow = sbt(st, "rrowA", [128, 2, 256])
                      ysa = sbt(st, "ysa", [64, 2, 256], BF16)
                      b_rr, b_ys = Buf(), Buf()
                      for ti in range(NT):
                          t0 = ti * 256
                          P.op("dve", lambda e: e.reciprocal(out=rrow[64:65, :, :], in_=accA[64:65, :, t0:t0 + 256]), reads=[b_acc], writes=[b_rr])
                          P.op("pe", lambda e: e.matmul(pq[2][0:64, :], lhsT=ones64[64:65, :], rhs=rrow[64:65, :, :].rearrange("p a b -> p (a b)"), start=True, stop=True),
                               reads=[b_rr, b_o64], writes=[b_pq[2]])
                          P.op("dve", lambda e: e.tensor_tensor(out=ysa[:], in0=accA[0:64, :, t0:t0 + 256], in1=pq[2][0:64, :].rearrange("p (a b) -> p a b", a=2), op=ALU.mult),
                               reads=[b_acc, b_pq[2]], writes=[b_ys])
                          P.dma(YT[0, :, t0:t0 + 256].rearrange("(h e) t -> e h t", e=64), ysa[:], reads=[b_ys], writes=[b_YT])
            P.barrier()

        def post_phase(l, xin, b_xin, xout, b_xout):
            with ExitStack() as st:
                wbr = sbt(st, "wbr", [128, 7, D], BF16)
                wo = sbt(st, "wo", [128, 8, D], BF16)
                b_wbr, b_wo, b_ms = Buf(), Buf(), Buf()
                load_w_bf16(wbr, w_branch[l], 7, b_wbr, piece=1024)
                load_w_bf16(wo, w_o[l], 8, b_wo, piece=1024)
                gp, lng, lnb = [sbt(st, "m%d" % i, [128, D]) for i in range(3)]
                load_bcast(gp, MODP[l, 5 * D:6 * D], b_ms)
                P.dma(lng[:], ln_g[l, 1, :].partition_broadcast(128), writes=[b_ms])
                P.dma(lnb[:], ln_b[l, 1, :].partition_broadcast(128), writes=[b_ms])
                xs = [sbt(st, "xs%d" % i, [128, 2, D]) for i in range(2)]
                b_xs = [Buf(), Buf()]
                yt = [sbt(st, "yt%d" % i, [128, 7, 256], BF16) for i in range(2)]
                gt = [sbt(st, "gt%d" % i, [128, 32, 256], BF16) for i in range(2)]
                b_yt, b_gt = [Buf(), Buf()], [Buf(), Buf()]
                mg = sbt(st, "mg", [128, 8, 256])
                mgb = sbt(st, "mgb", [128, 8, 256], BF16)
                tmp = [sbt(st, "tmp%d" % i, [128, 256]) for i in range(2)]
                b_mg, b_mgb = Buf(), Buf()
                b_tmp = [Buf(), Buf()]
                xo = sbt(st, "xo", [128, 2, D])
                b_xo = [Buf(), Buf()]
                t1 = sbt(st, "t1", [128, D])
                r = sbt(st, "r", [128, D])
                small = sbt(st, "small", [128, 32])
                b_t1, b_r, b_small = Buf(), Buf(), Buf()
                kch = ((0, 1), (1, 3), (3, 5), (5, 7))
                pc = 0
                for ti in range(NT):
                    t0 = ti * 256
                    k2 = ti % 2
                    P.dma(xs[k2][:], xin[t0:t0 + 256, :].rearrange("(j p) d -> p j d", p=128), reads=[b_xin], writes=[b_xs[k2]])
                    P.dma(yt[k2][:], YT[:, :, t0:t0 + 256].rearrange("c p t -> p c t"), reads=[b_YT], writes=[b_yt[k2]])
                    P.dma(gt[k2][:], GT[:, :, t0:t0 + 256].rearrange("c p t -> p c t"), reads=[b_GT], writes=[b_gt[k2]])
                    for fo in range(8):
                        for bi in range(4):
                            k = pc % 4
                            pc += 1
                            a, b = kch[bi]
                            for kc in range(a, b):
                                P.op("pe", lambda e: e.matmul(pq[k][:, 0:256], lhsT=wbr[:, kc, fo * 128:(fo + 1) * 128], rhs=yt[k2][:, kc, :], start=(kc == a), stop=(kc == b - 1)),
                                     reads=[b_wbr, b_yt[k2]], writes=[b_pq[k]])
                            if bi == 0:
                                P.op("dve", lambda e: e.tensor_tensor(out=mg[:, fo, :], in0=pq[k][:, 0:256], in1=gt[k2][:, bi * 8 + fo, :], op=ALU.mult), reads=[b_pq[k], b_gt[k2]], writes=[b_mg])
                            else:
                                kk = pc % 2
                                P.op("dve", lambda e: e.tensor_tensor(out=tmp[kk][:], in0=pq[k][:, 0:256], in1=gt[k2][:, bi * 8 + fo, :], op=ALU.mult), reads=[b_pq[k], b_gt[k2]], writes=[b_tmp[kk]])
                                if bi < 3:
                                    P.op("pool", lambda e: e.tensor_tensor(out=mg[:, fo, :], in0=mg[:, fo, :], in1=tmp[kk][:], op=ALU.add), reads=[b_mg, b_tmp[kk]], writes=[b_mg])
                                else:
                                    P.op("pool", lambda e: e.tensor_tensor(out=mgb[:, fo, :], in0=mg[:, fo, :], in1=tmp[kk][:], op=ALU.add), reads=[b_mg, b_tmp[kk]], writes=[b_mgb])
                    for j in range(2):
                        for nh in range(2):
                            for kc in range(8):
                                P.op("pe", lambda e: e.matmul(pw[:, 2 * j + nh, :], lhsT=mgb[:, kc, j * 128:(j + 1) * 128], rhs=wo[:, kc, nh * 512:(nh + 1) * 512], start=(kc == 0), stop=(kc == 7)),
                                     reads=[b_mgb, b_wo], writes=[b_pw[2 * j + nh]])
                    for j in range(2):
                        deepnorm_ln(j, xs[k2], b_xs[k2], gp, lng, lnb, b_ms, t1, r, b_t1, b_r, small, b_small, xo, b_xo[j])
                    P.dma(xout[t0:t0 + 256, :].rearrange("(j p) d -> p j d", p=128), xo[:], reads=[b_xo[0], b_xo[1]], writes=[b_xout])
            P.barrier()

        b_xin0 = Buf()
        b_y = Buf()
        dbg_stage = dbg if isinstance(dbg, int) and not isinstance(dbg, bool) else 99
        stages = 0
        cur, b_cur = x_in, b_xin0
        for l in range(2):
            last = (l == 1)
            if dbg_stage >= 1:
                ffn_phase(l, 0, 0, cur, b_cur, S1, b_S1)
            if dbg_stage >= 2:
                inproj_phase(l, S1, b_S1)
            if dbg_stage >= 3:
                attention_phase(l)
            if dbg_stage >= 4:
                post_phase(l, S1, b_S1, S2, b_S2)
            if dbg_stage >= 5:
                ffn_phase(l, 1, 2, S2, b_S2, y_out if last else S1, b_y if last else b_S1)
            cur, b_cur = S1, b_S1
            if dbg_stage < 99:
                break
        P.finish()
        build.stats = (P.ninst, P.nwait)
    return nc


def _consts(T):
    p = np.arange(128)
    cf = np.zeros((128, 5, 128), np.float32)
    cf[:, 0, :] = np.eye(128)
    cf[:, 1, :] = (p[:, None] >= p[None, :])
    cf[:, 2, :] = 1.0
    cf[0, 3, :] = 1.0
    cf[:, 4, :] = np.where(p[None, :] > p[:, None], -1e30, 0.0)
    cb = np.zeros((128, 15, 128), np.float32)
    cb[:, 13, :] = (p[:, None] >= p[None, :])
    cb[:, 14, :] = 1.0
    cb[:, 0, :] = np.eye(128)
    le = (p[:, None] <= p[None, :]).astype(np.float32)
    lt = (p[:, None] < p[None, :]).astype(np.float32)
    gev = (p[:, None] >= p[None, :]).astype(np.float32)
    for h in range(4):
        cb[:, 1 + h, :] = le
        cb[:, 5 + h, :] = lt
    for hh in range(2):
        cb[:, 9 + hh * 2 + 0, :] = gev
        cb[:, 9 + hh * 2 + 1, :] = le
    half = 8
    inv = 500000.0 ** (-(np.arange(half, dtype=np.float32) * (2.0 / 16)))
    ang = np.arange(T, dtype=np.float32)[None, :] * inv[:, None].astype(np.float32)
    cos, sin = np.cos(ang).astype(np.float32), np.sin(ang).astype(np.float32)
    C = np.ones((64, T), np.float32)
    S = np.zeros((64, T), np.float32)
    C[0:8], C[8:16] = cos, cos
    S[0:8], S[8:16] = -sin, sin
    rope = np.stack([np.concatenate([C, C], 0), np.concatenate([S, S], 0)], 0)
    return cf, cb.astype(ml_dtypes.bfloat16), rope


def _wm_cols():
    ar = np.arange
    chunks = []
    for g in range(3):
        chunks.append(ar(g * 128, (g + 1) * 128))
    for g in range(3):
        chunks.append(384 + ar(g * 128, (g + 1) * 128))
    for g in range(2):
        chunks.append(OFF_B + ar(g * 128, (g + 1) * 128))
    for g in range(2):
        chunks.append(OFF_B + 256 + ar(g * 128, (g + 1) * 128))
    for g in range(2):
        chunks.append(OFF_IQ + ar(g * 128, (g + 1) * 128))
    chunks.append(np.concatenate([OFF_IK + ar(64), OFF_IK + ar(64)]))
    for base in (OFF_C, OFF_C + 256, OFF_D, OFF_D + 256):
        for g in range(2):
            chunks.append(base + ar(g * 128, (g + 1) * 128))
    perm64 = np.arange(64)
    perm64[0:8] = np.arange(8, 16)
    perm64[8:16] = np.arange(0, 8)
    perm128 = np.concatenate([perm64, 64 + perm64])
    for ch in range(NROPE):
        chunks.append(chunks[ch][perm128])
    cols = np.concatenate(chunks + [OFF_FG + ar(4), 768 + ar(384), OFF_B + 512 + ar(256), OFF_C + 512 + ar(256),
                                    OFF_D + 512 + ar(256), OFF_IW + ar(4)])
    assert cols.shape[0] == NC1
    return cols


_NC_CACHE = {}


def _run(T, per_core, dbg=False):
    key = (T, dbg)
    if key not in _NC_CACHE:
        _NC_CACHE[key] = build(T, dbg)
    nc = _NC_CACHE[key]
    res = run_bass_kernel_spmd(nc, per_core, core_ids=list(range(len(per_core))))
    return res


def make_in_maps(T, x, c, ada_w, ada_b, ln_g, ln_b, ffn_w_in, ffn_w_out, mix_w_in, mix_b_gate, mix_b_forget,
                 mix_w_branch, mix_w_out):
    f = lambda a: np.ascontiguousarray(np.asarray(a, dtype=np.float32))
    cf, cb, rope = _consts(T)
    cols = _wm_cols()
    mw = np.asarray(mix_w_in, dtype=np.float32)
    shared = {
        "ada_w": f(ada_w), "ada_b": f(ada_b), "ln_g": f(ln_g), "ln_b": f(ln_b),
        "ffn_w_in": f(ffn_w_in), "ffn_w_out": f(ffn_w_out),
        "wm1": f(mw[:, :, cols]), "wgate": f(mw[:, :, OFF_GATE:OFF_GATE + 4096]),
        "b_gate": f(mix_b_gate), "b_forget": f(mix_b_forget), "w_branch": f(mix_w_branch), "w_o": f(mix_w_out),
        "cf": cf, "cb": cb, "rope": rope,
    }
    xs = np.asarray(x, dtype=np.float32)
    cs = np.asarray(c, dtype=np.float32)
    maps = []
    for b in range(xs.shape[0]):
        m = dict(shared)
        m["x"] = f(xs[b])
        m["c"] = f(cs[b])
        maps.append(m)
    return maps


def kernel(x, c, ada_w, ada_b, ln_g, ln_b, ffn_w_in, ffn_w_out, mix_w_in, mix_b_gate, mix_b_forget,
           mix_w_branch, mix_w_out):
    B, T, _ = np.asarray(x).shape
    maps = make_in_maps(T, x, c, ada_w, ada_b, ln_g, ln_b, ffn_w_in, ffn_w_out, mix_w_in, mix_b_gate,
                        mix_b_forget, mix_w_branch, mix_w_out)
    per_core = [maps[i] for i in range(B)]
    res = _run(T, per_core)
    out = np.stack([np.asarray(res.results[b]["y"], dtype=np.float32) for b in range(B)], axis=0)
    return out
```

```python
import numpy as np
import ml_dtypes
import concourse.bass as bass
import concourse.mybir as mybir
from concourse.bass_utils import run_bass_kernel_spmd
from contextlib import ExitStack

F32 = mybir.dt.float32
BF16 = mybir.dt.bfloat16
AF = mybir.ActivationFunctionType
ALU = mybir.AluOpType
AX = mybir.AxisListType

D = 1024
DFF = 2816
ALPHA = 4.0 ** 0.25
LN_EPS = 1e-5
NITER = 18
TOPK = 256
NDMA = 24
FFN_STOP = 9
ATT_STOP = 9
ATT_ONLY = -1
D_STOP = 9

A_QKV_W, B_QKV_W, IDX_Q_W, IDX_K_W, IDX_W_W, C_QKV_W, D_QKV_W, FG_W = 1152, 768, 256, 64, 4, 768, 768, 4
OFF_B = A_QKV_W
OFF_IQ = OFF_B + B_QKV_W
OFF_IK = OFF_IQ + IDX_Q_W
OFF_IW = OFF_IK + IDX_K_W
OFF_C = OFF_IW + IDX_W_W
OFF_D = OFF_C + C_QKV_W
OFF_FG = OFF_D + D_QKV_W
OFF_GATE = OFF_FG + FG_W

NFM = 34
NQK = 21
NROPE = 13
FGCOL = NFM * 128
TMCOL = FGCOL + 4
NTM = 1156
NC1 = TMCOL + NTM


class Buf:
    __slots__ = ("name", "w", "r")

    def __init__(self, name=""):
        self.name = name
        self.w = None
        self.r = {}


class Prog:
    def __init__(self, nc, es, sync_same=("act", "dve", "pool")):
        self.nc = nc
        self.eng = {"pe": nc.tensor, "act": nc.scalar, "dve": nc.vector, "pool": nc.gpsimd, "sp": nc.sync}
        self.sem = {k: es.enter_context(nc.semaphore("s_" + k)) for k in ("pe", "act", "dve", "pool")}
        self.cnt = {k: 0 for k in self.sem}
        self.seen = {k: {} for k in self.eng}
        self.dsem = [es.enter_context(nc.semaphore("d%d" % i)) for i in range(NDMA)]
        self.dcnt = [0] * NDMA
        self.drr = 0
        self.sync_same = set(sync_same)
        self.ninst = 0
        self.nwait = 0

    def _semh(self, key):
        return self.sem[key[1]] if key[0] == "e" else self.dsem[key[1]]

    def _deps(self, reads, writes):
        deps = {}
        for b in reads:
            if b.w is not None:
                k, v = b.w
                if deps.get(k, 0) < v:
                    deps[k] = v
        for b in writes:
            if b.w is not None:
                k, v = b.w
                if deps.get(k, 0) < v:
                    deps[k] = v
            for k, v in b.r.items():
                if deps.get(k, 0) < v:
                    deps[k] = v
        return deps

    def _waits(self, e, deps):
        seen = self.seen[e]
        for k, v in deps.items():
            if seen.get(k, 0) >= v:
                continue
            if k == ("e", e) and e not in self.sync_same:
                continue
            self.eng[e].wait_ge(self._semh(k), v)
            seen[k] = v
            self.nwait += 1

    def _mark(self, key, val, reads, writes):
        for b in reads:
            if b.r.get(key, 0) < val:
                b.r[key] = val
        for b in writes:
            b.w = (key, val)
            b.r = {}

    def op(self, e, fn, reads=(), writes=()):
        self._waits(e, self._deps(reads, writes))
        fn(self.eng[e]).then_inc(self.sem[e], 1)
        self.cnt[e] += 1
        self.ninst += 1
        self._mark(("e", e), self.cnt[e], reads, writes)

    def dma(self, out, in_, reads=(), writes=(), q="sp", **kw):
        i = self.drr
        self.drr = (i + 1) % NDMA
        deps = self._deps(reads, writes)
        if self.dcnt[i] > 0:
            deps[("d", i)] = self.dcnt[i]
        self._waits(q, deps)
        self.dcnt[i] += 16
        self.eng[q].dma_start(out=out, in_=in_, **kw).then_inc(self.dsem[i], 16)
        self.ninst += 1
        self._mark(("d", i), self.dcnt[i], reads, writes)

    def _all(self):
        deps = {("d", i): c for i, c in enumerate(self.dcnt) if c > 0}
        for k, c in self.cnt.items():
            if c > 0:
                deps[("e", k)] = c
        return deps

    def barrier(self):
        deps = self._all()
        for e in ("pe", "act", "dve", "pool", "sp"):
            self._waits(e, dict(deps))

    def finish(self):
        self._waits("sp", self._all())


def hr(h):
    return slice((h % 2) * 64, (h % 2) * 64 + 64)


def sl(h):
    return (h % 2) * 2 + h // 2


def v3(ap):
    return ap.rearrange("p (a b) -> p a b", a=2)


def build(T, dbg=False):
    NB = T // 128
    NT = T // 256
    nc = bass.Bass("TRN2", target_bir_lowering=False)

    def din(name, shape, dt=F32):
        return nc.dram_tensor(name, list(shape), dt, kind="ExternalInput").ap()

    def dscr(name, shape, dt=F32):
        if dbg:
            return nc.dram_tensor(name, list(shape), dt, kind="ExternalOutput").ap()
        return nc.dram_tensor(name, list(shape), dt).ap()

    x_in = din("x", [T, D])
    c_in = din("c", [D])
    ada_w = din("ada_w", [2, D, 9 * D])
    ada_b = din("ada_b", [2, 9 * D])
    ln_g = din("ln_g", [2, 3, D])
    ln_b = din("ln_b", [2, 3, D])
    w_in = din("ffn_w_in", [2, 2, D, 2 * DFF])
    w_out = din("ffn_w_out", [2, 2, DFF, D])
    wm1 = din("wm1", [2, D, NC1])
    wgate = din("wgate", [2, D, 4096])
    b_gate = din("b_gate", [2, 4096])
    b_forget = din("b_forget", [2, 4])
    w_branch = din("w_branch", [2, 896, D])
    w_o = din("w_o", [2, D, D])
    cf_in = din("cf", [128, 5, 128])
    cb_in = din("cb", [128, 15, 128], BF16)
    rope_in = din("rope", [2, 128, T])
    y_out = nc.dram_tensor("y", [T, D], F32, kind="ExternalOutput").ap()

    S1 = dscr("S1", [T, D])
    S2 = dscr("S2", [T, D])
    MODP = dscr("MODP", [2, 9 * D])
    QKT = dscr("QKT", [NQK, 128, T], BF16)
    FGT = dscr("FGT", [4, T])
    VTM = dscr("VTM", [T, 1152], BF16)
    IWT = dscr("IWT", [T, 4])
    GT = dscr("GT", [32, 128, T], BF16)
    YT = dscr("YT", [7, 128, T], BF16)
    b_S1, b_S2, b_MODP, b_QKT, b_FGT, b_VTM, b_IWT, b_GT, b_YT = [Buf() for _ in range(9)]

    with ExitStack() as es:
        P = Prog(nc, es)

        uid = [0]

        def sbt(st, name, shape, dt=F32):
            uid[0] += 1
            return st.enter_context(nc.sbuf_tensor("%s_%d" % (name, uid[0]), list(shape), dt))

        pqq = es.enter_context(nc.psum_tensor("pqq", [128, 4, 512], F32))
        pq = [pqq[:, i, :] for i in range(4)]
        SS = pqq[:, 0:2, 0:256]
        pw = es.enter_context(nc.psum_tensor("pw", [128, 4, 512], F32))
        b_pq = [Buf() for _ in range(4)]
        b_pw = [Buf() for _ in range(4)]

        cf = sbt(es, "cf", [128, 5, 128])
        cb = sbt(es, "cb", [128, 15, 128], BF16)
        b_c = Buf()
        P.dma(cf[:], cf_in, writes=[b_c])
        P.dma(cb[:], cb_in, writes=[b_c])
        ident, Uge, ones, E0, negmask = [cf[:, i, :] for i in range(5)]
        identb = cb[:, 0, :]
        LE4 = cb[:, 1:5, :]
        LT4 = cb[:, 5:9, :]
        MA4 = cb[:, 9:13, :]
        Ugeb = cb[:, 13, :]
        onesb = cb[:, 14, :]

        with ExitStack() as st:
            condT = sbt(st, "condT", [128, 8])
            modrow = sbt(st, "modrow", [1, 9 * D])
            brow = sbt(st, "brow", [1, 9 * D])
            wa = [sbt(st, "wa%d" % i, [128, 8, 512]) for i in range(2)]
            b_cond, b_mod, b_brow = Buf(), Buf(), Buf()
            b_wa = [Buf(), Buf()]
            P.dma(condT[:], c_in.rearrange("(c p) -> p c", p=128), writes=[b_cond], allow_slow_non_contiguous=True)
            P.op("act", lambda e: e.activation(out=condT[:], in_=condT[:], func=AF.Silu), reads=[b_cond], writes=[b_cond])
            for l in range(2):
                P.dma(brow[:], ada_b[l:l + 1, :], writes=[b_brow])
                for blk in range(18):
                    k = blk % 2
                    P.dma(wa[k][:], ada_w[l, :, blk * 512:(blk + 1) * 512].rearrange("(c p) n -> p c n", p=128), writes=[b_wa[k]])
                    for c in range(8):
                        P.op("pe", lambda e: e.matmul(pq[k][0:1, :], lhsT=condT[:, c:c + 1], rhs=wa[k][:, c, :], start=(c == 0), stop=(c == 7)),
                             reads=[b_cond, b_wa[k]], writes=[b_pq[k]])
                    P.op("dve", lambda e: e.tensor_tensor(out=modrow[:, blk * 512:(blk + 1) * 512], in0=pq[k][0:1, :], in1=brow[:, blk * 512:(blk + 1) * 512], op=ALU.add),
                         reads=[b_pq[k], b_brow], writes=[b_mod])
                for s in range(3):
                    sc_ = modrow[:, (s * 3 + 1) * D:(s * 3 + 2) * D]
                    gt_ = modrow[:, (s * 3 + 2) * D:(s * 3 + 3) * D]
                    P.op("dve", lambda e: e.tensor_scalar_add(out=sc_, in0=sc_, scalar1=1.0), reads=[b_mod], writes=[b_mod])
                    f = 1.0 if s == 1 else 0.5
                    P.op("dve", lambda e: e.tensor_scalar(out=gt_, in0=gt_, scalar1=1.0, scalar2=f, op0=ALU.add, op1=ALU.mult), reads=[b_mod], writes=[b_mod])
                P.dma(MODP[l:l + 1, :], modrow[:], reads=[b_mod], writes=[b_MODP])
        P.barrier()

        def load_bcast(tile, src_row, buf):
            P.dma(tile[:], src_row.partition_broadcast(128), reads=[b_MODP], writes=[buf])

        def make_uT(t0, xin, b_xin, xs, b_xs, u, b_u, xT, b_xT, s1, sh, b_ms):
            P.dma(xs[:], xin[t0:t0 + 256, :].rearrange("(j p) d -> p j d", p=128), reads=[b_xin], writes=[b_xs])
            for j in range(2):
                P.op("dve", lambda e: e.tensor_tensor(out=u[:, j, :], in0=xs[:, j, :], in1=s1[:], op=ALU.mult), reads=[b_xs, b_ms], writes=[b_u[j]])
                P.op("pool", lambda e: e.tensor_tensor(out=u[:, j, :], in0=u[:, j, :], in1=sh[:], op=ALU.add), reads=[b_u[j], b_ms], writes=[b_u[j]])
                for c in range(8):
                    P.op("pe", lambda e: e.transpose(out=pq[c // 4][:, (c % 4) * 128:(c % 4 + 1) * 128], in_=u[:, j, c * 128:(c + 1) * 128], identity=ident),
                         reads=[b_u[j], b_c], writes=[b_pq[c // 4]])
                for hh in range(2):
                    P.op("act", lambda e: e.activation(out=xT[:, hh * 4:(hh + 1) * 4, j * 128:(j + 1) * 128],
                                                       in_=pq[hh][:].rearrange("p (c n) -> p c n", c=4), func=AF.Copy),
                         reads=[b_pq[hh]], writes=[b_xT])

        def make_uT_a(t0, xin, b_xin, xs, b_xs, u, b_u, s1, sh, b_ms):
            P.dma(xs[:], xin[t0:t0 + 256, :].rearrange("(j p) d -> p j d", p=128), reads=[b_xin], writes=[b_xs])
            for j in range(2):
                P.op("dve", lambda e: e.tensor_tensor(out=u[:, j, :], in0=xs[:, j, :], in1=s1[:], op=ALU.mult), reads=[b_xs, b_ms], writes=[b_u[j]])
                P.op("pool", lambda e: e.tensor_tensor(out=u[:, j, :], in0=u[:, j, :], in1=sh[:], op=ALU.add), reads=[b_u[j], b_ms], writes=[b_u[j]])

        def make_uT_b(u, b_u, xT, b_xT):
            for j in range(2):
                for c in range(8):
                    P.op("pe", lambda e: e.transpose(out=pq[c // 4][:, (c % 4) * 128:(c % 4 + 1) * 128], in_=u[:, j, c * 128:(c + 1) * 128], identity=ident),
                         reads=[b_u[j], b_c], writes=[b_pq[c // 4]])
                for hh in range(2):
                    P.op("act", lambda e: e.activation(out=xT[:, hh * 4:(hh + 1) * 4, j * 128:(j + 1) * 128],
                                                       in_=pq[hh][:].rearrange("p (c n) -> p c n", c=4), func=AF.Copy),
                         reads=[b_pq[hh]], writes=[b_xT])

        def deepnorm_ln(j, xs, b_xs, gp, lng, lnb, b_ms, t1, r, b_t1, b_r, small, b_small, xo, b_xo):
            pwj = pw[:, 2 * j:2 * j + 2, :]
            P.op("dve", lambda e: e.tensor_tensor(out=t1[:].rearrange("p (a b) -> p a b", a=2), in0=pwj, in1=gp[:].rearrange("p (a b) -> p a b", a=2), op=ALU.mult),
                 reads=[b_pw[2 * j], b_pw[2 * j + 1], b_ms], writes=[b_t1])
            P.op("dve", lambda e: e.scalar_tensor_tensor(out=r[:], in0=xs[:, j, :], scalar=ALPHA, in1=t1[:], op0=ALU.mult, op1=ALU.add),
                 reads=[b_xs, b_t1], writes=[b_r])
            st6 = small[:, 0:12].rearrange("p (a b) -> p a b", a=2)
            for k in range(2):
                P.op("dve", lambda e: e.bn_stats(out=st6[:, k, :], in_=r[:, k * 512:(k + 1) * 512]), reads=[b_r], writes=[b_small])
            mv = small[:, 12:14]
            P.op("dve", lambda e: e.bn_aggr(out=mv, in_=st6), reads=[b_small], writes=[b_small])
            P.op("dve", lambda e: e.tensor_scalar_add(out=small[:, 14:15], in0=small[:, 13:14], scalar1=LN_EPS), reads=[b_small], writes=[b_small])
            P.op("act", lambda e: e.activation(out=small[:, 15:16], in_=small[:, 14:15], func=AF.Sqrt), reads=[b_small], writes=[b_small])
            P.op("dve", lambda e: e.reciprocal(out=small[:, 16:17], in_=small[:, 15:16]), reads=[b_small], writes=[b_small])
            P.op("dve", lambda e: e.tensor_scalar(out=small[:, 17:18], in0=small[:, 12:13], scalar1=small[:, 16:17], scalar2=-1.0, op0=ALU.mult, op1=ALU.mult),
                 reads=[b_small], writes=[b_small])
            P.op("act", lambda e: e.activation(out=t1[:], in_=r[:], func=AF.Identity, scale=small[:, 16:17], bias=small[:, 17:18]),
                 reads=[b_r, b_small], writes=[b_t1])
            P.op("dve", lambda e: e.tensor_tensor(out=t1[:], in0=t1[:], in1=lng[:], op=ALU.mult), reads=[b_t1, b_ms], writes=[b_t1])
            P.op("pool", lambda e: e.tensor_tensor(out=xo[:, j, :], in0=t1[:], in1=lnb[:], op=ALU.add), reads=[b_t1, b_ms], writes=[b_xo])

        def load_w_bf16(dst, src, kchunks, b_dst, piece=1408):
            N = src.shape[1]
            v = src.rearrange("(c p) n -> p c n", p=128)
            for c in range(kchunks):
                for n0 in range(0, N, piece):
                    n1 = min(N, n0 + piece)
                    P.dma(dst[:, c, n0:n1], v[:, c, n0:n1], writes=[b_dst], q="pool")

        def ffn_phase(l, f, sub, xin, b_xin, xout, b_xout):
            with ExitStack() as st:
                w1 = sbt(st, "w1", [128, 8, 2 * DFF], BF16)
                w2 = sbt(st, "w2", [128, 22, D], BF16)
                b_w1, b_w2, b_ms = Buf(), Buf(), Buf()
                load_w_bf16(w1, w_in[l, f], 8, b_w1)
                load_w_bf16(w2, w_out[l, f], 22, b_w2, piece=1024)
                s1, sh, gp, lng, lnb = [sbt(st, "m%d" % i, [128, D]) for i in range(5)]
                load_bcast(sh, MODP[l, (sub * 3 + 0) * D:(sub * 3 + 1) * D], b_ms)
                load_bcast(s1, MODP[l, (sub * 3 + 1) * D:(sub * 3 + 2) * D], b_ms)
                load_bcast(gp, MODP[l, (sub * 3 + 2) * D:(sub * 3 + 3) * D], b_ms)
                P.dma(lng[:], ln_g[l, sub, :].partition_broadcast(128), writes=[b_ms])
                P.dma(lnb[:], ln_b[l, sub, :].partition_broadcast(128), writes=[b_ms])
                xs = [sbt(st, "xs%d" % i, [128, 2, D]) for i in range(2)]
                b_xs = [[Buf(), Buf()], [Buf(), Buf()]]
                xo = sbt(st, "xo", [128, 2, D])
                b_xo = [Buf(), Buf()]
                xT = sbt(st, "xT", [128, 8, 256], BF16)
                b_xT = Buf()
                aT = sbt(st, "aT", [128, 22, 256], BF16)
                b_aT = Buf()
                sg = [sbt(st, "sg%d" % i, [128, 256]) for i in range(2)]
                b_sg = [Buf(), Buf()]
                r = sbt(st, "r", [128, D])
                small = sbt(st, "small", [128, 32])
                b_r, b_small = Buf(), Buf()

                def xtile(ap, ti):
                    return ap[ti * 256:ti * 256 + 256, :].rearrange("(j p) d -> p j d", p=128)

                def f_load(ti):
                    P.dma(xs[ti % 2][:], xtile(xin, ti), reads=[b_xin], writes=b_xs[ti % 2])

                def f_mod(ti):
                    k2 = ti % 2
                    for j in range(2):
                        P.op("dve", lambda e: e.tensor_tensor(out=xs[k2][:, j, :], in0=xs[k2][:, j, :], in1=s1[:], op=ALU.mult), reads=[b_xs[k2][j], b_ms], writes=[b_xs[k2][j]])
                        P.op("pool", lambda e: e.tensor_tensor(out=xs[k2][:, j, :], in0=xs[k2][:, j, :], in1=sh[:], op=ALU.add), reads=[b_xs[k2][j], b_ms], writes=[b_xs[k2][j]])

                def f_trans(ti):
                    k2 = ti % 2
                    for j in range(2):
                        for c in range(8):
                            P.op("pe", lambda e: e.transpose(out=pq[c // 4][:, (c % 4) * 128:(c % 4 + 1) * 128], in_=xs[k2][:, j, c * 128:(c + 1) * 128], identity=ident),
                                 reads=[b_xs[k2][j], b_c], writes=[b_pq[c // 4]])
                        for hh in range(2):
                            P.op("act", lambda e: e.activation(out=xT[:, hh * 4:(hh + 1) * 4, j * 128:(j + 1) * 128],
                                                               in_=pq[hh][:].rearrange("p (c n) -> p c n", c=4), func=AF.Copy),
                                 reads=[b_pq[hh]], writes=[b_xT])

                def f_ln(j):
                    xo_j = xo[:, j, :]
                    pwj = pw[:, 2 * j:2 * j + 2, :]
                    P.op("dve", lambda e: e.tensor_tensor(out=r[:].rearrange("p (a b) -> p a b", a=2), in0=pwj, in1=gp[:].rearrange("p (a b) -> p a b", a=2), op=ALU.mult),
                         reads=[b_pw[2 * j], b_pw[2 * j + 1], b_ms], writes=[b_r])
                    P.op("dve", lambda e: e.scalar_tensor_tensor(out=r[:], in0=xo_j, scalar=ALPHA, in1=r[:], op0=ALU.mult, op1=ALU.add), reads=[b_xo[j], b_r], writes=[b_r])
                    st6 = small[:, 0:12].rearrange("p (a b) -> p a b", a=2)
                    for k in range(2):
                        P.op("dve", lambda e: e.bn_stats(out=st6[:, k, :], in_=r[:, k * 512:(k + 1) * 512]), reads=[b_r], writes=[b_small])
                    P.op("dve", lambda e: e.bn_aggr(out=small[:, 12:14], in_=st6), reads=[b_small], writes=[b_small])
                    P.op("dve", lambda e: e.tensor_scalar_add(out=small[:, 14:15], in0=small[:, 13:14], scalar1=LN_EPS), reads=[b_small], writes=[b_small])
                    P.op("act", lambda e: e.activation(out=small[:, 15:16], in_=small[:, 14:15], func=AF.Sqrt), reads=[b_small], writes=[b_small])
                    P.op("dve", lambda e: e.reciprocal(out=small[:, 16:17], in_=small[:, 15:16]), reads=[b_small], writes=[b_small])
                    P.op("dve", lambda e: e.tensor_scalar(out=small[:, 17:18], in0=small[:, 12:13], scalar1=small[:, 16:17], scalar2=-1.0, op0=ALU.mult, op1=ALU.mult),
                         reads=[b_small], writes=[b_small])
                    P.op("act", lambda e: e.activation(out=xo_j, in_=r[:], func=AF.Identity, scale=small[:, 16:17], bias=small[:, 17:18]),
                         reads=[b_r, b_small], writes=[b_xo[j]])
                    P.op("dve", lambda e: e.tensor_tensor(out=xo_j, in0=xo_j, in1=lng[:], op=ALU.mult), reads=[b_xo[j], b_ms], writes=[b_xo[j]])
                    P.op("pool", lambda e: e.tensor_tensor(out=xo_j, in0=xo_j, in1=lnb[:], op=ALU.add), reads=[b_xo[j], b_ms], writes=[b_xo[j]])

                f_load(0)
                f_mod(0)
                f_trans(0)
                for ti in range(NT):
                    if ti + 1 < NT:
                        f_load(ti + 1)
                    for m in range(22):
                        k = 2 + (m % 2)
                        for half in range(2):
                            col = half * DFF + m * 128
                            for c in range(8):
                                P.op("pe", lambda e: e.matmul(pq[k][:, half * 256:(half + 1) * 256], lhsT=w1[:, c, col:col + 128], rhs=xT[:, c, :], start=(c == 0), stop=(c == 7)),
                                     reads=[b_xT, b_w1], writes=[b_pq[k]])
                        P.op("act", lambda e: e.activation(out=sg[m % 2][:], in_=pq[k][:, 0:256], func=AF.Silu), reads=[b_pq[k]], writes=[b_sg[m % 2]])
                        P.op("dve", lambda e: e.tensor_tensor(out=aT[:, m, :], in0=sg[m % 2][:], in1=pq[k][:, 256:512], op=ALU.mult),
                             reads=[b_sg[m % 2], b_pq[k]], writes=[b_aT])
                    if ti + 1 < NT:
                        f_mod(ti + 1)
                    for j in range(2):
                        for nh in range(2):
                            for m in range(22):
                                P.op("pe", lambda e: e.matmul(pw[:, 2 * j + nh, :], lhsT=aT[:, m, j * 128:(j + 1) * 128], rhs=w2[:, m, nh * 512:(nh + 1) * 512], start=(m == 0), stop=(m == 21)),
                                     reads=[b_aT, b_w2], writes=[b_pw[2 * j + nh]])
                    P.dma(xo[:], xtile(xin, ti), reads=[b_xin], writes=b_xo)
                    if ti + 1 < NT:
                        f_trans(ti + 1)
                    for j in range(2):
                        f_ln(j)
                    P.dma(xtile(xout, ti), xo[:], reads=b_xo, writes=[b_xout])
            P.barrier()

        def inproj_phase(l, xin, b_xin):
            with ExitStack() as st:
                wm = sbt(st, "wm", [128, 8, NC1], BF16)
                b_wm, b_ms = Buf(), Buf()
                load_w_bf16(wm, wm1[l], 8, b_wm, piece=1024)
                s1, sh = [sbt(st, "m%d" % i, [128, D]) for i in range(2)]
                load_bcast(sh, MODP[l, 3 * D:4 * D], b_ms)
                load_bcast(s1, MODP[l, 4 * D:5 * D], b_ms)
                xs2 = [sbt(st, "xs%d" % i, [128, 2, D]) for i in range(2)]
                b_xs2 = [Buf(), Buf()]
                u2 = [sbt(st, "u%d" % i, [128, 2, D]) for i in range(2)]
                b_u2 = [[Buf(), Buf()], [Buf(), Buf()]]
                xT2 = [sbt(st, "xT%d" % i, [128, 8, 256], BF16) for i in range(2)]
                b_xT2 = [Buf(), Buf()]
                rc = sbt(st, "rc", [128, 2, 256])
                b_rc = Buf()
                ta = [sbt(st, "ta%d" % i, [128, 256]) for i in range(2)]
                tb = [sbt(st, "tb%d" % i, [128, 256]) for i in range(2)]
                b_ta, b_tb = [Buf(), Buf()], [Buf(), Buf()]
                stg = sbt(st, "stg", [128, NQK, 256], BF16)
                fgs = sbt(st, "fgs", [4, 256])
                vst = sbt(st, "vst", [128, 2, 1152], BF16)
                iws = sbt(st, "iws", [128, 2, 4])
                b_stg, b_fgs, b_vst, b_iws = Buf(), Buf(), Buf(), Buf()
                make_uT_a(0, xin, b_xin, xs2[0], b_xs2[0], u2[0], b_u2[0], s1, sh, b_ms)
                make_uT_b(u2[0], b_u2[0], xT2[0], b_xT2[0])
                for ti in range(NT):
                    t0 = ti * 256
                    xT, b_xT = xT2[ti % 2], b_xT2[ti % 2]
                    n2 = (ti + 1) % 2
                    P.dma(rc[:], rope_in[:, :, t0:t0 + 256].rearrange("a p t -> p a t"), writes=[b_rc])
                    if ti + 1 < NT:
                        make_uT_a(t0 + 256, xin, b_xin, xs2[n2], b_xs2[n2], u2[n2], b_u2[n2], s1, sh, b_ms)
                    for ch in range(NQK):
                        if ti + 1 < NT and ch == 16:
                            make_uT_b(u2[n2], b_u2[n2], xT2[n2], b_xT2[n2])
                        if ch < NROPE:
                            k = 2 * (ch % 2)
                            for which, cc in ((0, ch), (1, NQK + ch)):
                                for c in range(8):
                                    P.op("pe", lambda e: e.matmul(pw[:, k + which, 0:256], lhsT=wm[:, c, cc * 128:(cc + 1) * 128], rhs=xT[:, c, :], start=(c == 0), stop=(c == 7)),
                                         reads=[b_xT, b_wm], writes=[b_pw[k + which]])
                            kk = ch % 2
                            P.op("dve", lambda e: e.tensor_tensor(out=ta[kk][:], in0=pw[:, k, 0:256], in1=rc[:, 0, :], op=ALU.mult), reads=[b_pw[k], b_rc], writes=[b_ta[kk]])
                            P.op("dve", lambda e: e.tensor_tensor(out=tb[kk][:], in0=pw[:, k + 1, 0:256], in1=rc[:, 1, :], op=ALU.mult), reads=[b_pw[k + 1], b_rc], writes=[b_tb[kk]])
                            P.op("pool", lambda e: e.tensor_tensor(out=stg[:, ch, :], in0=ta[kk][:], in1=tb[kk][:], op=ALU.add), reads=[b_ta[kk], b_tb[kk]], writes=[b_stg])
                        else:
                            k = ch % 4
                            for c in range(8):
                                P.op("pe", lambda e: e.matmul(pw[:, k, 0:256], lhsT=wm[:, c, ch * 128:(ch + 1) * 128], rhs=xT[:, c, :], start=(c == 0), stop=(c == 7)),
                                     reads=[b_xT, b_wm], writes=[b_pw[k]])
                            P.op("act", lambda e: e.activation(out=stg[:, ch, :], in_=pw[:, k, 0:256], func=AF.Copy), reads=[b_pw[k]], writes=[b_stg])
                    for c in range(8):
                        P.op("pe", lambda e: e.matmul(pw[0:4, 0, 0:256], lhsT=wm[:, c, FGCOL:FGCOL + 4], rhs=xT[:, c, :], start=(c == 0), stop=(c == 7)),
                             reads=[b_xT, b_wm], writes=[b_pw[0]])
                    P.op("act", lambda e: e.activation(out=fgs[:], in_=pw[0:4, 0, 0:256], func=AF.Copy), reads=[b_pw[0]], writes=[b_fgs])
                    P.dma(FGT[:, t0:t0 + 256], fgs[:], reads=[b_fgs], writes=[b_FGT])
                    P.dma(QKT[:, :, t0:t0 + 256].rearrange("c p t -> p c t"), stg[:], reads=[b_stg], writes=[b_QKT])
                    ki = 0
                    for j in range(2):
                        for (n0, n1) in ((0, 384), (384, 896), (896, 1156)):
                            k = 2 + (ki % 2)
                            ki += 1
                            for c in range(8):
                                P.op("pe", lambda e: e.matmul(pq[k][:, 0:n1 - n0], lhsT=xT[:, c, j * 128:(j + 1) * 128], rhs=wm[:, c, TMCOL + n0:TMCOL + n1], start=(c == 0), stop=(c == 7)),
                                     reads=[b_xT, b_wm], writes=[b_pq[k]])
                            if n1 <= 1152:
                                P.op("act", lambda e: e.activation(out=vst[:, j, n0:n1], in_=pq[k][:, 0:n1 - n0], func=AF.Copy), reads=[b_pq[k]], writes=[b_vst])
                            else:
                                P.op("act", lambda e: e.activation(out=vst[:, j, n0:1152], in_=pq[k][:, 0:1152 - n0], func=AF.Copy), reads=[b_pq[k]], writes=[b_vst])
                                P.op("dve", lambda e: e.tensor_copy(out=iws[:, j, :], in_=pq[k][:, 1152 - n0:1156 - n0]), reads=[b_pq[k]], writes=[b_iws])
                    P.dma(VTM[t0:t0 + 256, :].rearrange("(j p) n -> p j n", p=128), vst[:], reads=[b_vst], writes=[b_VTM])
                    P.dma(IWT[t0:t0 + 256, :].rearrange("(j p) n -> p j n", p=128), iws[:], reads=[b_iws], writes=[b_IWT])
            P.barrier()
            with ExitStack() as st:
                wg = sbt(st, "wg", [128, 8, 4096], BF16)
                b_wg, b_ms = Buf(), Buf()
                load_w_bf16(wg, wgate[l], 8, b_wg, piece=1024)
                s1, sh = [sbt(st, "m%d" % i, [128, D]) for i in range(2)]
                load_bcast(sh, MODP[l, 3 * D:4 * D], b_ms)
                load_bcast(s1, MODP[l, 4 * D:5 * D], b_ms)
                bg = sbt(st, "bg", [128, 32])
                P.dma(bg[:], b_gate[l].rearrange("(c p) -> p c", p=128), writes=[b_ms], allow_slow_non_contiguous=True)
                xs2 = [sbt(st, "xs%d" % i, [128, 2, D]) for i in range(2)]
                b_xs2 = [Buf(), Buf()]
                u2 = [sbt(st, "u%d" % i, [128, 2, D]) for i in range(2)]
                b_u2 = [[Buf(), Buf()], [Buf(), Buf()]]
                xT2 = [sbt(st, "xT%d" % i, [128, 8, 256], BF16) for i in range(2)]
                b_xT2 = [Buf(), Buf()]
                gst = [sbt(st, "gst%d" % i, [128, 32, 256], BF16) for i in range(2)]
                b_gst = [Buf(), Buf()]
                make_uT_a(0, xin, b_xin, xs2[0], b_xs2[0], u2[0], b_u2[0], s1, sh, b_ms)
                make_uT_b(u2[0], b_u2[0], xT2[0], b_xT2[0])
                for ti in range(NT):
                    t0 = ti * 256
                    g2 = ti % 2
                    xT, b_xT = xT2[g2], b_xT2[g2]
                    n2 = (ti + 1) % 2
                    for ch in range(32):
                        k = ch % 4
                        if ti + 1 < NT and ch == 0:
                            make_uT_a(t0 + 256, xin, b_xin, xs2[n2], b_xs2[n2], u2[n2], b_u2[n2], s1, sh, b_ms)
                        if ti + 1 < NT and ch == 24:
                            make_uT_b(u2[n2], b_u2[n2], xT2[n2], b_xT2[n2])
                        for c in range(8):
                            P.op("pe", lambda e: e.matmul(pw[:, k, 0:256], lhsT=wg[:, c, ch * 128:(ch + 1) * 128], rhs=xT[:, c, :], start=(c == 0), stop=(c == 7)),
                                 reads=[b_xT, b_wg], writes=[b_pw[k]])
                        P.op("act", lambda e: e.activation(out=gst[g2][:, ch, :], in_=pw[:, k, 0:256], func=AF.Sigmoid, bias=bg[:, ch:ch + 1]), reads=[b_pw[k], b_ms], writes=[b_gst[g2]])
                    P.dma(GT[:, :, t0:t0 + 256].rearrange("c p t -> p c t"), gst[g2][:], reads=[b_gst[g2]], writes=[b_GT])
            P.barrier()

        def attention_phase(l):
            with ExitStack() as sm:
                cumT = sbt(sm, "cumT", [128, NB * 4])
                Gb = sbt(sm, "Gb", [128, NB * 4])
                ones64 = sbt(sm, "ones64", [128, 64])
                b_cum = Buf()
                b_o64 = Buf()
                P.op("pool", lambda e: e.memset(ones64[:], 1.0), writes=[b_o64])
                with ExitStack() as st:
                    fg = sbt(st, "fg", [4, T])
                    sp = sbt(st, "sp", [4, T])
                    on4 = sbt(st, "on4", [4, T])
                    nb4 = sbt(st, "nb4", [4, 1])
                    b_fg, b_sp, b_on, b_nb = Buf(), Buf(), Buf(), Buf()
                    P.dma(fg[:], FGT, reads=[b_FGT], writes=[b_fg])
                    P.dma(nb4[:], b_forget[l].rearrange("(p a) -> p a", a=1), writes=[b_nb])
                    P.op("dve", lambda e: e.tensor_scalar_mul(out=nb4[:], in0=nb4[:], scalar1=-1.0), reads=[b_nb], writes=[b_nb])
                    P.op("pool", lambda e: e.memset(on4[:], 1.0), writes=[b_on])
                    P.op("act", lambda e: e.activation(out=sp[:], in_=fg[:], func=AF.Exp, scale=-1.0, bias=nb4[:, 0:1]), reads=[b_fg, b_nb], writes=[b_sp])
                    P.op("act", lambda e: e.activation(out=sp[:], in_=sp[:], func=AF.Ln, bias=1.0), reads=[b_sp], writes=[b_sp])
                    P.op("dve", lambda e: e.tensor_tensor_scan(out=fg[:], data0=on4[:], data1=sp[:], initial=0.0, op0=ALU.mult, op1=ALU.subtract),
                         reads=[b_on, b_sp], writes=[b_fg])
                    for blk in range(NB):
                        P.op("pe", lambda e: e.transpose(out=pq[0][:, blk * 4:(blk + 1) * 4], in_=fg[0:4, blk * 128:(blk + 1) * 128], identity=ident[0:4, 0:4]),
                             reads=[b_fg, b_c], writes=[b_pq[0]])
                    P.op("act", lambda e: e.activation(out=cumT[:], in_=pq[0][:, 0:NB * 4], func=AF.Copy), reads=[b_pq[0]], writes=[b_cum])
                    P.op("pe", lambda e: e.matmul(pq[1][:, 0:NB * 4], lhsT=E0, rhs=cumT[:], start=True, stop=True), reads=[b_cum, b_c], writes=[b_pq[1]])
                    P.op("act", lambda e: e.activation(out=Gb[:], in_=pq[1][:, 0:NB * 4], func=AF.Copy), reads=[b_pq[1]], writes=[b_cum])
                P.barrier()

                def load_kqv(st, qch, kch, vcol, nh, noq=False):
                    kt = sbt(st, "kt", [128, nh // 2, T], BF16)
                    qt = sbt(st, "qt", [128, nh // 2, 128 if noq else T], BF16)
                    va = sbt(st, "va", [128, NB, nh, 65], BF16)
                    b_k, b_q, b_v = Buf(), Buf(), Buf()
                    P.dma(kt[:], QKT[kch:kch + nh // 2].rearrange("c p t -> p c t"), reads=[b_QKT], writes=[b_k])
                    if not noq:
                        P.dma(qt[:], QKT[qch:qch + nh // 2].rearrange("c p t -> p c t"), reads=[b_QKT], writes=[b_q])
                    P.op("pool", lambda e: e.memset(va[:, :, :, 64:65], 1.0), writes=[b_v])
                    for h in range(nh):
                        P.dma(va[:, :, h, 0:64], VTM[:, vcol + h * 64:vcol + (h + 1) * 64].rearrange("(k p) e -> p k e", p=128), reads=[b_VTM], writes=[b_v])
                    return kt, qt, va, b_k, b_q, b_v

                Sk = [pqq[:, 0:2, 0:256], pqq[:, 2:4, 0:256]]
                b_Sk = [[b_pq[0], b_pq[1]], [b_pq[2], b_pq[3]]]
                po4 = pw[:, 0, :].rearrange("p (h t) -> p h t", h=4)
                b_po = b_pw[0]
                pbk = pw[:, 1, :]
                b_pb = b_pw[1]

                def sreg(k, h):
                    return pqq[:, 2 * k + h % 2, (h // 2) * 128:(h // 2 + 1) * 128]

                def qk4(k, kt, qt, b_k, b_q, i, j):
                    for h in range(4):
                        P.op("pe", lambda e: e.matmul(sreg(k, h), lhsT=kt[hr(h), h // 2, j * 128:(j + 1) * 128], rhs=qt[hr(h), h // 2, i * 128:(i + 1) * 128], start=True, stop=True),
                             reads=[b_k, b_q], writes=[b_pq[2 * k + h % 2]])

                def av4(pT, b_pT, va, b_v, j, first, last, M):
                    for h in range(4):
                        P.op("pe", lambda e: e.matmul(po4[0:M, h, :], lhsT=va[:, j, h, 0:M], rhs=pT[:, sl(h), :], start=(first and h == 0), stop=last, skip_group_check=True),
                             reads=[b_pT, b_v], writes=[b_po])

                def normalize_store(rrow, b_rr, rb, b_rb, yst, b_yst, ych, i):
                    P.op("dve", lambda e: e.reciprocal(out=rrow[64:65, :, :], in_=po4[64:65, :, :]), reads=[b_po], writes=[b_rr])
                    P.op("pe", lambda e: e.matmul(pbk[0:64, :], lhsT=ones64[64:65, :], rhs=rrow[64:65, :, :].rearrange("p a b -> p (a b)"), start=True, stop=True),
                         reads=[b_rr, b_o64], writes=[b_pb])
                    P.op("act", lambda e: e.activation(out=rb[0:64, :, :].rearrange("p a b -> p (a b)"), in_=pbk[0:64, :], func=AF.Copy), reads=[b_pb], writes=[b_rb])
                    P.op("dve", lambda e: e.tensor_tensor(out=yst[0:64, :, :], in0=po4[0:64, :, :], in1=rb[0:64, :, :], op=ALU.mult), reads=[b_po, b_rb], writes=[b_yst])
                    P.dma(YT[ych:ych + 2, :, i * 128:(i + 1) * 128].rearrange("c (h e) t -> e (c h) t", e=64), yst[0:64, :, :], reads=[b_yst], writes=[b_YT])

                def flat(t):
                    return t[:].rearrange("p a b -> p (a b)")

                with ExitStack() as st:
                  if ATT_STOP >= 1 and ATT_ONLY in (-1, 1):
                      kt, qt, va, b_k, b_q, b_v = load_kqv(st, 17, 19, 896, 4)
                      bm = [sbt(st, "bm%d" % i, [128, 4, NB]) for i in range(2)]
                      b_bm = [Buf(), Buf()]
                      tS = [sbt(st, "tS%d" % i, [128, 4, 128]) for i in range(2)]
                      b_tS = [Buf(), Buf()]
                      pT = [sbt(st, "pT%d" % i, [128, 4, 128], BF16) for i in range(2)]
                      b_pT = [Buf(), Buf()]
                      rrow = sbt(st, "rrow", [128, 4, 128])
                      rb = sbt(st, "rb", [64, 4, 128])
                      yst = sbt(st, "yst", [64, 4, 128], BF16)
                      b_rr, b_rb, b_yst = Buf(), Buf(), Buf()
                      cum3 = cumT[:].rearrange("p (k h) -> p k h", h=4)
                      pc = 0
                      for i in range(NB):
                          bi = i % 2
                          for h in range(4):
                              P.op("dve", lambda e: e.tensor_scalar(out=bm[bi][:, h, 0:i + 1], in0=cum3[:, 0:i + 1, h], scalar1=-1.0, scalar2=Gb[:, i * 4 + h:i * 4 + h + 1], op0=ALU.mult, op1=ALU.add),
                                   reads=[b_cum], writes=[b_bm[bi]])
                          qk4(0, kt, qt, b_k, b_q, i, 0)
                          for j in range(i + 1):
                              k = j % 2
                              if j < i:
                                  qk4((j + 1) % 2, kt, qt, b_k, b_q, i, j + 1)
                              kp = pc % 2
                              pc += 1
                              for h in (1, 3):
                                  P.op("dve", lambda e: e.tensor_scalar(out=tS[kp][:, sl(h), :], in0=sreg(k, h), scalar1=0.125, scalar2=bm[bi][:, h, j:j + 1], op0=ALU.mult, op1=ALU.add),
                                       reads=[b_pq[2 * k + h % 2], b_bm[bi]], writes=[b_tS[kp]])
                              for h in (0, 2):
                                  P.op("act", lambda e: e.activation(out=pT[kp][:, sl(h), :], in_=sreg(k, h), func=AF.Exp, scale=0.125, bias=bm[bi][:, h, j:j + 1]),
                                       reads=[b_pq[2 * k + h % 2], b_bm[bi]], writes=[b_pT[kp]])
                              P.op("act", lambda e: e.activation(out=pT[kp][:, 2:4, :], in_=tS[kp][:, 2:4, :], func=AF.Exp), reads=[b_tS[kp]], writes=[b_pT[kp]])
                              if j == i:
                                  P.op("pool", lambda e: e.tensor_tensor(out=pT[kp][:], in0=pT[kp][:], in1=LE4, op=ALU.mult), reads=[b_pT[kp], b_c], writes=[b_pT[kp]])
                              av4(pT[kp], b_pT[kp], va, b_v, j, j == 0, j == i, 65)
                          normalize_store(rrow, b_rr, rb, b_rb, yst, b_yst, 5, i)
                P.barrier()

                with ExitStack() as st:
                  if ATT_STOP >= 2 and ATT_ONLY in (-1, 2):
                      kt, qt, va, b_k, b_q, b_v = load_kqv(st, 13, 15, 640, 4)
                      ef = sbt(st, "ef", [128, 512])
                      spf = [sbt(st, "spf%d" % i, [128, 512]) for i in range(2)]
                      tt = sbt(st, "tt", [128, 512])
                      arg = sbt(st, "arg", [128, 512])
                      carry = sbt(st, "carry", [128, 512])
                      b_ef, b_tt, b_arg, b_carry = Buf(), Buf(), Buf(), Buf()
                      b_spf = [Buf(), Buf()]
                      shi = [sbt(st, "shi%d" % i, [128, 512], BF16) for i in range(2)]
                      slo = [sbt(st, "slo%d" % i, [128, 512], BF16) for i in range(2)]
                      b_shi, b_slo = [Buf(), Buf()], [Buf(), Buf()]
                      pT = [sbt(st, "pT%d" % i, [128, 4, 128], BF16) for i in range(2)]
                      b_pT = [Buf(), Buf()]
                      yst = sbt(st, "yst", [64, 4, 128], BF16)
                      b_yst = Buf()
                      LT4f = LT4.rearrange("p a b -> p (a b)")
                      pA, pB = pw[:, 2, :], pw[:, 3, :]
                      b_pA, b_pB = b_pw[2], b_pw[3]

                      def c_front(i, j, k):
                          qk4(k, kt, qt, b_k, b_q, i, j)
                          P.op("act", lambda e: e.activation(out=v3(ef[:]), in_=Sk[k], func=AF.Exp, scale=0.125), reads=b_Sk[k], writes=[b_ef])
                          P.op("act", lambda e: e.activation(out=spf[k][:], in_=ef[:], func=AF.Ln, bias=1.0), reads=[b_ef], writes=[b_spf[k]])
                          if j == i:
                              P.op("dve", lambda e: e.tensor_tensor(out=spf[k][:], in0=spf[k][:], in1=LT4f, op=ALU.mult), reads=[b_spf[k], b_c], writes=[b_spf[k]])
                          P.op("dve", lambda e: e.tensor_copy(out=shi[k][:], in_=spf[k][:]), reads=[b_spf[k]], writes=[b_shi[k]])
                          P.op("pool", lambda e: e.tensor_tensor(out=slo[k][:], in0=spf[k][:], in1=shi[k][:], op=ALU.subtract), reads=[b_spf[k], b_shi[k]], writes=[b_slo[k]])

                      def c_back(i, j, k):
                          P.op("pe", lambda e: e.matmul(pA, lhsT=Ugeb, rhs=shi[k][:], start=True, stop=False), reads=[b_shi[k], b_c], writes=[b_pA])
                          P.op("pe", lambda e: e.matmul(pA, lhsT=Ugeb, rhs=slo[k][:], start=False, stop=True), reads=[b_slo[k], b_c], writes=[b_pA])
                          if j > 0:
                              P.op("pe", lambda e: e.matmul(pB, lhsT=onesb, rhs=shi[k][:], start=True, stop=False), reads=[b_shi[k], b_c], writes=[b_pB])
                              P.op("pe", lambda e: e.matmul(pB, lhsT=onesb, rhs=slo[k][:], start=False, stop=True), reads=[b_slo[k], b_c], writes=[b_pB])
                          P.op("dve", lambda e: e.tensor_tensor(out=tt[:], in0=pA, in1=carry[:], op=ALU.add), reads=[b_pA, b_carry], writes=[b_tt])
                          P.op("dve", lambda e: e.scalar_tensor_tensor(out=v3(arg[:]), in0=Sk[k], scalar=0.125, in1=v3(tt[:]), op0=ALU.mult, op1=ALU.subtract),
                               reads=b_Sk[k] + [b_tt], writes=[b_arg])
                          if j > 0:
                              P.op("dve", lambda e: e.tensor_tensor(out=carry[:], in0=pB, in1=carry[:], op=ALU.add), reads=[b_pB, b_carry], writes=[b_carry])
                          P.op("act", lambda e: e.activation(out=flat(pT[k]), in_=arg[:], func=AF.Exp), reads=[b_arg], writes=[b_pT[k]])
                          if j == i:
                              P.op("pool", lambda e: e.tensor_tensor(out=pT[k][:], in0=pT[k][:], in1=LT4, op=ALU.mult), reads=[b_pT[k], b_c], writes=[b_pT[k]])

                      def c_av(i, j, k):
                          av4(pT[k], b_pT[k], va, b_v, j, j == i, j == 0, 64)

                      for i in range(NB):
                          P.op("pool", lambda e: e.memset(carry[:], 0.0), writes=[b_carry])
                          pairs = list(range(i, -1, -1))
                          m_ = len(pairs)
                          c_front(i, pairs[0], 0)
                          if m_ > 1:
                              c_front(i, pairs[1], 1)
                          c_back(i, pairs[0], 0)
                          for n, j in enumerate(pairs):
                              if n + 2 < m_:
                                  c_front(i, pairs[n + 2], (n + 2) % 2)
                              if n + 1 < m_:
                                  c_back(i, pairs[n + 1], (n + 1) % 2)
                              c_av(i, j, n % 2)
                          P.op("act", lambda e: e.activation(out=yst[0:64, :, :], in_=po4[0:64, :, :], func=AF.Copy), reads=[b_po], writes=[b_yst])
                          P.dma(YT[3:5, :, i * 128:(i + 1) * 128].rearrange("c (h e) t -> e (c h) t", e=64), yst[0:64, :, :], reads=[b_yst], writes=[b_YT])
                P.barrier()

                with ExitStack() as st:
                  if ATT_STOP >= 3 and ATT_ONLY in (-1, 3):
                      kt, qt_full, va, b_k, b_q_full, b_v = load_kqv(st, 6, 8, 384, 4, noq=True)
                      qtb = [sbt(st, "qtb%d" % i, [128, 2, 128], BF16) for i in range(2)]
                      qib = [sbt(st, "qib%d" % i, [128, 2, 128], BF16) for i in range(2)]
                      b_qtb = [Buf(), Buf()]
                      kit = sbt(st, "kit", [128, T], BF16)
                      wi = sbt(st, "wi", [128, NB, 4])
                      b_qi = Buf()
                      P.dma(kit[:], QKT[12], reads=[b_QKT], writes=[b_qi])
                      P.dma(wi[:], IWT.rearrange("(k p) h -> p k h", p=128), reads=[b_IWT], writes=[b_qi])
                      sc = sbt(st, "sc", [128, T])
                      mk = [sbt(st, "mk%d" % i, [128, T], BF16) for i in range(2)]
                      rl = [sbt(st, "rl%d" % i, [128, 512]) for i in range(2)]
                      b_sc = Buf()
                      b_mk = [Buf(), Buf()]
                      b_jD = [Buf(), Buf()]
                      b_jA = [Buf(), Buf()]
                      b_rl = [Buf(), Buf()]
                      sm_ = sbt(st, "smallb", [128, 12])
                      b_sm = Buf()
                      b_sA = Buf()
                      b_nm = Buf()
                      lo, hi, w0, nmid, cnt, ge, mid, tcb = [sm_[:, a:a + 1] for a in range(8)]
                      sA = sm_[:, 8:9]
                      pT = [sbt(st, "pT%d" % i, [128, 4, 128], BF16) for i in range(2)]
                      b_pT = [Buf(), Buf()]
                      rrow = sbt(st, "rrow", [128, 4, 128])
                      rb = sbt(st, "rb", [64, 4, 128])
                      yst = sbt(st, "yst", [64, 4, 128], BF16)
                      b_rr, b_rb, b_yst = Buf(), Buf(), Buf()
                      pm = [pw[:, 2, :].bitcast(BF16), pw[:, 3, :].bitcast(BF16)]
                      b_pm = [b_pw[2], b_pw[3]]

                      def b_front(qb_, m, j, k):
                          for h in range(4):
                              P.op("pe", lambda e: e.transpose(out=pm[k][:, h * 128:(h + 1) * 128], in_=mk[m][:, j * 128:(j + 1) * 128], identity=identb),
                                   reads=[b_mk[m], b_c], writes=[b_pm[k]])
                          qk4(k, kt, qtb[qb_], b_k, b_qtb[qb_], 0, j)
                          P.op("act", lambda e: e.activation(out=v3(flat(pT[k])), in_=Sk[k], func=AF.Exp, scale=0.125), reads=b_Sk[k], writes=[b_pT[k]])

                      def b_back(i, j, k):
                          P.op("dve", lambda e: e.tensor_tensor(out=flat(pT[k]), in0=flat(pT[k]), in1=pm[k][:, 0:512], op=ALU.mult),
                               reads=[b_pT[k], b_pm[k]], writes=[b_pT[k]])
                          av4(pT[k], b_pT[k], va, b_v, j, j == 0, j == i, 65)

                      for i in range(NB):
                          n = 128 * (i + 1)
                          nchunk = (n + 511) // 512
                          qb_ = i % 2
                          m = i % 2
                          P.dma(qtb[qb_][:], QKT[6:8, :, i * 128:(i + 1) * 128].rearrange("c p t -> p c t"), reads=[b_QKT], writes=[b_qtb[qb_]])
                          P.dma(qib[qb_][:], QKT[10:12, :, i * 128:(i + 1) * 128].rearrange("c p t -> p c t"), reads=[b_QKT], writes=[b_qtb[qb_]])
                          for c in range(nchunk):
                              w = min(512, n - 512 * c)
                              for h in range(4):
                                  k = h % 2
                                  P.op("pe", lambda e: e.matmul(pq[k][:, 0:w], lhsT=qib[qb_][hr(h), h // 2, :], rhs=kit[hr(h), c * 512:c * 512 + w], start=True, stop=True),
                                       reads=[b_qi, b_qtb[qb_]], writes=[b_pq[k]])
                                  P.op("act", lambda e: e.activation(out=rl[k][:, 0:w], in_=pq[k][:, 0:w], func=AF.Relu), reads=[b_pq[k]], writes=[b_rl[k]])
                                  if h == 0:
                                      P.op("dve", lambda e: e.tensor_scalar(out=sc[:, c * 512:c * 512 + w], in0=rl[k][:, 0:w], scalar1=wi[:, i, 0:1], scalar2=None, op0=ALU.mult),
                                           reads=[b_rl[k], b_qi], writes=[b_sc])
                                  else:
                                      P.op("dve", lambda e: e.scalar_tensor_tensor(out=sc[:, c * 512:c * 512 + w], in0=rl[k][:, 0:w], scalar=wi[:, i, h:h + 1], in1=sc[:, c * 512:c * 512 + w], op0=ALU.mult, op1=ALU.add),
                                           reads=[b_rl[k], b_qi, b_sc], writes=[b_sc])
                          P.op("dve", lambda e: e.tensor_reduce(out=lo, in_=sc[:, 0:n], axis=AX.X, op=ALU.min), reads=[b_sc], writes=[b_sm])
                          P.op("dve", lambda e: e.tensor_reduce(out=hi, in_=sc[:, 0:n], axis=AX.X, op=ALU.max), reads=[b_sc], writes=[b_sm])
                          P.op("dve", lambda e: e.tensor_tensor(out=sc[:, n - 128:n], in0=sc[:, n - 128:n], in1=negmask, op=ALU.add), reads=[b_sc, b_c], writes=[b_sc])
                          P.op("dve", lambda e: e.tensor_tensor(out=w0, in0=hi, in1=lo, op=ALU.subtract), reads=[b_sm], writes=[b_sm])
                          P.op("dve", lambda e: e.memset(mk[m][:, 0:2], 0.0), writes=[b_mk[m], b_jD[m], b_jA[m]])
                          nd = max(0, ((int(0.457 * n) - 924) // 64) * 64)
                          na = n - nd
                          for it in range(1, NITER + 1):
                              f = 2.0 ** (-it)
                              P.op("dve", lambda e: e.scalar_tensor_tensor(out=mid, in0=w0, scalar=f, in1=lo, op0=ALU.mult, op1=ALU.add), reads=[b_sm], writes=[b_nm])
                              P.op("act", lambda e: e.activation(out=mk[m][:, nd:n], in_=sc[:, nd:n], func=AF.Sign, bias=mid, scale=-1.0, accum_out=sA),
                                   reads=[b_sc, b_nm], writes=[b_jA[m], b_sA])
                              if nd > 0:
                                  P.op("dve", lambda e: e.tensor_scalar(out=mk[m][:, 0:nd], in0=sc[:, 0:nd], scalar1=mid, scalar2=None, op0=ALU.is_ge, op1=ALU.add, accum_out=cnt),
                                       reads=[b_sc, b_nm], writes=[b_jD[m], b_sm])
                                  P.op("dve", lambda e: e.scalar_tensor_tensor(out=cnt, in0=sA, scalar=-0.5, in1=cnt, op0=ALU.mult, op1=ALU.add), reads=[b_sA, b_sm], writes=[b_sm])
                                  P.op("dve", lambda e: e.tensor_scalar(out=ge, in0=cnt, scalar1=TOPK - 0.5 - na / 2.0, scalar2=f, op0=ALU.is_ge, op1=ALU.mult), reads=[b_sm], writes=[b_sm])
                              else:
                                  P.op("dve", lambda e: e.tensor_scalar(out=ge, in0=sA, scalar1=float(n - 2 * TOPK + 1), scalar2=f, op0=ALU.is_le, op1=ALU.mult), reads=[b_sA], writes=[b_sm])
                              P.op("dve", lambda e: e.scalar_tensor_tensor(out=lo, in0=ge, scalar=w0, in1=lo, op0=ALU.mult, op1=ALU.add), reads=[b_sm], writes=[b_sm])
                          P.op("dve", lambda e: e.tensor_scalar(out=mk[m][:, 0:n], in0=sc[:, 0:n], scalar1=lo, scalar2=None, op0=ALU.is_ge), reads=[b_sc, b_sm], writes=[b_mk[m], b_jD[m], b_jA[m]])
                          b_front(qb_, m, 0, 0)
                          for j in range(i + 1):
                              if j < i:
                                  b_front(qb_, m, j + 1, (j + 1) % 2)
                              b_back(i, j, j % 2)
                          normalize_store(rrow, b_rr, rb, b_rb, yst, b_yst, 1, i)
                P.barrier()


                with ExitStack() as st:
                  if ATT_STOP >= 4 and ATT_ONLY in (-1, 4):
                      accA = sbt(st, "accA", [65, 2, T])
                      b_acc = Buf()
                      kt = sbt(st, "kt", [128, T], BF16)
                      qt = sbt(st, "qt", [128, T], BF16)
                      va = sbt(st, "va", [128, NB, 2, 65], BF16)
                      b_k, b_q, b_v = Buf(), Buf(), Buf()
                      pT = [sbt(st, "pT%d" % i, [128, 4, 128], BF16) for i in range(2)]
                      b_pT = [Buf(), Buf()]
                      P.op("pool", lambda e: e.memset(va[:, :, :, 64:65], 1.0), writes=[b_v])
                      pc = 0
                      for g, d in enumerate((1, 4, 16)):
                          nbs = NB // d
                          P.dma(kt[:], QKT[3 + g], reads=[b_QKT], writes=[b_k])
                          P.dma(qt[:], QKT[g], reads=[b_QKT], writes=[b_q])
                          vsrc = VTM[:, g * 128:(g + 1) * 128].rearrange("(k p r) (h e) -> r p k h e", p=128, r=d, e=64)
                          for r_ in range(d):
                              for hh in range(2):
                                  P.dma(va[:, r_ * nbs:(r_ + 1) * nbs, hh, 0:64], vsrc[r_, :, :, hh, :], reads=[b_VTM], writes=[b_v])
                          for r_ in range(d):
                              for kb in range(nbs):
                                  k = pc % 2
                                  pc += 1

                                  def tok(kk):
                                      base = r_ + d * 128 * kk
                                      return slice(base, base + d * 127 + 1, d) if d > 1 else slice(base, base + 128)
                                  for hh in range(2):
                                      for wch in range(2):
                                          if kb == 0 and wch == 0:
                                              continue
                                          P.op("pe", lambda e: e.matmul(pqq[:, hh, wch * 128:(wch + 1) * 128], lhsT=kt[hr(hh), tok(kb - 1 + wch)], rhs=qt[hr(hh), tok(kb)], start=True, stop=True),
                                               reads=[b_k, b_q], writes=[b_pq[hh]])
                                  if kb == 0:
                                      for hh in range(2):
                                          P.op("act", lambda e: e.activation(out=pT[k][:, hh * 2 + 1, :], in_=pqq[:, hh, 128:256], func=AF.Exp, scale=0.125), reads=[b_pq[hh]], writes=[b_pT[k]])
                                          P.op("pool", lambda e: e.tensor_tensor(out=pT[k][:, hh * 2 + 1, :], in0=pT[k][:, hh * 2 + 1, :], in1=MA4[:, hh * 2 + 1, :], op=ALU.mult), reads=[b_pT[k], b_c], writes=[b_pT[k]])
                                  else:
                                      P.op("act", lambda e: e.activation(out=v3(pT[k][:].rearrange("p a b -> p (a b)")), in_=SS, func=AF.Exp, scale=0.125), reads=[b_pq[0], b_pq[1]], writes=[b_pT[k]])
                                      P.op("pool", lambda e: e.tensor_tensor(out=pT[k][:], in0=pT[k][:], in1=MA4, op=ALU.mult), reads=[b_pT[k], b_c], writes=[b_pT[k]])
                                  for hh in range(2):
                                      if kb > 0:
                                          P.op("pe", lambda e: e.matmul(pw[0:65, hh, 0:128], lhsT=va[:, r_ * nbs + kb - 1, hh, :], rhs=pT[k][:, hh * 2, :], start=True, stop=False),
                                               reads=[b_pT[k], b_v], writes=[b_pw[hh]])
                                      P.op("pe", lambda e: e.matmul(pw[0:65, hh, 0:128], lhsT=va[:, r_ * nbs + kb, hh, :], rhs=pT[k][:, hh * 2 + 1, :], start=(kb == 0), stop=True),
                                           reads=[b_pT[k], b_v], writes=[b_pw[hh]])
                                  dst = accA[0:65, :, tok(kb)]
                                  if g == 0:
                                      P.op("dve", lambda e: e.tensor_copy(out=dst, in_=pw[0:65, 0:2, 0:128]), reads=[b_pw[0], b_pw[1]], writes=[b_acc])
                                  else:
                                      P.op("dve", lambda e: e.tensor_tensor(out=dst, in0=pw[0:65, 0:2, 0:128], in1=dst, op=ALU.add), reads=[b_pw[0], b_pw[1], b_acc], writes=[b_acc])
                      rrow = sbt(st, "rrowA", [128, 2, 256])
                      ysa = sbt(st, "ysa", [64, 2, 256], BF16)
                      b_rr, b_ys = Buf(), Buf()
                      for ti in range(NT):
                          t0 = ti * 256
                          P.op("dve", lambda e: e.reciprocal(out=rrow[64:65, :, :], in_=accA[64:65, :, t0:t0 + 256]), reads=[b_acc], writes=[b_rr])
                          P.op("pe", lambda e: e.matmul(pq[2][0:64, :], lhsT=ones64[64:65, :], rhs=rrow[64:65, :, :].rearrange("p a b -> p (a b)"), start=True, stop=True),
                               reads=[b_rr, b_o64], writes=[b_pq[2]])
                          P.op("dve", lambda e: e.tensor_tensor(out=ysa[:], in0=accA[0:64, :, t0:t0 + 256], in1=pq[2][0:64, :].rearrange("p (a b) -> p a b", a=2), op=ALU.mult),
                               reads=[b_acc, b_pq[2]], writes=[b_ys])
                          P.dma(YT[0, :, t0:t0 + 256].rearrange("(h e) t -> e h t", e=64), ysa[:], reads=[b_ys], writes=[b_YT])
            P.barrier()

        def post_phase(l, xin, b_xin, xout, b_xout):
            with ExitStack() as st:
                wbr = sbt(st, "wbr", [128, 7, D], BF16)
                wo = sbt(st, "wo", [128, 8, D], BF16)
                b_wbr, b_wo, b_ms = Buf(), Buf(), Buf()
                load_w_bf16(wbr, w_branch[l], 7, b_wbr, piece=1024)
                load_w_bf16(wo, w_o[l], 8, b_wo, piece=1024)
                gp, lng, lnb = [sbt(st, "m%d" % i, [128, D]) for i in range(3)]
                load_bcast(gp, MODP[l, 5 * D:6 * D], b_ms)
                P.dma(lng[:], ln_g[l, 1, :].partition_broadcast(128), writes=[b_ms])
                P.dma(lnb[:], ln_b[l, 1, :].partition_broadcast(128), writes=[b_ms])
                xs = [sbt(st, "xs%d" % i, [128, 2, D]) for i in range(2)]
                b_xs = [Buf(), Buf()]
                yt = [sbt(st, "yt%d" % i, [128, 7, 256], BF16) for i in range(2)]
                gt = [sbt(st, "gt%d" % i, [128, 32, 256], BF16) for i in range(2)]
                b_yt, b_gt = [Buf(), Buf()], [Buf(), Buf()]
                mg = sbt(st, "mg", [128, 8, 256])
                mgb = sbt(st, "mgb", [128, 8, 256], BF16)
                tmp = [sbt(st, "tmp%d" % i, [128, 256]) for i in range(2)]
                b_mg, b_mgb = Buf(), Buf()
                b_tmp = [Buf(), Buf()]
                xo = sbt(st, "xo", [128, 2, D])
                b_xo = [Buf(), Buf()]
                t1 = sbt(st, "t1", [128, D])
                r = sbt(st, "r", [128, D])
                small = sbt(st, "small", [128, 32])
                b_t1, b_r, b_small = Buf(), Buf(), Buf()
                kch = ((0, 1), (1, 3), (3, 5), (5, 7))
                pc = 0
                def post_load(tj):
                    q2 = tj % 2
                    tq = tj * 256
                    P.dma(yt[q2][:], YT[:, :, tq:tq + 256].rearrange("c p t -> p c t"), reads=[b_YT], writes=[b_yt[q2]])
                    P.dma(gt[q2][:], GT[:, :, tq:tq + 256].rearrange("c p t -> p c t"), reads=[b_GT], writes=[b_gt[q2]])
                    P.dma(xs[q2][:], xin[tq:tq + 256, :].rearrange("(j p) d -> p j d", p=128), reads=[b_xin], writes=[b_xs[q2]])

                post_load(0)
                for ti in range(NT):
                    t0 = ti * 256
                    k2 = ti % 2
                    if ti + 1 < NT:
                        post_load(ti + 1)
                    for fo in range(8):
                        for bi in range(4):
                            k = pc % 4
                            pc += 1
                            a, b = kch[bi]
                            for kc in range(a, b):
                                P.op("pe", lambda e: e.matmul(pq[k][:, 0:256], lhsT=wbr[:, kc, fo * 128:(fo + 1) * 128], rhs=yt[k2][:, kc, :], start=(kc == a), stop=(kc == b - 1)),
                                     reads=[b_wbr, b_yt[k2]], writes=[b_pq[k]])
                            if bi == 0:
                                P.op("dve", lambda e: e.tensor_tensor(out=mg[:, fo, :], in0=pq[k][:, 0:256], in1=gt[k2][:, bi * 8 + fo, :], op=ALU.mult), reads=[b_pq[k], b_gt[k2]], writes=[b_mg])
                            else:
                                kk = pc % 2
                                P.op("dve", lambda e: e.tensor_tensor(out=tmp[kk][:], in0=pq[k][:, 0:256], in1=gt[k2][:, bi * 8 + fo, :], op=ALU.mult), reads=[b_pq[k], b_gt[k2]], writes=[b_tmp[kk]])
                                if bi < 3:
                                    P.op("pool", lambda e: e.tensor_tensor(out=mg[:, fo, :], in0=mg[:, fo, :], in1=tmp[kk][:], op=ALU.add), reads=[b_mg, b_tmp[kk]], writes=[b_mg])
                                else:
                                    P.op("pool", lambda e: e.tensor_tensor(out=mgb[:, fo, :], in0=mg[:, fo, :], in1=tmp[kk][:], op=ALU.add), reads=[b_mg, b_tmp[kk]], writes=[b_mgb])
                    for j in range(2):
                        for nh in range(2):
                            for kc in range(8):
                                P.op("pe", lambda e: e.matmul(pw[:, 2 * j + nh, :], lhsT=mgb[:, kc, j * 128:(j + 1) * 128], rhs=wo[:, kc, nh * 512:(nh + 1) * 512], start=(kc == 0), stop=(kc == 7)),
                                     reads=[b_mgb, b_wo], writes=[b_pw[2 * j + nh]])
                    for j in range(2):
                        deepnorm_ln(j, xs[k2], b_xs[k2], gp, lng, lnb, b_ms, t1, r, b_t1, b_r, small, b_small, xo, b_xo[j])
                    P.dma(xout[t0:t0 + 256, :].rearrange("(j p) d -> p j d", p=128), xo[:], reads=[b_xo[0], b_xo[1]], writes=[b_xout])
            P.barrier()

        b_xin0 = Buf()
        b_y = Buf()
        dbg_stage = dbg if isinstance(dbg, int) and not isinstance(dbg, bool) else 99
        stages = 0
        cur, b_cur = x_in, b_xin0
        for l in range(2):
            last = (l == 1)
            if dbg_stage >= 1:
                ffn_phase(l, 0, 0, cur, b_cur, S1, b_S1)
            if dbg_stage >= 2:
                inproj_phase(l, S1, b_S1)
            if dbg_stage >= 3:
                attention_phase(l)
            if dbg_stage >= 4:
                post_phase(l, S1, b_S1, S2, b_S2)
            if dbg_stage >= 5:
                ffn_phase(l, 1, 2, S2, b_S2, y_out if last else S1, b_y if last else b_S1)
            cur, b_cur = S1, b_S1
            if dbg_stage < 99:
                break
        P.finish()
        build.stats = (P.ninst, P.nwait)
    return nc


def _consts(T):
    p = np.arange(128)
    cf = np.zeros((128, 5, 128), np.float32)
    cf[:, 0, :] = np.eye(128)
    cf[:, 1, :] = (p[:, None] >= p[None, :])
    cf[:, 2, :] = 1.0
    cf[0, 3, :] = 1.0
    cf[:, 4, :] = np.where(p[None, :] > p[:, None], -1e30, 0.0)
    cb = np.zeros((128, 15, 128), np.float32)
    cb[:, 13, :] = (p[:, None] >= p[None, :])
    cb[:, 14, :] = 1.0
    cb[:, 0, :] = np.eye(128)
    le = (p[:, None] <= p[None, :]).astype(np.float32)
    lt = (p[:, None] < p[None, :]).astype(np.float32)
    gev = (p[:, None] >= p[None, :]).astype(np.float32)
    for h in range(4):
        cb[:, 1 + h, :] = le
        cb[:, 5 + h, :] = lt
    for hh in range(2):
        cb[:, 9 + hh * 2 + 0, :] = gev
        cb[:, 9 + hh * 2 + 1, :] = le
    half = 8
    inv = 500000.0 ** (-(np.arange(half, dtype=np.float32) * (2.0 / 16)))
    ang = np.arange(T, dtype=np.float32)[None, :] * inv[:, None].astype(np.float32)
    cos, sin = np.cos(ang).astype(np.float32), np.sin(ang).astype(np.float32)
    C = np.ones((64, T), np.float32)
    S = np.zeros((64, T), np.float32)
    C[0:8], C[8:16] = cos, cos
    S[0:8], S[8:16] = -sin, sin
    rope = np.stack([np.concatenate([C, C], 0), np.concatenate([S, S], 0)], 0)
    return cf, cb.astype(ml_dtypes.bfloat16), rope


def _wm_cols():
    ar = np.arange
    chunks = []
    for g in range(3):
        chunks.append(ar(g * 128, (g + 1) * 128))
    for g in range(3):
        chunks.append(384 + ar(g * 128, (g + 1) * 128))
    for g in range(2):
        chunks.append(OFF_B + ar(g * 128, (g + 1) * 128))
    for g in range(2):
        chunks.append(OFF_B + 256 + ar(g * 128, (g + 1) * 128))
    for g in range(2):
        chunks.append(OFF_IQ + ar(g * 128, (g + 1) * 128))
    chunks.append(np.concatenate([OFF_IK + ar(64), OFF_IK + ar(64)]))
    for base in (OFF_C, OFF_C + 256, OFF_D, OFF_D + 256):
        for g in range(2):
            chunks.append(base + ar(g * 128, (g + 1) * 128))
    perm64 = np.arange(64)
    perm64[0:8] = np.arange(8, 16)
    perm64[8:16] = np.arange(0, 8)
    perm128 = np.concatenate([perm64, 64 + perm64])
    for ch in range(NROPE):
        chunks.append(chunks[ch][perm128])
    cols = np.concatenate(chunks + [OFF_FG + ar(4), 768 + ar(384), OFF_B + 512 + ar(256), OFF_C + 512 + ar(256),
                                    OFF_D + 512 + ar(256), OFF_IW + ar(4)])
    assert cols.shape[0] == NC1
    return cols


_NC_CACHE = {}


def _run(T, per_core, dbg=False):
    key = (T, dbg)
    if key not in _NC_CACHE:
        _NC_CACHE[key] = build(T, dbg)
    nc = _NC_CACHE[key]
    res = run_bass_kernel_spmd(nc, per_core, core_ids=list(range(len(per_core))))
    return res


def make_in_maps(T, x, c, ada_w, ada_b, ln_g, ln_b, ffn_w_in, ffn_w_out, mix_w_in, mix_b_gate, mix_b_forget,
                 mix_w_branch, mix_w_out):
    f = lambda a: np.ascontiguousarray(np.asarray(a, dtype=np.float32))
    cf, cb, rope = _consts(T)
    cols = _wm_cols()
    mw = np.asarray(mix_w_in, dtype=np.float32)
    shared = {
        "ada_w": f(ada_w), "ada_b": f(ada_b), "ln_g": f(ln_g), "ln_b": f(ln_b),
        "ffn_w_in": f(ffn_w_in), "ffn_w_out": f(ffn_w_out),
        "wm1": f(mw[:, :, cols]), "wgate": f(mw[:, :, OFF_GATE:OFF_GATE + 4096]),
        "b_gate": f(mix_b_gate), "b_forget": f(mix_b_forget), "w_branch": f(mix_w_branch), "w_o": f(mix_w_out),
        "cf": cf, "cb": cb, "rope": rope,
    }
    xs = np.asarray(x, dtype=np.float32)
    cs = np.asarray(c, dtype=np.float32)
    maps = []
    for b in range(xs.shape[0]):
        m = dict(shared)
        m["x"] = f(xs[b])
        m["c"] = f(cs[b])
        maps.append(m)
    return maps


def kernel(x, c, ada_w, ada_b, ln_g, ln_b, ffn_w_in, ffn_w_out, mix_w_in, mix_b_gate, mix_b_forget,
           mix_w_branch, mix_w_out):
    B, T, _ = np.asarray(x).shape
    maps = make_in_maps(T, x, c, ada_w, ada_b, ln_g, ln_b, ffn_w_in, ffn_w_out, mix_w_in, mix_b_gate,
                        mix_b_forget, mix_w_branch, mix_w_out)
    per_core = [maps[i] for i in range(B)]
    res = _run(T, per_core)
    out = np.stack([np.asarray(res.results[b]["y"], dtype=np.float32) for b in range(B)], axis=0)
    return out
```
